# Optimizing a Trainium2 kernel written in Bass

```python
import math
import jax, jax.numpy as jnp
from jax import lax
import numpy as np

D_MODEL = 1024
BATCH = 2
SEQ = 16384
DEPTH = 1
DEC_BATCH = 32
DEC_SEQ = 16
PAST_LEN = 2048

CHUNK = 64
Q_BLOCK = 128
EPS = 1e-6

MLA_HEADS = 8
QK_NOPE = 64
QK_ROPE = 32
V_HEAD = 64
Q_LORA = 384
KV_LORA = 256
ROPE_THETA = 10000.0
MLA_WIDTH = MLA_HEADS * V_HEAD
MLA_SCALE = 1.0 / math.sqrt(QK_NOPE + QK_ROPE)

HG_HEADS = 4
HG_EXPAND = 128
HG_VDIM = 128
HG_WIDTH = HG_HEADS * HG_EXPAND
HG_SCALE = 1.0 / math.sqrt(HG_EXPAND)

D_FF = ((8 * D_MODEL // 3 + 255) // 256) * 256

IN_SIZES = (Q_LORA, KV_LORA, QK_ROPE, HG_WIDTH, HG_WIDTH, HG_WIDTH, HG_WIDTH, D_MODEL, D_MODEL)
IN_DIM = sum(IN_SIZES)
IN_OFFSETS = tuple(int(v) for v in np.cumsum(IN_SIZES)[:-1])

kernel_name = "hybrid_mla_hgrn2_streaming_step"


def rms_norm(x, w):
    xf = x.astype(jnp.float32)
    y = xf * lax.rsqrt(jnp.mean(xf * xf, axis=-1, keepdims=True) + EPS)
    return (y * w.astype(jnp.float32)).astype(x.dtype)


def rope(x, pos):
    half = x.shape[-1] // 2
    inv = ROPE_THETA ** (-jnp.arange(half, dtype=jnp.float32) / half)
    ang = pos.astype(jnp.float32)[:, None] * inv[None, :]
    shp = (1, pos.shape[0]) + (1,) * (x.ndim - 3) + (half,)
    cos = jnp.cos(ang).reshape(shp)
    sin = jnp.sin(ang).reshape(shp)
    xf = x.astype(jnp.float32)
    x1, x2 = xf[..., :half], xf[..., half:]
    return jnp.concatenate([x1 * cos - x2 * sin, x1 * sin + x2 * cos], axis=-1).astype(x.dtype)


def chunk_attention(q, k, v, q_chunk, k_chunk):
    s = jnp.einsum('bqhd,bkhd->bhqk', q, k).astype(jnp.float32) * MLA_SCALE
    mask = k_chunk[None, :] <= q_chunk[:, None]
    s = jnp.where(mask[None, None], s, -jnp.inf)
    p = jax.nn.softmax(s, axis=-1).astype(v.dtype)
    return jnp.einsum('bhqk,bkhd->bqhd', p, v)


def hgrn2_recurrence(q, k, v, g, s0, block):
    B, T, H, K = q.shape
    V = v.shape[-1]
    nb = T // block
    f32 = jnp.float32

    def to_blocks(a):
        return a.astype(f32).reshape(B, nb, block, H, a.shape[-1]).transpose(1, 0, 3, 2, 4)

    tri = jnp.tril(jnp.ones((block, block), dtype=bool))[None, None, :, :, None]

    def step(S, blk):
        qb, kb, vb, gb = blk
        bc = jnp.cumsum(gb, axis=2)
        diff = bc[:, :, :, None, :] - bc[:, :, None, :, :]
        dec = jnp.exp(jnp.where(tri, diff, -jnp.inf))
        att = jnp.einsum('bhtk,bhsk,bhtsk->bhts', qb, kb, dec)
        o = (jnp.einsum('bhts,bhsv->bhtv', att, vb)
             + jnp.einsum('bhtk,bhkv->bhtv', qb * jnp.exp(bc), S))
        last = bc[:, :, -1:, :]
        S = (jnp.exp(last[:, :, 0, :])[..., None] * S
             + jnp.einsum('bhsk,bhsv->bhkv', kb * jnp.exp(last - bc), vb))
        return S, o

    S, o = lax.scan(step, s0.astype(f32), (to_blocks(q), to_blocks(k), to_blocks(v), to_blocks(g)))
    o = o.transpose(1, 0, 3, 2, 4).reshape(B, T, H, V)
    return o, S


def token_mixers(h, pos, p, l, past_ckv, past_krope, s0, hg_block):
    B, T, _ = h.shape
    z = h @ p["w_in"][l]
    q_lat, kv_lat, kr_raw, hq, hf, hi, hgate, ga, gb = jnp.split(z, IN_OFFSETS, axis=-1)

    c_q = rms_norm(q_lat, p["q_norm"][l])
    q = (c_q @ p["w_uq"][l]).reshape(B, T, MLA_HEADS, QK_NOPE + QK_ROPE)
    q = jnp.concatenate([q[..., :QK_NOPE], rope(q[..., QK_NOPE:], pos)], axis=-1)
    c_kv = rms_norm(kv_lat, p["kv_norm"][l])
    k_rope = rope(kr_raw, pos)
    if past_ckv is None:
        all_ckv, all_kr, k_pos = c_kv, k_rope, pos
    else:
        all_ckv = jnp.concatenate([past_ckv.astype(c_kv.dtype), c_kv], axis=1)
        all_kr = jnp.concatenate([past_krope.astype(k_rope.dtype), k_rope], axis=1)
        k_pos = jnp.arange(all_ckv.shape[1], dtype=jnp.int32)
    Tk = all_ckv.shape[1]
    kv = (all_ckv @ p["w_ukv"][l]).reshape(B, Tk, MLA_HEADS, QK_NOPE + V_HEAD)
    k = jnp.concatenate([kv[..., :QK_NOPE],
                         jnp.broadcast_to(all_kr[:, :, None, :], (B, Tk, MLA_HEADS, QK_ROPE))], axis=-1)
    v = kv[..., QK_NOPE:]
    q_chunk = pos // CHUNK
    k_chunk = k_pos // CHUNK
    if T <= Q_BLOCK:
        attn = chunk_attention(q, k, v, q_chunk, k_chunk)
    else:
        nqb = T // Q_BLOCK
        qb = q.reshape(B, nqb, Q_BLOCK, MLA_HEADS, QK_NOPE + QK_ROPE).transpose(1, 0, 2, 3, 4)
        qc = q_chunk.reshape(nqb, Q_BLOCK)
        attn = lax.map(lambda a: chunk_attention(a[0], k, v, a[1], k_chunk), (qb, qc))
        attn = attn.transpose(1, 0, 2, 3, 4).reshape(B, T, MLA_HEADS, V_HEAD)

    lb = jnp.cumsum(jax.nn.softmax(p["lb_param"].astype(jnp.float32), axis=0), axis=0)[l]
    hf32 = hf.astype(jnp.float32)
    log_f = jnp.log(lb + (1.0 - lb) * jax.nn.sigmoid(hf32))
    k_in = (1.0 - lb) * jax.nn.sigmoid(-hf32)
    qh = jax.nn.silu(hq.astype(jnp.float32)) * HG_SCALE
    rs = lambda a, d: a.reshape(B, T, HG_HEADS, d)
    o, s_new = hgrn2_recurrence(rs(qh, HG_EXPAND), rs(k_in, HG_EXPAND), rs(hi, HG_VDIM),
                                rs(log_f, HG_EXPAND), s0, hg_block)
    o = rms_norm(o.astype(h.dtype), p["hg_norm"][l]) * jax.nn.silu(rs(hgate, HG_VDIM))

    y_a = attn.reshape(B, T, MLA_WIDTH) @ p["w_pa"][l]
    y_b = o.reshape(B, T, HG_HEADS * HG_VDIM) @ p["w_pb"][l]
    mixed = jax.nn.sigmoid(ga) * y_a + jax.nn.sigmoid(gb) * y_b
    return mixed @ p["w_out"][l], c_kv, k_rope, s_new


def trunk(x, c, pos, p, caches_ckv, caches_kr, states, hg_block):
    B = x.shape[0]
    new_ckv, new_kr, new_s = [], [], []
    for l in range(DEPTH):
        mod = jax.nn.silu(c) @ p["w_ada"][l] + p["b_ada"][l]
        sh1, sc1, g1, sh2, sc2, g2 = [t[:, None, :] for t in jnp.split(mod, 6, axis=-1)]
        h = rms_norm(x, p["norm1"][l]) * (1.0 + sc1) + sh1
        if caches_ckv is None:
            past_ckv, past_kr = None, None
            s0 = jnp.zeros((B, HG_HEADS, HG_EXPAND, HG_VDIM), dtype=jnp.float32)
        else:
            past_ckv, past_kr, s0 = caches_ckv[l], caches_kr[l], states[l]
        m, ckv, kr, s = token_mixers(h, pos, p, l, past_ckv, past_kr, s0, hg_block)
        x = x + g1 * m
        h = rms_norm(x, p["norm2"][l]) * (1.0 + sc2) + sh2
        gate, up = jnp.split(h @ p["w_gu"][l], 2, axis=-1)
        x = x + g2 * ((jax.nn.silu(gate) * up) @ p["w_down"][l])
        new_ckv.append(ckv)
        new_kr.append(kr)
        new_s.append(s.astype(x.dtype))
    y = rms_norm(x, p["final_norm"])
    return y, jnp.stack(new_ckv), jnp.stack(new_kr), jnp.stack(new_s)


def setup_inputs(seed: int = 0) -> dict:
    key = jax.random.key(seed)
    ks = jax.random.split(key, 32)
    f32 = jnp.float32
    nrm = lambda k, shape, s: jax.random.normal(k, shape, f32) * s
    gain = lambda k, shape: 1.0 + 0.02 * jax.random.normal(k, shape, f32)
    return {
        "x_prompt": nrm(ks[0], (BATCH, SEQ, D_MODEL), 1.0),
        "x_sample": nrm(ks[1], (DEC_BATCH, DEC_SEQ, D_MODEL), 1.0),
        "cache_ckv": nrm(ks[2], (DEPTH, DEC_BATCH, PAST_LEN, KV_LORA), 1.0),
        "cache_krope": nrm(ks[3], (DEPTH, DEC_BATCH, PAST_LEN, QK_ROPE), 1.0),
        "state_hgrn": nrm(ks[4], (DEPTH, DEC_BATCH, HG_HEADS, HG_EXPAND, HG_VDIM), 0.3),
        "c_prompt": nrm(ks[5], (BATCH, D_MODEL), 1.0),
        "c_sample": nrm(ks[6], (DEC_BATCH, D_MODEL), 1.0),
        "w_in": nrm(ks[7], (DEPTH, D_MODEL, IN_DIM), D_MODEL ** -0.5),
        "q_norm": gain(ks[8], (DEPTH, Q_LORA)),
        "w_uq": nrm(ks[9], (DEPTH, Q_LORA, MLA_HEADS * (QK_NOPE + QK_ROPE)), Q_LORA ** -0.5),
        "kv_norm": gain(ks[10], (DEPTH, KV_LORA)),
        "w_ukv": nrm(ks[11], (DEPTH, KV_LORA, MLA_HEADS * (QK_NOPE + V_HEAD)), KV_LORA ** -0.5),
        "lb_param": nrm(ks[12], (DEPTH + 1, HG_WIDTH), 0.5),
        "hg_norm": gain(ks[13], (DEPTH, HG_VDIM)),
        "w_pa": nrm(ks[14], (DEPTH, MLA_WIDTH, D_MODEL), MLA_WIDTH ** -0.5),
        "w_pb": nrm(ks[15], (DEPTH, HG_HEADS * HG_VDIM, D_MODEL), (HG_HEADS * HG_VDIM) ** -0.5),
        "w_out": nrm(ks[16], (DEPTH, D_MODEL, D_MODEL), D_MODEL ** -0.5),
        "norm1": gain(ks[17], (DEPTH, D_MODEL)),
        "norm2": gain(ks[18], (DEPTH, D_MODEL)),
        "w_ada": nrm(ks[19], (DEPTH, D_MODEL, 6 * D_MODEL), 0.5 * D_MODEL ** -0.5),
        "b_ada": nrm(ks[20], (DEPTH, 6 * D_MODEL), 0.02),
        "w_gu": nrm(ks[21], (DEPTH, D_MODEL, 2 * D_FF), D_MODEL ** -0.5),
        "w_down": nrm(ks[22], (DEPTH, D_FF, D_MODEL), D_FF ** -0.5),
        "final_norm": gain(ks[23], (D_MODEL,)),
    }


def reference(x_prompt, x_sample, cache_ckv, cache_krope, state_hgrn, c_prompt, c_sample,
              w_in, q_norm, w_uq, kv_norm, w_ukv, lb_param, hg_norm, w_pa, w_pb, w_out,
              norm1, norm2, w_ada, b_ada, w_gu, w_down, final_norm):
    p = {"w_in": w_in, "q_norm": q_norm, "w_uq": w_uq, "kv_norm": kv_norm, "w_ukv": w_ukv,
         "lb_param": lb_param, "hg_norm": hg_norm, "w_pa": w_pa, "w_pb": w_pb, "w_out": w_out,
         "norm1": norm1, "norm2": norm2, "w_ada": w_ada, "b_ada": b_ada, "w_gu": w_gu,
         "w_down": w_down, "final_norm": final_norm}
    t_prompt = x_prompt.shape[1]
    t_sample = x_sample.shape[1]
    pos_prompt = jnp.arange(t_prompt, dtype=jnp.int32)
    pos_sample = PAST_LEN + jnp.arange(t_sample, dtype=jnp.int32)
    y_prompt, ckv_p, kr_p, s_p = trunk(x_prompt, c_prompt, pos_prompt, p, None, None, None, CHUNK)
    y_sample, ckv_s, kr_s, s_s = trunk(x_sample, c_sample, pos_sample, p, cache_ckv, cache_krope,
                                       state_hgrn, t_sample)
    return (y_prompt, y_sample, ckv_p, kr_p, s_p, ckv_s, kr_s, s_s)
```

```python
import math
import os
from contextlib import ExitStack
import numpy as np
import concourse.bass as bass
import concourse.mybir as mybir
from concourse.bass_utils import run_bass_kernel_spmd

F32 = mybir.dt.float32
BF16 = mybir.dt.bfloat16
AF = mybir.ActivationFunctionType
ALU = mybir.AluOpType
AX = mybir.AxisListType

D = 1024
KC = 8
QL, KVL, RD = 384, 256, 32
NH, QKN, VH = 8, 64, 64
HGH, HGK = 4, 128
HW = 512
DFF = 2816
NFC = DFF // 128
EPS = 1e-6
MLA_SCALE = 1.0 / math.sqrt(96.0)
HG_SCALE = 1.0 / math.sqrt(128.0)
O_Q, O_KV, O_HQ, O_HF, O_HI, O_HG, O_GA = 0, 384, 672, 1184, 1696, 2208, 2720
NA = 2720


class _Op:
    __slots__ = ("eng", "fn", "dma", "deps", "odeps", "signal", "tok", "cost", "idx", "pos", "fin")

    def __init__(self, eng, fn, dma, cost):
        self.eng, self.fn, self.dma, self.cost = eng, fn, dma, cost
        self.deps, self.odeps, self.signal, self.tok = [], [], False, None
        self.idx = self.pos = 0
        self.fin = 0.0


SYNC_NS = 300.0


class Prog:
    def __init__(self, nc, stack):
        self.nc, self.stack = nc, stack
        self.engs = {"pe": nc.tensor, "act": nc.scalar, "dve": nc.vector, "pool": nc.gpsimd, "sp": nc.sync}
        self.sem, self.cnt = {}, {}
        self.seen = {e: {} for e in self.engs}
        self.last, self.readers = {}, {}
        self.ops = []
        self.nops = 0
        self.reorder = False
        self.cap = None
        self.reorder_engs = set(os.environ.get('REORDER_ENGS', 'pe,act,dve,pool,sp').split(','))

    def _sem(self, name):
        if name not in self.sem:
            self.sem[name] = self.stack.enter_context(self.nc.semaphore("s_" + name))
            self.cnt[name] = 0
        return self.sem[name]

    @staticmethod
    def _key(k):
        return k if isinstance(k, str) else k.name

    def op(self, eng, fn, r=(), w=(), dma=None, cost=300.0):
        if self.cap is not None:
            self.cap.append((eng, fn, list(r), list(w), dma, cost))
            return None
        o = _Op(eng, fn, dma, cost)
        deps = {}
        rk = [self._key(k) for k in r]
        wk = [self._key(k) for k in w]
        for k in rk:
            lw = self.last.get(k)
            if lw is not None:
                deps[id(lw)] = lw
        for k in wk:
            lw = self.last.get(k)
            if lw is not None:
                deps[id(lw)] = lw
            for rd in self.readers.get(k, ()):
                deps[id(rd)] = rd
        for p in deps.values():
            if p is o:
                continue
            o.odeps.append(p)
            if p.eng == eng and p.dma is None and dma is None and eng == "pe":
                continue
            o.deps.append(p)
        for k in rk:
            self.readers.setdefault(k, []).append(o)
        for k in wk:
            self.last[k] = o
            self.readers[k] = []
        o.idx = len(self.ops)
        self.ops.append(o)
        return o

    def interleave(self, *thunks):
        chains = []
        for t in thunks:
            self.cap = []
            t()
            chains.append(self.cap)
        self.cap = None
        idx = [0] * len(chains)
        while True:
            best, bk = None, None
            for k, ch in enumerate(chains):
                if idx[k] < len(ch):
                    fr = (idx[k] + 0.5) / len(ch)
                    if best is None or fr < best:
                        best, bk = fr, k
            if bk is None:
                break
            self.op(*chains[bk][idx[bk]])
            idx[bk] += 1

    def barrier(self):
        self._barrier = True
        self.last, self.readers = {}, {}

    def _schedule(self, ops):
        import heapq
        n = len(ops)
        if not self.reorder:
            for i, o in enumerate(ops):
                o.pos = i
            return {e: [o for o in ops if o.eng == e] for e in self.engs}
        ndep = [len(o.odeps) for o in ops]
        users = [[] for _ in range(n)]
        for o in ops:
            for p in o.odeps:
                users[p.idx].append(o)
        ready_t = [0.0] * n
        heaps = {e: [] for e in self.engs}
        for o in ops:
            if ndep[o.idx] == 0:
                heapq.heappush(heaps[o.eng], (0.0, o.idx))
        tE = {e: 0.0 for e in self.engs}
        order = {e: [] for e in self.engs}
        done = 0
        WIN = 600
        nxt = 0
        sched = [False] * n
        byeng = {e: [o for o in ops if o.eng == e] for e in self.engs}
        pq = {e: 0 for e in self.engs}
        while done < n:
            best = None
            for e in self.engs:
                h = heaps[e]
                if not h:
                    continue
                cand = None
                tmp = []
                if e not in self.reorder_engs:
                    while pq[e] < len(byeng[e]) and sched[byeng[e][pq[e]].idx]:
                        pq[e] += 1
                    if pq[e] < len(byeng[e]):
                        want = byeng[e][pq[e]].idx
                        found = None
                        for it in h:
                            if it[1] == want:
                                found = it
                                break
                        if found is not None:
                            h.remove(found)
                            heapq.heapify(h)
                            cand = found
                    h_iter = []
                else:
                    h_iter = h
                while h_iter:
                    rt, i = heapq.heappop(h)
                    if i <= nxt + WIN:
                        cand = (rt, i)
                        break
                    tmp.append((rt, i))
                for it in tmp:
                    heapq.heappush(h, it)
                if cand is None:
                    continue
                st = max(tE[e], cand[0])
                if best is None or (st, cand[1]) < (best[0], best[2]):
                    if best is not None:
                        heapq.heappush(heaps[best[1]], (best[3], best[2]))
                    best = (st, e, cand[1], cand[0])
                else:
                    heapq.heappush(h, cand)
            if best is None:
                WIN *= 2
                continue
            st, e, i, rt = best
            o = ops[i]
            issue = 60.0 if o.dma is not None else o.cost
            o.fin = st + (o.cost if o.dma is None else o.cost)
            tE[e] = st + issue
            o.pos = len(order[e])
            order[e].append(o)
            sched[i] = True
            done += 1
            while nxt < n and sched[nxt]:
                nxt += 1
            for u in users[i]:
                ndep[u.idx] -= 1
                lat = 0.0 if (u.eng == e and o.dma is None) else SYNC_NS
                ready_t[u.idx] = max(ready_t[u.idx], o.fin + lat)
                if ndep[u.idx] == 0:
                    heapq.heappush(heaps[u.eng], (ready_t[u.idx], u.idx))
        return order

    def emit(self):
        ops = self.ops
        self.nops += len(ops)
        order = self._schedule(ops)
        for o in ops:
            best = {}
            keep = []
            for p in o.deps:
                if p.dma is not None:
                    keep.append(p)
                else:
                    b = best.get(p.eng)
                    if b is None or p.pos > b.pos:
                        best[p.eng] = p
            o.deps = keep + list(best.values())
            for p in o.deps:
                p.signal = True
        lasts = []
        if getattr(self, "_barrier", False):
            for e in self.engs:
                if order[e]:
                    cl = [o for o in order[e] if o.dma is None]
                    if cl:
                        cl[-1].signal = True
                        lasts.append(cl[-1])
            dl = {}
            for o in ops:
                if o.dma is not None:
                    dl[o.dma] = o if (o.dma not in dl or True) else dl[o.dma]
            lasts_dma = set(o.dma for o in ops if o.dma is not None)
        for e in self.engs:
            for o in order[e]:
                if o.dma is not None:
                    self._sem(o.dma)
                    self.cnt[o.dma] += 16
                    o.tok = (o.dma, self.cnt[o.dma])
                elif o.signal:
                    self._sem(o.eng)
                    self.cnt[o.eng] += 1
                    o.tok = (o.eng, self.cnt[o.eng])
        bar = []
        if getattr(self, "_barrier", False):
            bar = [p.tok for p in lasts] + [(s, self.cnt[s]) for s in lasts_dma]
            self._barrier = False

        def run(ename):
            def f(e):
                seen = self.seen[ename]
                for o in order[ename]:
                    for s, v in sorted((p.tok for p in o.deps), key=lambda t: -t[1]):
                        if seen.get(s, 0) < v:
                            e.wait_ge(self.sem[s], v)
                            seen[s] = v
                    ins = o.fn(e)
                    if o.tok is not None:
                        ins.then_inc(self.sem[o.tok[0]], 16 if o.dma is not None else 1)
                for s, v in bar:
                    if seen.get(s, 0) < v:
                        e.wait_ge(self.sem[s], v)
                        seen[s] = v
            return f

        with self.nc.Block() as block:
            block.tensor(run("pe"))
            block.scalar(run("act"))
            block.vector(run("dve"))
            block.gpsimd(run("pool"))
            block.sync(run("sp"))
        self.ops = []


def _consts(sb, rows):
    s = np.arange(128)
    sub = np.where(s < rows, s // sb, -1)
    same = (sub[:, None] == sub[None, :]) & (sub[:, None] >= 0)
    tri = (same & (s[:, None] <= s[None, :])).astype(np.float32)
    mid = np.where(sub >= 0, sub * sb + sb // 2 - 1, 0)
    tdm = tri - tri[:, mid] * (sub[None, :] >= 0)
    ind = np.zeros((128, 4), np.float32)
    for c in range(4):
        ind[sub == c, c] = 1.0
    trev = (same & (s[:, None] > s[None, :])).astype(np.float32)
    colmask = np.zeros((128, 4, 128), np.float32)
    for c in range(4):
        colmask[:, c, sub == c] = 1.0
    cg = np.concatenate([tri, tdm, ind], axis=1)
    return cg.astype(np.float32), trev, tri, colmask.reshape(128, 512), ind


def build(T, PAST):
    NT = T // 128
    NO = NT // 4
    NG = NO // 4
    NO1 = NO + 1
    NTOK = NO1 * 128
    NKS = PAST // 128
    nc = bass.Bass("TRN2", target_bir_lowering=False)
    stack = ExitStack()
    P = Prog(nc, stack)

    def din(name, shape, dt=F32):
        return nc.dram_tensor(name, list(shape), dt, kind="ExternalInput").ap()

    def dout(name, shape):
        return nc.dram_tensor(name, list(shape), F32, kind="ExternalOutput").ap()

    def dscr(name, shape, dt):
        return nc.dram_tensor(name, list(shape), dt, kind="Internal").ap()

    x_seq = din("x_seq", [T, D])
    x_own = din("x_own", [NTOK, D])
    rope_seq = din("rope_seq", [T, 64])
    rope_own = din("rope_own", [NTOK, 64])
    ropeT_own = din("ropeT_own", [2, 96, NTOK])
    c5T = din("c5T", [128, KC, 5])
    oh5 = din("oh5", [5, 256])
    w_ada = din("w_ada", [D, 6 * D])
    b_ada = din("b_ada", [1, 6 * D])
    w_in = din("w_in", [D, 4768])
    w_uq = din("w_uq", [QL, 768])
    w_uqr = din("w_uqr", [QL, 256])
    w_ukv = din("w_ukv", [KVL, 1024])
    w_pa = din("w_pa", [512, D])
    w_pb = din("w_pb", [512, D])
    w_out = din("w_out", [D, D])
    w_gu = din("w_gu", [D, 2 * DFF])
    w_down = din("w_down", [DFF, D])
    nrm = din("nrm", [3, D])
    q_norm = din("q_norm", [1, QL])
    kv_norm = din("kv_norm", [1, KVL])
    hg_norm = din("hg_norm", [1, 128])
    lb_param = din("lb_param", [2, HW])
    cst = din("cst", [128, 2, 1284])
    sel = din("sel", [128, 4])
    mdiag = din("mdiag", [128, 4, 128])
    ckvT_c = din("ckvT_c", [4, 128, 2, PAST])
    krT_c = din("krT_c", [4, 32, PAST])
    state_s = din("state_s", [128, 4, 4, 128])

    y_o = dout("y_o", [NTOK, D])
    ckv_o = dout("ckv_o", [NTOK, KVL])
    kr_o = dout("kr_o", [NTOK, RD])
    hst_p = dout("hst_p", [128, 4, 128])
    hst_s = dout("hst_s", [128, 4, 4, 128])

    modt = dscr("modt", [12, 128, D], F32)
    ckT_scr = dscr("ckT_scr", [128, 2, T], BF16)
    krT_scr = dscr("krT_scr", [32, T], BF16)
    ckTs_scr = dscr("ckTs_scr", [128, 2, 128], BF16)
    krTs_scr = dscr("krTs_scr", [32, 128], BF16)
    qT_scr = dscr("qT_scr", [96, NH, NTOK], BF16)
    aT_scr = dscr("aT_scr", [64, NH, NTOK], BF16)
    oT_scr = dscr("oT_scr", [128, 4, NTOK], BF16)
    x1_scr = dscr("x1_scr", [NTOK, D], F32)

    uid = {"i": 0}

    def sb(st, name, shape, dt=F32):
        uid["i"] += 1
        return st.enter_context(nc.sbuf_tensor(f"{name}_u{uid['i']}", list(shape), dt))

    def psb(st, name):
        return st.enter_context(nc.psum_tensor(name, [128, 512], F32))

    rr = {"i": 0}

    def cast_eng():
        rr["i"] += 1
        return ("pool", "dve", "act")[rr["i"] % 3]

    def fsz(ap):
        n = 1
        for s in ap.shape[1:]:
            n *= int(s)
        return n

    def ecost(eng, ap):
        n = fsz(ap)
        if eng == "act":
            return 220.0 + n * 1.0
        if eng == "pool":
            return 200.0 + n * 2.0
        return 120.0 + n * 1.05

    def copy(eng, out, in_, r=None, w=None):
        r = [in_] if r is None else r
        w = [out] if w is None else w
        if eng == "act":
            return P.op("act", lambda e, o=out, i=in_: e.copy(out=o, in_=i), r=r, w=w, cost=ecost("act", out))
        return P.op(eng, lambda e, o=out, i=in_: e.tensor_copy(out=o, in_=i), r=r, w=w, cost=ecost(eng, out))

    def dma(eng, out, in_, sem, r=None, w=None):
        r = [in_] if r is None else r
        w = [out] if w is None else w
        nb = fsz(out) * int(out.shape[0]) * (2 if out.dtype == BF16 else 4)
        return P.op(eng, lambda e, o=out, i=in_: e.dma_start(out=o, in_=i), r=r, w=w, dma=sem, cost=2000.0 + nb / 100.0)

    def mm(out, lhsT, rhs, start=True, stop=True, r=None, w=None, tr=False):
        r = [lhsT, rhs] if r is None else r
        w = [out] if w is None else w
        c = 30.0 + fsz(rhs) / 2.0 * (4.0 if rhs.dtype == F32 else 1.0)
        if tr:
            return P.op("pe", lambda e: e.matmul(out, lhsT, rhs, start=True, stop=True, is_transpose=True), r=r, w=w, cost=c)
        return P.op("pe", lambda e: e.matmul(out, lhsT, rhs, start=start, stop=stop), r=r, w=w, cost=c)

    def act(out, in_, func, bias=None, scale=None, accum=None, r=None, w=None):
        r = [in_] + ([bias] if (bias is not None and not isinstance(bias, float)) else []) if r is None else r
        w = ([out] + ([accum] if accum is not None else [])) if w is None else w
        kw = {}
        if bias is not None:
            kw["bias"] = bias
        if scale is not None:
            kw["scale"] = scale
        if accum is not None:
            kw["accum_out"] = accum
        return P.op("act", lambda e: e.activation(out=out, in_=in_, func=func, **kw), r=r, w=w, cost=ecost("act", out))

    def tt(out, in0, in1, op, eng="dve", r=None, w=None):
        r = [in0, in1] if r is None else r
        w = [out] if w is None else w
        return P.op(eng, lambda e: e.tensor_tensor(out=out, in0=in0, in1=in1, op=op), r=r, w=w, cost=ecost(eng, out))

    def ts(out, in0, s1, op0, s2=None, op1=None, eng="dve", r=None, w=None):
        r = [in0] + [s for s in (s1, s2) if s is not None and not isinstance(s, float)] if r is None else r
        w = [out] if w is None else w
        if op1 is None:
            return P.op(eng, lambda e: e.tensor_scalar(out=out, in0=in0, scalar1=s1, scalar2=None, op0=op0), r=r, w=w, cost=ecost(eng, out))
        return P.op(eng, lambda e: e.tensor_scalar(out=out, in0=in0, scalar1=s1, scalar2=s2, op0=op0, op1=op1), r=r, w=w, cost=ecost(eng, out))

    def stt(out, in0, scalar, in1, op0, op1, eng="dve", r=None, w=None):
        r = [in0, in1] + ([scalar] if not isinstance(scalar, float) else []) if r is None else r
        w = [out] if w is None else w
        return P.op(eng, lambda e: e.scalar_tensor_tensor(out=out, in0=in0, scalar=scalar, in1=in1, op0=op0, op1=op1), r=r, w=w, cost=ecost(eng, out))

    def memset(eng, ap, val, w=None):
        return P.op(eng, lambda e: e.memset(ap, val), r=[], w=[ap] if w is None else w, cost=ecost(eng, ap) * 0.5)

    def recip(out, in_):
        return P.op("dve", lambda e: e.reciprocal(out=out, in_=in_), r=[in_], w=[out], cost=ecost("dve", out))

    def rstd_act(dst, tmp, ssq, n):
        act(tmp, ssq, AF.Ln, bias=EPS, scale=1.0 / n)
        act(dst, tmp, AF.Exp, scale=-0.5)

    def bc(ap, shape):
        return ap.to_broadcast(list(shape))

    cb = sb(stack, "cb", [128, 2, 1284], BF16)
    stg = [sb(stack, f"stg{i}", [128, 1024]) for i in range(2)]
    ps = [psb(stack, f"ps{i}") for i in range(8)]

    def psbf(i):
        return ps[i][:].bitcast(BF16)

    C_CG, C_TREV, C_TRI, C_TREVT, C_ID, C_COL = 0, 260, 388, 516, 644, 772

    def load_w(dst, src, ncols, kcn, c0=0, kc0=0, extra=()):
        step = 1024
        bufs = list(stg) + list(extra)
        for kc in range(kcn):
            for cc in range(0, ncols, step):
                n = min(step, ncols - cc)
                bi = rr["i"] % len(bufs)
                s = bufs[bi]
                rr["i"] += 1
                dma("sp" if bi < 2 else "pool", s[:, 0:n], src[(kc0 + kc) * 128:(kc0 + kc + 1) * 128, c0 + cc:c0 + cc + n], "stg" + s.name)
                copy(("dve", "act")[rr["i"] % 2], dst[:, kc, cc:cc + n], s[:, 0:n])

    def ident_b(set_=0):
        return cb[:, set_, C_ID:C_ID + 128]

    with ExitStack() as st:
        sc5 = sb(st, "sc5", [128, KC, 5])
        mod5 = sb(st, "mod5", [5, 6 * D])
        bad = sb(st, "bad", [5, 6 * D])
        ohs = sb(st, "ohs", [5, 256])
        nrb = sb(st, "nrb", [128, 2, D])
        wst = [sb(st, f"wst{i}", [128, KC, 512]) for i in range(2)]
        mt = [sb(st, f"mt{i}", [128, D]) for i in range(2)]
        cf0 = sb(st, "cf0", [128, 2, 1284])
        dma("sp", cf0[:], cst, "cf0")
        copy("dve", cb[:], cf0[:])
        dma("sp", sc5[:], c5T, "sc5")
        act(sc5[:], sc5[:], AF.Silu)
        dma("sp", bad[:], b_ada.partition_broadcast(5), "bad")
        dma("sp", ohs[:], oh5, "ohs")
        dma("sp", nrb[:, 0, :], nrm[0:1, :].partition_broadcast(128), "nrb0")
        dma("sp", nrb[:, 1, :], nrm[1:2, :].partition_broadcast(128), "nrb1")
        for j in range(12):
            wt = wst[j % 2]
            dma("sp" if j % 2 == 0 else "pool", wt[:], w_ada[:, j * 512:(j + 1) * 512].rearrange("(kc p) n -> p kc n", p=128), "wst" + wt.name)
            for kc in range(KC):
                mm(ps[j % 2][0:5, :], sc5[:, kc, :], wt[:, kc, :], start=(kc == 0), stop=(kc == KC - 1))
            tt(mod5[:, j * 512:(j + 1) * 512], ps[j % 2][0:5, :], bad[:, j * 512:(j + 1) * 512], ALU.add)
        k = 0
        for grp in range(2):
            oh = ohs[:, grp * 128:(grp + 1) * 128]
            for blk in range(2):
                for part, kind in ((1, "W"), (0, "B"), (2, "G")):
                    col = (blk * 3 + part) * D
                    m = mt[k % 2]
                    for hf in range(2):
                        mm(ps[2 + hf][:], oh, mod5[:, col + hf * 512:col + (hf + 1) * 512])
                    for hf in range(2):
                        dst = m[:, hf * 512:(hf + 1) * 512]
                        if kind == "W":
                            stt(dst, ps[2 + hf][:], 1.0, nrb[:, blk, hf * 512:(hf + 1) * 512], ALU.add, ALU.mult)
                        else:
                            copy("act", dst, ps[2 + hf][:])
                    idx = grp * 6 + blk * 3 + {"W": 0, "B": 1, "G": 2}[kind]
                    dma("sp", modt[idx], m[:], "mt" + m.name)
                    k += 1
        P.barrier()
        P.emit()

    with ExitStack() as st:
        cf = sb(st, "cf", [128, 2, 772])
        dma("sp", cf[:], cst[:, :, 0:772], "cf")
        wA = sb(st, "wA", [128, KC, NA], BF16)
        wq = sb(st, "wq", [128, 3, 768], BF16)
        wqr = sb(st, "wqr", [128, 3, NH, 96], BF16)
        W1 = [sb(st, "W1_0", [128, D])]
        B1 = [sb(st, "B1_0", [128, D])]
        qnb = sb(st, "qnb", [128, QL])
        kvnb = sb(st, "kvnb", [128, KVL])
        hgnb = sb(st, "hgnb", [128, 128])
        lbt = sb(st, "lbt", [128, HW])
        omlt = sb(st, "omlt", [128, HW])
        lbtmp = sb(st, "lbtmp", [128, HW])
        selt = sb(st, "selt", [128, 4])
        S = sb(st, "S", [128, HW])
        Sown = sb(st, "Sown", [128, HW])
        S0s = sb(st, "S0s", [128, 4, HW])
        Snew = sb(st, "Snew", [128, 4, HW])
        xt = [sb(st, f"xt{i}", [128, D]) for i in range(2)]
        rp = [sb(st, f"rp{i}", [128, 64]) for i in range(4)]
        junk = sb(st, "junk", [128, D], BF16)
        junk2 = sb(st, "junk2", [128, QL], BF16)
        tmpf = sb(st, "tmpf", [128, D])
        hb = sb(st, "hb", [128, D], BF16)
        hbo = sb(st, "hbo", [128, D], BF16)
        xto = sb(st, "xto", [128, D])
        rpo = sb(st, "rpo", [128, 64])
        rpTo = sb(st, "rpTo", [96, 2, 128])
        hT = sb(st, "hT", [128, KC, 128], BF16)
        st1 = sb(st, "st1", [128, 8])
        sth = sb(st, "sth", [128, 8])
        zkvo = sb(st, "zkvo", [128, 288])
        stA = sb(st, "stA", [128, 4])
        zkvs2 = [sb(st, f"zkvs{i}", [128, 288]) for i in range(2)]
        ckv = sb(st, "ckv", [128, KVL])
        ckvb = sb(st, "ckvb", [128, KVL], BF16)
        ckT = sb(st, "ckT", [128, 2, 128], BF16)
        kr = sb(st, "kr", [128, RD])
        krt = sb(st, "krt", [128, RD])
        krb = sb(st, "krb", [128, RD], BF16)
        krT = sb(st, "krT", [32, 128], BF16)
        sig2 = [sb(st, f"sig{i}", [128, HW]) for i in range(2)]
        tsg2 = [sb(st, f"tsg{i}", [128, HW]) for i in range(2)]
        gg2 = [sb(st, f"gg{i}", [128, HW]) for i in range(2)]
        kk2 = [sb(st, f"kk{i}", [128, HW]) for i in range(2)]
        sig, tsg, gg, kk = sig2[0], tsg2[0], gg2[0], kk2[0]
        qq = sb(st, "qq", [128, HW])
        sgt = sb(st, "sgt", [128, HW])
        Vb2 = [sb(st, f"Vb{i}", [128, HW], BF16) for i in range(2)]
        Vb = Vb2[0]
        erev = sb(st, "erev", [128, HW])
        Ke = sb(st, "Ke", [128, HW], BF16)
        Kem = sb(st, "Kem", [128, 4, HW], BF16)
        at4 = sb(st, "at4", [128, 8])
        cq = sb(st, "cq", [128, QL], BF16)
        cqT = sb(st, "cqT", [128, 3, 128], BF16)
        qTs = sb(st, "qTs", [96, NH, 128], BF16)
        qtmp = sb(st, "qtmp", [96, 4, 128])
        E12 = sb(st, "E12", [128, 4, 2, 128])
        E3 = sb(st, "E3", [128, 4, 128])
        aexp = sb(st, "aexp", [128, 16])
        Qo4 = sb(st, "Qo4", [128, 4, 128], BF16)
        Qom4 = sb(st, "Qom4", [128, 4, 4, 128], BF16)
        Qd4 = sb(st, "Qd4", [128, 4, 128], BF16)
        Kd4 = sb(st, "Kd4", [128, 4, 128], BF16)
        Scf4 = sb(st, "Scf4", [128, 4, 128])
        Scb4 = sb(st, "Scb4", [128, 4, 4, 128], BF16)
        attb4 = sb(st, "attb4", [128, 4, 128], BF16)
        osq = sb(st, "osq", [128, HW])
        og = sb(st, "og", [128, HW])
        ogb = sb(st, "ogb", [128, HW], BF16)
        oTs = sb(st, "oTs", [128, 4, 128], BF16)

        load_w(wA, w_in, NA, KC, extra=[tmpf])
        load_w(wq, w_uq, 768, 3, extra=[tmpf])
        memset("pool", wqr[:], 0.0)
        for kc in range(3):
            s = stg[rr["i"] % 2]
            rr["i"] += 1
            dma("sp", s[:, 0:256], w_uqr[kc * 128:(kc + 1) * 128, :], "stg" + s.name)
            copy("dve", wqr[:, kc, :, 64:96], s[:, 0:256].rearrange("p (h d) -> p h d", h=NH))
        dma("sp", W1[0][:], modt[0], "W10")
        dma("sp", B1[0][:], modt[1], "B10")
        dma("sp", qnb[:], q_norm.partition_broadcast(128), "qnb")
        dma("sp", kvnb[:], kv_norm.partition_broadcast(128), "kvnb")
        dma("sp", hgnb[:], hg_norm.partition_broadcast(128), "hgnb")
        dma("sp", lbt[:], lb_param[0:1, :].partition_broadcast(128), "lbt")
        dma("sp", lbtmp[:], lb_param[1:2, :].partition_broadcast(128), "lbtmp")
        dma("sp", selt[:], sel, "selt")
        dma("sp", S0s[:].rearrange("p b (h v) -> p b h v", h=4), state_s, "S0s")
        tt(lbt[:], lbt[:], lbtmp[:], ALU.subtract)
        act(lbt[:], lbt[:], AF.Sigmoid)
        ts(omlt[:], lbt[:], -1.0, ALU.mult, 1.0, ALU.add)
        memset("dve", S[:], 0.0)
        memset("dve", Sown[:], 0.0)

        def norm_only(xtile, g, hb_):
            act(junk[:], xtile[:], AF.Square, accum=stA[:, 0:1])
            rstd_act(stA[:, 2:3], stA[:, 1:2], stA[:, 0:1], D)
            stt(tmpf[:], xtile[:], stA[:, 2:3], W1[g][:], ALU.mult, ALU.mult)
            tt(hb_[:], tmpf[:], B1[g][:], ALU.add)

        def trans_only(hb_):
            pv = psbf(7)
            for kc in range(KC):
                mm(pv[:, kc * 128:(kc + 1) * 128], hb_[:, kc * 128:(kc + 1) * 128], ident_b(), tr=True)
            copy("act", hT[:].rearrange("p a b -> p (a b)"), pv[:, 0:1024])

        def zproj(groups):
            for kc in range(KC):
                for (pa, c0, n) in groups:
                    mm(pa, hT[:, kc, :], wA[:, kc, c0:c0 + n], start=(kc == 0), stop=(kc == KC - 1))

        def latents(zkv, rpt, out_row=None, scan_cols=None, smp=False):
            act(junk2[:, 0:KVL], zkv[:, 0:KVL], AF.Square, accum=st1[:, 3:4])
            rstd_act(st1[:, 5:6], st1[:, 4:5], st1[:, 3:4], KVL)
            stt(ckv[:], zkv[:, 0:KVL], st1[:, 5:6], kvnb[:], ALU.mult, ALU.mult)
            tt(krt[:, 0:16], zkv[:, KVL + 16:KVL + 32], rpt[:, 32:48], ALU.mult)
            tt(krt[:, 16:32], zkv[:, KVL:KVL + 16], rpt[:, 48:64], ALU.mult)
            tt(kr[:], zkv[:, KVL:KVL + 32], rpt[:, 0:32], ALU.mult)
            tt(kr[:], kr[:], krt[:], ALU.add)
            if out_row is not None:
                dma("pool", ckv_o[out_row:out_row + 128, :], ckv[:], "ckvo")
                dma("pool", kr_o[out_row:out_row + 128, :], kr[:], "kro")
            if scan_cols is not None or smp:
                copy("act", ckvb[:], ckv[:])
                copy("act", krb[:], kr[:])
                pv = psbf(6)
                for c in range(2):
                    mm(pv[:, c * 128:(c + 1) * 128], ckvb[:, c * 128:(c + 1) * 128], ident_b(), tr=True)
                mm(pv[0:32, 256:384], krb[:], ident_b(), tr=True)
                copy("dve", ckT[:].rearrange("p a b -> p (a b)"), pv[:, 0:256])
                copy("dve", krT[:], pv[0:32, 256:384])
                if smp:
                    dma("pool", ckTs_scr, ckT[:], "ckTo")
                    dma("pool", krTs_scr, krT[:], "krTo")
                else:
                    dma("pool", ckT_scr[:, :, scan_cols:scan_cols + 128], ckT[:], "ckTo")
                    dma("pool", krT_scr[:, scan_cols:scan_cols + 128], krT[:], "krTo")

        def gates_fk(zhf, zhi, ez_done=None, p=0):
            sig, tsg, gg, kk, Vb = sig2[p], tsg2[p], gg2[p], kk2[p], Vb2[p]
            if ez_done is None:
                act(sig[:], zhf, AF.Exp, scale=-1.0)
            act(sig[:], sig[:], AF.Ln, bias=1.0)
            act(sig[:], sig[:], AF.Exp, scale=-1.0)
            tt(tsg[:], sig[:], omlt[:], ALU.mult)
            tt(gg[:], tsg[:], lbt[:], ALU.add)
            act(gg[:], gg[:], AF.Ln)
            tt(kk[:], omlt[:], tsg[:], ALU.subtract)
            if ez_done is None:
                copy("act", Vb[:], zhi)

        def scan_A1(j):
            xs = xt[j % 2]
            rpt = rp[j % 4]
            dma("sp", xs[:], x_seq[j * 128:(j + 1) * 128, :], "xt" + xs.name)
            dma("sp", rpt[:], rope_seq[j * 128:(j + 1) * 128, :], "rp" + rpt.name)
            norm_only(xs, 0, hb)

        def scan_A2(j):
            trans_only(hb)
            zproj([(ps[0][:, 0:288], O_KV, 288), (ps[1][:], O_HF, 512), (ps[2][:], O_HI, 512)])

        def scan_E(j):
            act(sig2[j % 2][:], ps[1][:], AF.Exp, scale=-1.0)
            copy("act", Vb2[j % 2][:], ps[2][:])
            copy("act", zkvs2[j % 2][:], ps[0][:, 0:288])

        def scan_L(j):
            latents(zkvs2[j % 2], rp[j % 4], scan_cols=j * 128)

        def scan_G1(j):
            gates_fk(None, None, ez_done=True, p=j % 2)

        def scan_G2(j):
            gg, kk, Vb = gg2[j % 2], kk2[j % 2], Vb2[j % 2]
            mm(ps[3][:], cf[:, 0, C_TREVT:C_TREVT + 128], gg[:])
            act(erev[:], ps[3][:], AF.Exp)
            tt(Ke[:], kk[:], erev[:], ALU.mult)
            for h in range(4):
                mm(ps[5][:, h * 128:(h + 1) * 128], Ke[:, h * 128:(h + 1) * 128], Vb[:, h * 128:(h + 1) * 128])
            stt(Sown[:], S[:], selt[:, j % 4:j % 4 + 1], Sown[:], ALU.mult, ALU.add)
            scan_tile_state(j)

        def scan_tile_state(j):
            gg = gg2[j % 2]
            for h in range(4):
                mm(ps[4][:, 4 + h:5 + h], gg[:, h * 128:(h + 1) * 128], onesc)
            act(at4[:, 0:4], ps[4][:, 4:8], AF.Exp)
            for h in range(4):
                stt(S[:, h * 128:(h + 1) * 128], S[:, h * 128:(h + 1) * 128], at4[:, h:h + 1],
                    ps[5][:, h * 128:(h + 1) * 128], ALU.mult, ALU.add)

        ones_t = sb(st, "ones_t", [128, 2])
        memset("dve", ones_t[:], 1.0)
        onesc = ones_t[:, 0:1]

        def own_pre(m, smp):
            g = 1 if smp else 0
            row = m * 128
            dma("sp", xto[:], x_own[row:row + 128, :], "xto")
            dma("sp", rpo[:], rope_own[row:row + 128, :], "rpo")
            dma("sp", rpTo[:], ropeT_own[:, :, row:row + 128].rearrange("a p t -> p a t"), "rpTo")
            norm_only(xto, g, hbo)

        def own_tile(m, smp):
            g = 1 if smp else 0
            cs = 1 if smp else 0
            row = m * 128
            rpt = rpo
            rT = rpTo
            trans_only(hbo)
            zproj([(ps[6][:, 0:QL], O_Q, QL), (ps[7][:, 0:288], O_KV, 288),
                   (ps[0][:], O_HQ, 512), (ps[1][:], O_HF, 512), (ps[2][:], O_HI, 512), (ps[3][:], O_HG, 512)])

            def chain_q():
                act(junk2[:, 0:QL], ps[6][:, 0:QL], AF.Square, accum=st1[:, 6:7])
                copy("act", zkvo[:], ps[7][:, 0:288])
                rstd_act(st1[:, 2:3], st1[:, 7:8], st1[:, 6:7], QL)
                stt(cq[:], ps[6][:, 0:QL], st1[:, 2:3], qnb[:], ALU.mult, ALU.mult)
                pv = psbf(6)
                for c in range(3):
                    mm(pv[:, c * 128:(c + 1) * 128], cq[:, c * 128:(c + 1) * 128], ident_b(), tr=True)
                copy("act", cqT[:].rearrange("p a b -> p (a b)"), pv[:, 0:384])
                for hh in range(2):
                    for h in range(hh * 4, hh * 4 + 4):
                        pa = ps[6][0:96, (h % 4) * 128:(h % 4 + 1) * 128]
                        for c in range(3):
                            mm(pa, wq[:, c, h * 96:(h + 1) * 96], cqT[:, c, :], start=(c == 0), stop=(c == 2))
                    for h in range(hh * 4, hh * 4 + 4):
                        pb = ps[7][0:96, (h % 4) * 128:(h % 4 + 1) * 128]
                        for c in range(3):
                            mm(pb, wqr[:, c, h, :], cqT[:, c, :], start=(c == 0), stop=(c == 2))
                    pa3 = ps[6][0:96, :].rearrange("p (h t) -> p h t", h=4)
                    pb3 = ps[7][0:96, :].rearrange("p (h t) -> p h t", h=4)
                    qs = qTs[:, hh * 4:(hh + 1) * 4, :]
                    qt_ = qtmp[:, 0:4, :]
                    copy("act", qs[0:64], pa3[0:64])
                    tt(qt_[64:96], pa3[64:96], bc(rT[64:96, 0:1, :], [32, 4, 128]), ALU.mult)
                    tt(qs[64:96], pb3[64:96], bc(rT[64:96, 1:2, :], [32, 4, 128]), ALU.mult)
                    tt(qs[64:96], qs[64:96], qt_[64:96], ALU.add)
                dma("pool", qT_scr[:, :, row:row + 128], qTs[:], "qTo")
                latents(zkvo, rpt, out_row=row, smp=smp)

            def chain_h():
                hgrn_own(cs, smp, row)

            P.interleave(chain_h, chain_q)

        def hgrn_own(cs, smp, row):
            st1 = sth
            gates_fk(ps[1][:], ps[2][:])
            act(qq[:], ps[0][:], AF.Silu)
            act(sgt[:], ps[3][:], AF.Silu)
            tt(sgt[:].rearrange("p (h v) -> p h v", h=4), sgt[:].rearrange("p (h v) -> p h v", h=4),
               bc(hgnb[:].unsqueeze(1), [128, 4, 128]), ALU.mult)
            mm(ps[0][:], cf[:, cs, C_TREV:C_TREV + 128], gg[:])
            act(erev[:], ps[0][:], AF.Exp)
            tt(Ke[:], kk[:], erev[:], ALU.mult)
            tt(Kem[:], bc(Ke[:].unsqueeze(1), [128, 4, HW]),
               bc(cb[:, cs, C_CG + 256:C_CG + 260].unsqueeze(2), [128, 4, HW]), ALU.mult, eng="pool")
            CGb = [ps[1], ps[2]]
            QKb = [ps[3], ps[4]]
            idf = cf[:, 0, C_ID:C_ID + 128]
            for h in range(4):
                hs = slice(h * 128, (h + 1) * 128)
                o2 = (h % 2) * 256
                mm(CGb[h // 2][:, o2:o2 + 256], gg[:, hs], cf[:, cs, C_CG:C_CG + 256])
            for h in range(4):
                hs = slice(h * 128, (h + 1) * 128)
                mm(ps[5][:, h * 4:(h + 1) * 4], gg[:, hs], cf[:, cs, C_CG + 256:C_CG + 260])
            for h in range(4):
                hs = slice(h * 128, (h + 1) * 128)
                o2 = (h % 2) * 256
                mm(QKb[h // 2][:, o2:o2 + 128], qq[:, hs], idf)
                mm(QKb[h // 2][:, o2 + 128:o2 + 256], kk[:, hs], idf)
            for b2 in range(2):
                act(E12[:, 2 * b2:2 * b2 + 2, :, :].rearrange("p h x t -> p (h x t)"), CGb[b2][:], AF.Exp)
                act(E3[:, 2 * b2:2 * b2 + 2, :], CGb[b2][:].rearrange("p (h x t) -> p h x t", h=2, x=2)[:, :, 1, :], AF.Exp, scale=-1.0)
            act(aexp[:], ps[5][:, 0:16], AF.Exp)
            for b2 in range(2):
                qk4 = QKb[b2][:].rearrange("p (h x t) -> p h x t", h=2, x=2)
                h2 = slice(2 * b2, 2 * b2 + 2)
                stt(Qo4[:, h2, :], qk4[:, :, 0, :], HG_SCALE, E12[:, h2, 0, :], ALU.mult, ALU.mult)
                stt(Qd4[:, h2, :], qk4[:, :, 0, :], HG_SCALE, E12[:, h2, 1, :], ALU.mult, ALU.mult)
                tt(Kd4[:, h2, :], qk4[:, :, 1, :], E3[:, h2, :], ALU.mult)
            cm4 = cb[:, cs, C_COL:C_COL + 512].rearrange("p (c t) -> p c t", c=4)
            for h in range(4):
                tt(Qom4[:, h, :, :], bc(Qo4[:, h:h + 1, :], [128, 4, 128]), cm4, ALU.mult, eng="pool")
            DSb = [ps[1], ps[2], ps[3], ps[4]]
            for c in range(4):
                for h in range(4):
                    hs = slice(h * 128, (h + 1) * 128)
                    mm(DSb[c][:, hs], Kem[:, c, hs], Vb[:, hs])
            a4 = aexp[:].rearrange("p (h c) -> p h c", h=4)
            if smp:
                copy("act", Scb4[:].rearrange("p h c v -> p c h v"), S0s[:].rearrange("p c (h v) -> p c h v", h=4))
                for c in range(4):
                    tt(Scf4[:], S0s[:, c, :].rearrange("p (h v) -> p h v", h=4), bc(a4[:, :, c:c + 1], [128, 4, 128]), ALU.mult)
                    tt(Snew[:, c, :], Scf4[:].rearrange("p h v -> p (h v)"), DSb[c][:], ALU.add)
            else:
                copy("act", Scb4[:, :, 0, :], Sown[:].rearrange("p (h v) -> p h v", h=4))
                for c in range(3):
                    src_ = Sown[:].rearrange("p (h v) -> p h v", h=4) if c == 0 else Scf4[:]
                    tt(Scf4[:], src_, bc(a4[:, :, c:c + 1], [128, 4, 128]), ALU.mult)
                    tt(Scf4[:], Scf4[:], DSb[c][:].rearrange("p (h v) -> p h v", h=4), ALU.add)
                    copy("act", Scb4[:, :, c + 1, :], Scf4[:])
            for h in range(4):
                mm(ps[0][:, h * 128:(h + 1) * 128], Kd4[:, h, :], Qd4[:, h, :])
            tt(attb4[:], ps[0][:].rearrange("p (h t) -> p h t", h=4), bc(cf[:, cs, C_TRI:C_TRI + 128].unsqueeze(1), [128, 4, 128]), ALU.mult)
            for h in range(4):
                hs = slice(h * 128, (h + 1) * 128)
                po = ps[5][:, hs]
                for c in range(4):
                    mm(po, Qom4[:, h, c, :], Scb4[:, h, c, :], start=(c == 0), stop=False)
                mm(po, attb4[:, h, :], Vb[:, hs], start=False, stop=True)
            act(osq[:], ps[5][:], AF.Square)
            P.op("dve", lambda e: e.tensor_reduce(out=st1[:, 0:4], in_=osq[:].rearrange("p (h v) -> p h v", h=4), axis=AX.X, op=ALU.add),
                 r=[osq], w=[st1])
            rstd_act(st1[:, 0:4], st1[:, 4:8], st1[:, 0:4], 128)
            tt(og[:].rearrange("p (h v) -> p h v", h=4), ps[5][:].rearrange("p (h v) -> p h v", h=4),
               bc(st1[:, 0:4].unsqueeze(2), [128, 4, 128]), ALU.mult)
            tt(ogb[:], og[:], sgt[:], ALU.mult)
            pv = psbf(0)
            for h in range(4):
                mm(pv[:, h * 128:(h + 1) * 128], ogb[:, h * 128:(h + 1) * 128], ident_b(), tr=True)
            copy("act", oTs[:].rearrange("p a b -> p (a b)"), pv[:, 0:512])
            dma("pool", oT_scr[:, :, row:row + 128], oTs[:], "oTo")
            if not smp:
                memset("dve", Sown[:], 0.0)

        for m in range(NO):
            scan_A1(4 * m)
            scan_A2(4 * m)
            scan_A1(4 * m + 1)
            for rr_ in range(4):
                j = 4 * m + rr_
                scan_E(j)
                if rr_ < 3:
                    scan_A2(j + 1)
                if rr_ < 2:
                    scan_A1(j + 2)
                if rr_ == 2:
                    own_pre(m, False)
                if rr_ > 0:
                    P.interleave(lambda j=j: scan_G1(j), lambda j=j: scan_G2(j - 1), lambda j=j: scan_L(j - 1))
                else:
                    scan_G1(j)
            P.interleave(lambda: scan_G2(4 * m + 3), lambda: scan_L(4 * m + 3))
            own_tile(m, False)
        dma("sp", hst_p.rearrange("p h v -> p (h v)"), S[:], "hstp")
        W1.append(xt[0])
        B1.append(xt[1])
        dma("sp", xt[0][:], modt[6], "xt" + xt[0].name)
        dma("sp", xt[1][:], modt[7], "xt" + xt[1].name)
        own_pre(NO, True)
        own_tile(NO, True)
        dma("sp", hst_s.rearrange("p b h v -> p b (h v)"), Snew[:], "hsts")
        P.barrier()
        P.emit()

    with ExitStack() as st:
        TK = max(T, PAST + 128)
        CK = sb(st, "CK", [128, 2, TK], BF16)
        KT = sb(st, "KT", [96, TK], BF16)
        Va = sb(st, "Va", [128, TK // 128 + 1, 128], BF16)
        wkv = sb(st, "wkv", [128, 2, 1024], BF16)
        QTh = [sb(st, f"QTh{i}", [96, NTOK], BF16) for i in range(2)]
        PT = [sb(st, f"PT{i}", [128, 512], BF16) for i in range(3)]
        mdg = sb(st, "mdg", [128, 4, 128])
        mdgb = sb(st, "mdgb", [128, 4, 128], BF16)
        rsum = sb(st, "rsum", [128, 512])
        aTn = [sb(st, f"aTn{i}", [64, 512], BF16) for i in range(2)]
        QTs = sb(st, "QTs", [96, NH, 128], BF16)
        load_w(wkv, w_ukv, 1024, 2)
        dma("sp", mdg[:], mdiag, "mdg")
        copy("dve", mdgb[:], mdg[:])
        memset("pool", Va[:, :, 64:128], 1.0)
        pti = {"i": 0}
        evi = {"i": 0}

        def expand_head(h, ntile, nk):
            for c0 in range(0, nk, 512):
                n = min(512, nk - c0)
                pk = ps[evi["i"] % 2]
                evi["i"] += 1
                for c in range(2):
                    mm(pk[0:64, 0:n], wkv[:, c, h * 128:h * 128 + 64], CK[:, c, c0:c0 + n], start=(c == 0), stop=(c == 1))
                copy("dve" if (c0 // 512) % 2 == 0 else "pool" if False else "dve", KT[0:64, c0:c0 + n], pk[0:64, 0:n],
                     w=[f"KT{c0 // 512}"])
            for t0 in range(0, ntile, 8):
                nt_ = min(8, ntile - t0)
                pvv = ps[2]
                for t in range(nt_):
                    kt = t0 + t
                    rows = min(128, nk - kt * 128)
                    for c in range(2):
                        mm(pvv[0:rows, t * 64:(t + 1) * 64], CK[:, c, kt * 128:kt * 128 + rows], wkv[:, c, h * 128 + 64:h * 128 + 128],
                           start=(c == 0), stop=(c == 1))
                rows_all = 128 if (t0 + nt_) * 128 <= nk else None
                if rows_all is not None:
                    copy("dve", Va[:, t0:t0 + nt_, 0:64], pvv[:, 0:nt_ * 64].rearrange("p (t v) -> p t v", v=64), w=[f"Va{t0 // 8}"])
                else:
                    for t in range(nt_):
                        kt = t0 + t
                        rows = min(128, nk - kt * 128)
                        copy("dve", Va[0:rows, kt, 0:64], pvv[0:rows, t * 64:(t + 1) * 64], w=[f"Va{t0 // 8}"])

        def attend(h, qt, qcols, ktiles, out_cols, nq, diag_base=None):
            LOOK = 2
            po = ps[3 + (pti["i"] % 2)]
            pti["i"] += 1
            n_k = len(ktiles)
            for it in range(n_k + LOOK):
                if it < n_k:
                    kt, rows, qoff = ktiles[it]
                    pS = ps[5 + it % 3]
                    mm(pS[0:rows, qoff:nq], KT[:, kt * 128:kt * 128 + rows], qt[:, qcols + qoff:qcols + nq],
                       r=[f"KT{kt // 4}", qt])
                idx = it - LOOK
                if idx < 0:
                    continue
                kt, rows, qoff = ktiles[idx]
                pS = ps[5 + idx % 3]
                pt = PT[idx % 3]
                act(pt[0:rows, qoff:nq], pS[0:rows, qoff:nq], AF.Exp, scale=MLA_SCALE)
                if diag_base is not None and kt >= diag_base:
                    loc = kt - diag_base
                    r_ = loc // 4
                    tt(pt[:, r_ * 128:(r_ + 1) * 128], pt[:, r_ * 128:(r_ + 1) * 128], mdgb[:, loc % 4, :], ALU.mult, eng="pool")
                mm(po[:, qoff:nq], Va[0:rows, kt, :], pt[0:rows, qoff:nq], start=(idx == 0), stop=(idx == n_k - 1),
                   r=[f"Va{kt // 8}", pt])
            return po

        def finish(po, h, out_cols, nq):
            an = aTn[h % 2]
            copy("dve", rsum[64:128, 0:nq], po[64:128, 0:nq])
            recip(rsum[64:128, 0:nq], rsum[64:128, 0:nq])
            tt(an[:, 0:nq], po[0:64, 0:nq], rsum[64:128, 0:nq], ALU.mult)
            dma("pool", aT_scr[:, h, out_cols:out_cols + nq], an[:, 0:nq], "aTo" + an.name)

        CH = min(T, 4096)
        for c in range(2):
            for t0 in range(0, T, CH):
                dma("sp" if c == 0 else "pool", CK[:, c, t0:t0 + CH], ckT_scr[:, c, t0:t0 + CH], f"CK{c}")
        for t0 in range(0, T, CH):
            dma("sp", KT[64:96, t0:t0 + CH], krT_scr[:, t0:t0 + CH], "KTr", w=[f"KT{g}" for g in range(TK // 512 + 2)])
        for h in range(NH):
            qt = QTh[h % 2]
            dma("sp", qt[:], qT_scr[:, h, :], "QTh" + qt.name)
            expand_head(h, NT, T)
            for g in range(NG):
                kts = [(kt, 128, 0) for kt in range(16 * g)]
                kts += [(16 * g + loc, 128, (loc // 4) * 128) for loc in range(16)]
                po = attend(h, qt, g * 512, kts, g * 512, 512, diag_base=16 * g)
                finish(po, h, g * 512, 512)
        nk_s = PAST + 16
        nts = NKS + 1
        allkt = [f"KT{g}" for g in range(TK // 512 + 2)]
        dma("sp", QTs[:], qT_scr[:, :, NO * 128:NO * 128 + 128], "QTs")
        zpad = sb(st, "zpad", [64, NH, 64], BF16)
        memset("pool", zpad[:], 0.0)
        dma("pool", aT_scr[:, :, NO * 128 + 64:NO * 128 + 128], zpad[:], "zpad")
        for b in range(4):
            si = 0
            for c in range(2):
                for p0 in range(0, PAST, 1024):
                    pn = min(1024, PAST - p0)
                    s_ = stg[si % 2]
                    si += 1
                    dma("sp", s_[:, 0:pn], ckvT_c[b, :, c, p0:p0 + pn], "stg" + s_.name)
                    copy("dve" if si % 2 == 0 else "pool", CK[:, c, p0:p0 + pn], s_[:, 0:pn])
            for p0 in range(0, PAST, 1024):
                pn = min(1024, PAST - p0)
                s_ = stg[si % 2]
                si += 1
                dma("sp", s_[0:32, 0:pn], krT_c[b, :, p0:p0 + pn], "stg" + s_.name)
                copy("act", KT[64:96, p0:p0 + pn], s_[0:32, 0:pn], w=allkt)
            dma("sp", CK[:, :, PAST:PAST + 16], ckTs_scr[:, :, b * 16:(b + 1) * 16], "CKn")
            dma("sp", KT[64:96, PAST:PAST + 16], krTs_scr[:, b * 16:(b + 1) * 16], "KTn", w=allkt)
            for h in range(NH):
                expand_head(h, nts, nk_s)
                kts = [(kt, min(128, nk_s - kt * 128), 0) for kt in range(nts)]
                po = attend(h, QTs[:, h, :], b * 16, kts, NO * 128 + b * 16, 16)
                finish(po, h, NO * 128 + b * 16, 16)
        P.barrier()
        P.emit()

    def make_norm(Wt, Bt, junk, tmpf, hb, st1, hT_dst_fn):
        def norm_T(xtile, g, k=0):
            act(junk[:], xtile[:], AF.Square, accum=st1[:, 0:1])
            act(st1[:, 1:2], st1[:, 0:1], AF.Sqrt, bias=EPS, scale=1.0 / D)
            recip(st1[:, 2:3], st1[:, 1:2])
            stt(tmpf[:], xtile[:], st1[:, 2:3], Wt[g][:], ALU.mult, ALU.mult)
            tt(hb[:], tmpf[:], Bt[g][:], ALU.add)
            pv = psbf(7)
            for kc in range(KC):
                mm(pv[:, kc * 128:(kc + 1) * 128], hb[:, kc * 128:(kc + 1) * 128], ident_b(), tr=True)
            copy("act", hT_dst_fn(k), pv[:, 0:1024].rearrange("p (a b) -> p a b", a=KC))
        return norm_T

    with ExitStack() as st:
        wG = sb(st, "wG", [128, KC, 2048], BF16)
        wpa = sb(st, "wpa", [128, 4, D], BF16)
        wpb = sb(st, "wpb", [128, 4, D], BF16)
        wo = sb(st, "wo", [128, KC, D], BF16)
        W1 = [sb(st, f"W1_{g}", [128, D]) for g in range(2)]
        B1 = [sb(st, f"B1_{g}", [128, D]) for g in range(2)]
        G1 = [sb(st, f"G1_{g}", [128, D]) for g in range(2)]
        xt = [sb(st, f"xt{i}", [128, D]) for i in range(3)]
        aTt = [sb(st, f"aTt{i}", [128, 4, 128], BF16) for i in range(3)]
        oTt = [sb(st, f"oTt{i}", [128, 4, 128], BF16) for i in range(3)]
        junk = sb(st, "junk", [128, D], BF16)
        tmpf = sb(st, "tmpf", [128, D])
        hbs = [sb(st, f"hb{i}", [128, D], BF16) for i in range(2)]
        hTs = [sb(st, f"hT{i}", [128, KC, 128], BF16) for i in range(2)]
        st3 = [sb(st, f"st3_{i}", [128, 4]) for i in range(2)]
        sgq = [sb(st, f"sgq{i}", [128, 512]) for i in range(2)]
        tq = [sb(st, f"tq{i}", [128, 512]) for i in range(2)]
        mixb = sb(st, "mixb", [128, D], BF16)
        mT = sb(st, "mT", [128, KC, 128], BF16)
        x1 = [sb(st, f"x1_{i}", [128, D]) for i in range(2)]
        load_w(wG, w_in, 2048, KC, c0=O_GA, extra=[tmpf])
        load_w(wpa, w_pa, D, 4, extra=[tmpf])
        load_w(wpb, w_pb, D, 4, extra=[tmpf])
        load_w(wo, w_out, D, KC, extra=[tmpf])
        for g in range(2):
            dma("sp", W1[g][:], modt[g * 6 + 0], f"W1{g}")
            dma("sp", B1[g][:], modt[g * 6 + 1], f"B1{g}")
            dma("sp", G1[g][:], modt[g * 6 + 2], f"G1{g}")
        aT_v = aT_scr.rearrange("d (c hh) t -> d hh c t", hh=2)

        def p3_load(m):
            row = m * 128
            p = m % 3
            xs, at_, ot_ = xt[p], aTt[p], oTt[p]
            dma("sp", xs[:], x_own[row:row + 128, :], "xt" + xs.name)
            for hh in range(2):
                dma("sp", at_[hh * 64:(hh + 1) * 64, :, :], aT_v[:, hh, :, row:row + 128], f"aTt{hh}" + at_.name)
            dma("sp", ot_[:], oT_scr[:, :, row:row + 128], "oTt" + ot_.name)

        def p3_load_norm(m):
            g = 1 if m == NO else 0
            p = m % 2
            xs, s3, hb_ = xt[m % 3], st3[p], hbs[p]
            act(junk[:], xs[:], AF.Square, accum=s3[:, 0:1])
            rstd_act(s3[:, 2:3], s3[:, 1:2], s3[:, 0:1], D)
            stt(tmpf[:], xs[:], s3[:, 2:3], W1[g][:], ALU.mult, ALU.mult)
            tt(hb_[:], tmpf[:], B1[g][:], ALU.add)

        def p3_trans(m):
            pv = psbf(4)
            hb_ = hbs[m % 2]
            for kc in range(KC):
                mm(pv[:, kc * 128:(kc + 1) * 128], hb_[:, kc * 128:(kc + 1) * 128], ident_b(), tr=True)

        def p3_tcopy(m):
            copy("act", hTs[m % 2][:], psbf(4)[:, 0:1024].rearrange("p (a b) -> p a b", a=KC))

        def p3_quarter_mm(m, q):
            p = m % 2
            hT_, at_, ot_ = hTs[p], aTt[m % 3], oTt[m % 3]
            A, B = ps[(q % 2) * 2], ps[(q % 2) * 2 + 1]
            c0 = q * 256
            for kc in range(KC):
                mm(A[:, 0:256], hT_[:, kc, :], wG[:, kc, c0:c0 + 256], start=(kc == 0), stop=(kc == KC - 1))
            for kc in range(KC):
                mm(A[:, 256:512], hT_[:, kc, :], wG[:, kc, D + c0:D + c0 + 256], start=(kc == 0), stop=(kc == KC - 1))
            for c in range(4):
                mm(B[:, 0:256], at_[:, c, :], wpa[:, c, c0:c0 + 256], start=(c == 0), stop=(c == 3))
            for c in range(4):
                mm(B[:, 256:512], ot_[:, c, :], wpb[:, c, c0:c0 + 256], start=(c == 0), stop=(c == 3))

        def p3_quarter_ew(m, q):
            A, B = ps[(q % 2) * 2], ps[(q % 2) * 2 + 1]
            sg_, t_ = sgq[q % 2], tq[q % 2]
            act(sg_[:], A[:], AF.Exp, scale=-1.0)
            act(sg_[:], sg_[:], AF.Ln, bias=1.0)
            act(sg_[:], sg_[:], AF.Exp, scale=-1.0)
            tt(t_[:], sg_[:], B[:], ALU.mult)
            tt(mixb[:, q * 256:(q + 1) * 256], t_[:, 0:256], t_[:, 256:512], ALU.add)

        def p3_out(m):
            g = 1 if m == NO else 0
            row = m * 128
            xs, xo = xt[m % 3], x1[m % 2]
            pv = psbf(5)
            for kc in range(KC):
                mm(pv[:, kc * 128:(kc + 1) * 128], mixb[:, kc * 128:(kc + 1) * 128], ident_b(), tr=True)
            copy("act", mT[:], pv[:, 0:1024].rearrange("p (a b) -> p a b", a=KC))
            for kc in range(KC):
                for hf in range(2):
                    mm(ps[6 + hf][:], mT[:, kc, :], wo[:, kc, hf * 512:(hf + 1) * 512], start=(kc == 0), stop=(kc == KC - 1))
            for hf in range(2):
                cs_ = slice(hf * 512, (hf + 1) * 512)
                tt(xo[:, cs_], ps[6 + hf][:], G1[g][:, cs_], ALU.mult)
                tt(xo[:, cs_], xo[:, cs_], xs[:, cs_], ALU.add)
            dma("pool", x1_scr[row:row + 128, :], xo[:], "x1o" + xo.name)

        p3_load(0)
        p3_load_norm(0)
        p3_trans(0)
        p3_tcopy(0)
        if NO1 > 1:
            p3_load(1)
        if NO1 > 2:
            p3_load(2)
        p3_quarter_mm(0, 0)
        p3_quarter_mm(0, 1)
        for m in range(NO1):
            if m + 1 < NO1:
                p3_load_norm(m + 1)
            p3_quarter_ew(m, 0)
            p3_quarter_mm(m, 2)
            p3_quarter_ew(m, 1)
            p3_quarter_mm(m, 3)
            if m + 1 < NO1:
                p3_trans(m + 1)
            p3_quarter_ew(m, 2)
            p3_quarter_ew(m, 3)
            if m + 1 < NO1:
                p3_tcopy(m + 1)
                p3_quarter_mm(m + 1, 0)
                p3_quarter_mm(m + 1, 1)
            p3_out(m)
            if m + 3 < NO1:
                p3_load(m + 3)
        P.barrier()
        P.emit()

    with ExitStack() as st:
        GT = 2
        wgu = sb(st, "wgu", [128, KC, 2 * DFF], BF16)
        wd = sb(st, "wd", [128, NFC, D], BF16)
        W2 = sb(st, "W2", [128, D])
        B2 = sb(st, "B2", [128, D])
        G2 = sb(st, "G2", [128, D])
        fnb = sb(st, "fnb", [128, D])
        xt = [[sb(st, f"xt{s}_{i}", [128, D]) for i in range(GT)] for s in range(2)]
        junk = sb(st, "junk", [128, D], BF16)
        tmpf = sb(st, "tmpf", [128, D])
        hb = sb(st, "hb", [128, D], BF16)
        hT4 = [sb(st, f"hT4_{s}", [128, KC, GT * 128], BF16) for s in range(2)]
        st4 = sb(st, "st4", [128, 4])
        st5 = sb(st, "st5", [128, 4])
        sgl = [sb(st, f"sgl{i}", [128, GT * 128]) for i in range(2)]
        actT = sb(st, "actT", [128, NFC, GT * 128], BF16)
        load_w(wgu, w_gu, 2 * DFF, KC, extra=[tmpf])
        load_w(wd, w_down, D, NFC, extra=[tmpf])
        dma("sp", fnb[:], nrm[2:3, :].partition_broadcast(128), "fnb")
        groups = [list(range(a, min(a + GT, NO))) for a in range(0, NO, GT)] + [[NO]]
        cur = {"g": -1}

        def p4_A(gi):
            grp = groups[gi]
            s = gi % 2
            g = 1 if grp[0] == NO else 0
            if g != cur["g"]:
                dma("sp", W2[:], modt[g * 6 + 3], "W2")
                dma("sp", B2[:], modt[g * 6 + 4], "B2")
                cur["g"] = g
            for k, m in enumerate(grp):
                xk = xt[s][k]
                dma("sp", xk[:], x1_scr[m * 128:(m + 1) * 128, :], "xt" + xk.name)
                act(junk[:], xk[:], AF.Square, accum=st4[:, 0:1])
                rstd_act(st4[:, 2:3], st4[:, 1:2], st4[:, 0:1], D)
                stt(tmpf[:], xk[:], st4[:, 2:3], W2[:], ALU.mult, ALU.mult)
                tt(hb[:], tmpf[:], B2[:], ALU.add)
                pv = psbf(7)
                for kc in range(KC):
                    mm(pv[:, kc * 128:(kc + 1) * 128], hb[:, kc * 128:(kc + 1) * 128], ident_b(), tr=True)
                copy("act", hT4[s][:, :, k * 128:(k + 1) * 128], pv[:, 0:1024].rearrange("p (a b) -> p a b", a=KC))

        def p4_B(gi):
            grp = groups[gi]
            s = gi % 2
            n = len(grp) * 128
            for c in range(NFC):
                pg, pu = ps[(c % 2) * 2], ps[(c % 2) * 2 + 1]
                for kc in range(KC):
                    mm(pg[:, 0:n], wgu[:, kc, c * 128:(c + 1) * 128], hT4[s][:, kc, 0:n], start=(kc == 0), stop=(kc == KC - 1))
                for kc in range(KC):
                    mm(pu[:, 0:n], wgu[:, kc, DFF + c * 128:DFF + (c + 1) * 128], hT4[s][:, kc, 0:n], start=(kc == 0), stop=(kc == KC - 1))
                sg_ = sgl[c % 2]
                act(sg_[:, 0:n], pg[:, 0:n], AF.Silu)
                tt(actT[:, c, 0:n], sg_[:, 0:n], pu[:, 0:n], ALU.mult)

        def p4_C(gi):
            grp = groups[gi]
            s = gi % 2
            g = 1 if grp[0] == NO else 0
            if g == 1 or gi == 0:
                dma("sp", G2[:], modt[g * 6 + 5], "G2")
            for k, m in enumerate(grp):
                for c in range(NFC):
                    for hf in range(2):
                        mm(ps[4 + hf][:], actT[:, c, k * 128:(k + 1) * 128], wd[:, c, hf * 512:(hf + 1) * 512], start=(c == 0), stop=(c == NFC - 1))
                xk = xt[s][k]
                for hf in range(2):
                    cs_ = slice(hf * 512, (hf + 1) * 512)
                    tt(tmpf[:, cs_], ps[4 + hf][:], G2[:, cs_], ALU.mult)
                    tt(xk[:, cs_], xk[:, cs_], tmpf[:, cs_], ALU.add)
                act(junk[:], xk[:], AF.Square, accum=st5[:, 0:1])
                rstd_act(st5[:, 2:3], st5[:, 1:2], st5[:, 0:1], D)
                stt(xk[:], xk[:], st5[:, 2:3], fnb[:], ALU.mult, ALU.mult)
                dma("pool", y_o[m * 128:(m + 1) * 128, :], xk[:], "yo" + xk.name)

        p4_A(0)
        for gi in range(len(groups)):
            p4_B(gi)
            if gi + 1 < len(groups):
                p4_A(gi + 1)
            p4_C(gi)
        P.barrier()
        P.emit()
    return nc, stack, P


def _rope_tabs(pos):
    half = 16
    inv = (np.float32(10000.0) ** (-np.arange(half, dtype=np.float32) / np.float32(half))).astype(np.float32)
    ang = pos.astype(np.float32)[:, None] * inv[None, :]
    return np.cos(ang).astype(np.float32), np.sin(ang).astype(np.float32)


def make_inputs(inp, T, PAST, n_cores=8):
    NT = T // 128
    NO = NT // 4
    NTOK = (NO + 1) * 128
    f = np.float32
    g = lambda k: np.asarray(inp[k], dtype=f)
    xp, xs_ = g("x_prompt"), g("x_sample")
    cst = np.zeros((128, 2, 1284), f)
    for si, (sbk, rows) in enumerate(((32, 128), (16, 64))):
        cg, trev, tri, colmask, ind = _consts(sbk, rows)
        cst[:, si, 0:260] = cg
        cst[:, si, 260:388] = trev
        cst[:, si, 388:516] = tri
        s = np.arange(128)
        cst[:, si, 516:644] = (s[:, None] > s[None, :]).astype(f)
        cst[:, si, 644:772] = np.eye(128, dtype=f)
        cst[:, si, 772:1284] = colmask
    w_uq = g("w_uq")[0]
    perm = []
    for h in range(NH):
        base = h * 96 + 64
        perm += [base + 16 + d for d in range(16)] + [base + d for d in range(16)]
    w_uqr = np.ascontiguousarray(w_uq[:, perm])
    oh5 = np.zeros((5, 256), f)
    oh5[0, 0:128] = 1.0
    for b in range(4):
        oh5[1 + b, 128 + 16 * b:128 + 16 * (b + 1)] = 1.0
    shared = {
        "w_ada": g("w_ada")[0], "b_ada": g("b_ada"), "w_in": g("w_in")[0], "w_uq": w_uq, "w_uqr": w_uqr,
        "w_ukv": g("w_ukv")[0], "w_pa": g("w_pa")[0], "w_pb": g("w_pb")[0], "w_out": g("w_out")[0],
        "w_gu": g("w_gu")[0], "w_down": g("w_down")[0],
        "nrm": np.stack([g("norm1")[0], g("norm2")[0], g("final_norm")]),
        "q_norm": g("q_norm"), "kv_norm": g("kv_norm"), "hg_norm": g("hg_norm"), "lb_param": g("lb_param"),
        "cst": cst, "oh5": oh5,
    }
    pos_seq = np.arange(T)
    cs_seq, sn_seq = _rope_tabs(pos_seq)
    rope_seq = np.concatenate([cs_seq, cs_seq, -sn_seq, sn_seq], axis=1)
    pos_s = np.zeros(128, np.int64)
    for b in range(4):
        pos_s[16 * b:16 * (b + 1)] = PAST + np.arange(16)
    maps = []
    for c in range(n_cores):
        b, i = c // 4, c % 4
        own = np.concatenate([np.arange((4 * m + i) * 128, (4 * m + i + 1) * 128) for m in range(NO)])
        xsm = np.zeros((128, D), f)
        xsm[0:64] = xs_[4 * c:4 * c + 4].reshape(64, D)
        pos_own = np.concatenate([own, pos_s])
        co, so = _rope_tabs(pos_own)
        rope_own = np.concatenate([co, co, -so, so], axis=1)
        ropeT = np.zeros((2, 96, NTOK), f)
        ropeT[0, 64:96] = np.concatenate([co, co], axis=1).T
        ropeT[1, 64:96] = np.concatenate([-so, so], axis=1).T
        c5 = np.concatenate([g("c_prompt")[b:b + 1], g("c_sample")[4 * c:4 * c + 4]], axis=0)
        c5T = np.ascontiguousarray(c5.T.reshape(KC, 128, 5).transpose(1, 0, 2))
        sel = np.zeros((128, 4), f)
        sel[:, i] = 1.0
        md = np.zeros((128, 4, 128), f)
        for r in range(4):
            if r < i:
                md[:, r, :] = 1.0
            elif r == i:
                kk_ = np.arange(128)[:, None] // 64
                qq_ = np.arange(128)[None, :] // 64
                md[:, r, :] = (kk_ <= qq_).astype(f)
        ck = g("cache_ckv")[0, 4 * c:4 * c + 4]
        ckvT = np.ascontiguousarray(ck.reshape(4, PAST, 2, 128).transpose(0, 3, 2, 1))
        krT = np.ascontiguousarray(g("cache_krope")[0, 4 * c:4 * c + 4].transpose(0, 2, 1))
        sts = np.ascontiguousarray(g("state_hgrn")[0, 4 * c:4 * c + 4].transpose(2, 0, 1, 3))
        m = dict(shared)
        m.update({
            "x_seq": np.ascontiguousarray(xp[b]), "x_own": np.concatenate([xp[b][own], xsm], axis=0),
            "rope_seq": rope_seq.astype(f), "rope_own": rope_own.astype(f), "ropeT_own": ropeT,
            "c5T": c5T, "sel": sel, "mdiag": md, "ckvT_c": ckvT, "krT_c": krT, "state_s": sts,
        })
        maps.append(m)
    return maps


def assemble(results, T, PAST, n_cores=8):
    NT = T // 128
    NO = NT // 4
    f = np.float32
    nb = n_cores // 4
    y_p = np.zeros((nb, T, D), f)
    ckv_p = np.zeros((1, nb, T, KVL), f)
    kr_p = np.zeros((1, nb, T, RD), f)
    hs_p = np.zeros((1, nb, 4, 128, 128), f)
    y_s = np.zeros((4 * n_cores, 16, D), f)
    ckv_s = np.zeros((1, 4 * n_cores, 16, KVL), f)
    kr_s = np.zeros((1, 4 * n_cores, 16, RD), f)
    hs_s = np.zeros((1, 4 * n_cores, 4, 128, 128), f)
    for c in range(n_cores):
        r = results[c]
        b, i = c // 4, c % 4
        own = np.concatenate([np.arange((4 * m + i) * 128, (4 * m + i + 1) * 128) for m in range(NO)])
        y_p[b, own] = r["y_o"][0:NO * 128]
        ckv_p[0, b, own] = r["ckv_o"][0:NO * 128]
        kr_p[0, b, own] = r["kr_o"][0:NO * 128]
        if i == 0:
            hs_p[0, b] = r["hst_p"].transpose(1, 0, 2)
        y_s[4 * c:4 * c + 4] = r["y_o"][NO * 128:NO * 128 + 64].reshape(4, 16, D)
        ckv_s[0, 4 * c:4 * c + 4] = r["ckv_o"][NO * 128:NO * 128 + 64].reshape(4, 16, KVL)
        kr_s[0, 4 * c:4 * c + 4] = r["kr_o"][NO * 128:NO * 128 + 64].reshape(4, 16, RD)
        hs_s[0, 4 * c:4 * c + 4] = r["hst_s"].transpose(1, 2, 0, 3)
    return (y_p, y_s, ckv_p, kr_p, hs_p, ckv_s, kr_s, hs_s)


def kernel(**inputs):
    T, PAST = 16384, 2048
    nc, stack, _ = build(T, PAST)
    maps = make_inputs(inputs, T, PAST)
    res = run_bass_kernel_spmd(nc, maps, core_ids=list(range(8)))
    stack.close()
    return assemble(res.results, T, PAST)
```

```python
import math
import os
from contextlib import ExitStack
import numpy as np
import concourse.bass as bass
import concourse.mybir as mybir
from concourse.bass_utils import run_bass_kernel_spmd

F32 = mybir.dt.float32
BF16 = mybir.dt.bfloat16
AF = mybir.ActivationFunctionType
ALU = mybir.AluOpType
AX = mybir.AxisListType

D = 1024
KC = 8
QL, KVL, RD = 384, 256, 32
NH, QKN, VH = 8, 64, 64
HGH, HGK = 4, 128
HW = 512
DFF = 2816
NFC = DFF // 128
EPS = 1e-6
MLA_SCALE = 1.0 / math.sqrt(96.0)
HG_SCALE = 1.0 / math.sqrt(128.0)
O_Q, O_KV, O_HQ, O_HF, O_HI, O_HG, O_GA = 0, 384, 672, 1184, 1696, 2208, 2720
NA = 2720


class _Op:
    __slots__ = ("eng", "fn", "dma", "deps", "odeps", "signal", "tok", "cost", "idx", "pos", "fin")

    def __init__(self, eng, fn, dma, cost):
        self.eng, self.fn, self.dma, self.cost = eng, fn, dma, cost
        self.deps, self.odeps, self.signal, self.tok = [], [], False, None
        self.idx = self.pos = 0
        self.fin = 0.0


SYNC_NS = 300.0


class Prog:
    def __init__(self, nc, stack):
        self.nc, self.stack = nc, stack
        self.engs = {"pe": nc.tensor, "act": nc.scalar, "dve": nc.vector, "pool": nc.gpsimd, "sp": nc.sync}
        self.sem, self.cnt = {}, {}
        self.seen = {e: {} for e in self.engs}
        self.last, self.readers = {}, {}
        self.ops = []
        self.nops = 0
        self.reorder = False
        self.cap = None
        self.reorder_engs = set(os.environ.get('REORDER_ENGS', 'pe,act,dve,pool,sp').split(','))

    def _sem(self, name):
        if name not in self.sem:
            self.sem[name] = self.stack.enter_context(self.nc.semaphore("s_" + name))
            self.cnt[name] = 0
        return self.sem[name]

    @staticmethod
    def _key(k):
        return k if isinstance(k, str) else k.name

    def op(self, eng, fn, r=(), w=(), dma=None, cost=300.0):
        if self.cap is not None:
            self.cap.append((eng, fn, list(r), list(w), dma, cost))
            return None
        o = _Op(eng, fn, dma, cost)
        deps = {}
        rk = [self._key(k) for k in r]
        wk = [self._key(k) for k in w]
        for k in rk:
            lw = self.last.get(k)
            if lw is not None:
                deps[id(lw)] = lw
        for k in wk:
            lw = self.last.get(k)
            if lw is not None:
                deps[id(lw)] = lw
            for rd in self.readers.get(k, ()):
                deps[id(rd)] = rd
        for p in deps.values():
            if p is o:
                continue
            o.odeps.append(p)
            if p.eng == eng and p.dma is None and dma is None and eng == "pe":
                continue
            o.deps.append(p)
        for k in rk:
            self.readers.setdefault(k, []).append(o)
        for k in wk:
            self.last[k] = o
            self.readers[k] = []
        o.idx = len(self.ops)
        self.ops.append(o)
        return o

    def interleave(self, *thunks):
        chains = []
        for t in thunks:
            self.cap = []
            t()
            chains.append(self.cap)
        self.cap = None
        idx = [0] * len(chains)
        while True:
            best, bk = None, None
            for k, ch in enumerate(chains):
                if idx[k] < len(ch):
                    fr = (idx[k] + 0.5) / len(ch)
                    if best is None or fr < best:
                        best, bk = fr, k
            if bk is None:
                break
            self.op(*chains[bk][idx[bk]])
            idx[bk] += 1

    def barrier(self):
        self._barrier = True
        self.last, self.readers = {}, {}

    def _schedule(self, ops):
        import heapq
        n = len(ops)
        if not self.reorder:
            for i, o in enumerate(ops):
                o.pos = i
            return {e: [o for o in ops if o.eng == e] for e in self.engs}
        ndep = [len(o.odeps) for o in ops]
        users = [[] for _ in range(n)]
        for o in ops:
            for p in o.odeps:
                users[p.idx].append(o)
        ready_t = [0.0] * n
        heaps = {e: [] for e in self.engs}
        for o in ops:
            if ndep[o.idx] == 0:
                heapq.heappush(heaps[o.eng], (0.0, o.idx))
        tE = {e: 0.0 for e in self.engs}
        order = {e: [] for e in self.engs}
        done = 0
        WIN = 600
        nxt = 0
        sched = [False] * n
        byeng = {e: [o for o in ops if o.eng == e] for e in self.engs}
        pq = {e: 0 for e in self.engs}
        while done < n:
            best = None
            for e in self.engs:
                h = heaps[e]
                if not h:
                    continue
                cand = None
                tmp = []
                if e not in self.reorder_engs:
                    while pq[e] < len(byeng[e]) and sched[byeng[e][pq[e]].idx]:
                        pq[e] += 1
                    if pq[e] < len(byeng[e]):
                        want = byeng[e][pq[e]].idx
                        found = None
                        for it in h:
                            if it[1] == want:
                                found = it
                                break
                        if found is not None:
                            h.remove(found)
                            heapq.heapify(h)
                            cand = found
                    h_iter = []
                else:
                    h_iter = h
                while h_iter:
                    rt, i = heapq.heappop(h)
                    if i <= nxt + WIN:
                        cand = (rt, i)
                        break
                    tmp.append((rt, i))
                for it in tmp:
                    heapq.heappush(h, it)
                if cand is None:
                    continue
                st = max(tE[e], cand[0])
                if best is None or (st, cand[1]) < (best[0], best[2]):
                    if best is not None:
                        heapq.heappush(heaps[best[1]], (best[3], best[2]))
                    best = (st, e, cand[1], cand[0])
                else:
                    heapq.heappush(h, cand)
            if best is None:
                WIN *= 2
                continue
            st, e, i, rt = best
            o = ops[i]
            issue = 60.0 if o.dma is not None else o.cost
            o.fin = st + (o.cost if o.dma is None else o.cost)
            tE[e] = st + issue
            o.pos = len(order[e])
            order[e].append(o)
            sched[i] = True
            done += 1
            while nxt < n and sched[nxt]:
                nxt += 1
            for u in users[i]:
                ndep[u.idx] -= 1
                lat = 0.0 if (u.eng == e and o.dma is None) else SYNC_NS
                ready_t[u.idx] = max(ready_t[u.idx], o.fin + lat)
                if ndep[u.idx] == 0:
                    heapq.heappush(heaps[u.eng], (ready_t[u.idx], u.idx))
        return order

    def emit(self):
        ops = self.ops
        self.nops += len(ops)
        order = self._schedule(ops)
        for o in ops:
            best = {}
            keep = []
            for p in o.deps:
                if p.dma is not None:
                    keep.append(p)
                else:
                    b = best.get(p.eng)
                    if b is None or p.pos > b.pos:
                        best[p.eng] = p
            o.deps = keep + list(best.values())
            for p in o.deps:
                p.signal = True
        lasts = []
        if getattr(self, "_barrier", False):
            for e in self.engs:
                if order[e]:
                    cl = [o for o in order[e] if o.dma is None]
                    if cl:
                        cl[-1].signal = True
                        lasts.append(cl[-1])
            dl = {}
            for o in ops:
                if o.dma is not None:
                    dl[o.dma] = o if (o.dma not in dl or True) else dl[o.dma]
            lasts_dma = set(o.dma for o in ops if o.dma is not None)
        for e in self.engs:
            for o in order[e]:
                if o.dma is not None:
                    self._sem(o.dma)
                    self.cnt[o.dma] += 16
                    o.tok = (o.dma, self.cnt[o.dma])
                elif o.signal:
                    self._sem(o.eng)
                    self.cnt[o.eng] += 1
                    o.tok = (o.eng, self.cnt[o.eng])
        bar = []
        if getattr(self, "_barrier", False):
            bar = [p.tok for p in lasts] + [(s, self.cnt[s]) for s in lasts_dma]
            self._barrier = False

        def run(ename):
            def f(e):
                seen = self.seen[ename]
                for o in order[ename]:
                    for s, v in sorted((p.tok for p in o.deps), key=lambda t: -t[1]):
                        if seen.get(s, 0) < v:
                            e.wait_ge(self.sem[s], v)
                            seen[s] = v
                    ins = o.fn(e)
                    if o.tok is not None:
                        ins.then_inc(self.sem[o.tok[0]], 16 if o.dma is not None else 1)
                for s, v in bar:
                    if seen.get(s, 0) < v:
                        e.wait_ge(self.sem[s], v)
                        seen[s] = v
            return f

        with self.nc.Block() as block:
            block.tensor(run("pe"))
            block.scalar(run("act"))
            block.vector(run("dve"))
            block.gpsimd(run("pool"))
            block.sync(run("sp"))
        self.ops = []


def _consts(sb, rows):
    s = np.arange(128)
    sub = np.where(s < rows, s // sb, -1)
    same = (sub[:, None] == sub[None, :]) & (sub[:, None] >= 0)
    tri = (same & (s[:, None] <= s[None, :])).astype(np.float32)
    mid = np.where(sub >= 0, sub * sb + sb // 2 - 1, 0)
    tdm = tri - tri[:, mid] * (sub[None, :] >= 0)
    ind = np.zeros((128, 4), np.float32)
    for c in range(4):
        ind[sub == c, c] = 1.0
    trev = (same & (s[:, None] > s[None, :])).astype(np.float32)
    colmask = np.zeros((128, 4, 128), np.float32)
    for c in range(4):
        colmask[:, c, sub == c] = 1.0
    cg = np.concatenate([tri, tdm, ind], axis=1)
    return cg.astype(np.float32), trev, tri, colmask.reshape(128, 512), ind


def build(T, PAST):
    NT = T // 128
    NO = NT // 4
    NG = NO // 4
    NO1 = NO + 1
    NTOK = NO1 * 128
    NKS = PAST // 128
    nc = bass.Bass("TRN2", target_bir_lowering=False)
    stack = ExitStack()
    P = Prog(nc, stack)

    def din(name, shape, dt=F32):
        return nc.dram_tensor(name, list(shape), dt, kind="ExternalInput").ap()

    def dout(name, shape):
        return nc.dram_tensor(name, list(shape), F32, kind="ExternalOutput").ap()

    def dscr(name, shape, dt):
        return nc.dram_tensor(name, list(shape), dt, kind="Internal").ap()

    x_seq = din("x_seq", [T, D])
    x_own = din("x_own", [NTOK, D])
    rope_seq = din("rope_seq", [T, 64])
    rope_own = din("rope_own", [NTOK, 64])
    ropeT_own = din("ropeT_own", [2, 96, NTOK])
    c5T = din("c5T", [128, KC, 5])
    oh5 = din("oh5", [5, 256])
    w_ada = din("w_ada", [D, 6 * D])
    b_ada = din("b_ada", [1, 6 * D])
    w_in = din("w_in", [D, 4768])
    w_uq = din("w_uq", [QL, 768])
    w_uqr = din("w_uqr", [QL, 256])
    w_ukv = din("w_ukv", [KVL, 1024])
    w_pa = din("w_pa", [512, D])
    w_pb = din("w_pb", [512, D])
    w_out = din("w_out", [D, D])
    w_gu = din("w_gu", [D, 2 * DFF])
    w_down = din("w_down", [DFF, D])
    nrm = din("nrm", [3, D])
    q_norm = din("q_norm", [1, QL])
    kv_norm = din("kv_norm", [1, KVL])
    hg_norm = din("hg_norm", [1, 128])
    lb_param = din("lb_param", [2, HW])
    cst = din("cst", [128, 2, 1284])
    sel = din("sel", [128, 4])
    mdiag = din("mdiag", [128, 4, 128])
    ckvT_c = din("ckvT_c", [4, 128, 2, PAST])
    krT_c = din("krT_c", [4, 32, PAST])
    state_s = din("state_s", [128, 4, 4, 128])

    y_o = dout("y_o", [NTOK, D])
    ckv_o = dout("ckv_o", [NTOK, KVL])
    kr_o = dout("kr_o", [NTOK, RD])
    hst_p = dout("hst_p", [128, 4, 128])
    hst_s = dout("hst_s", [128, 4, 4, 128])

    modt = dscr("modt", [12, 128, D], F32)
    ckT_scr = dscr("ckT_scr", [128, 2, T], BF16)
    krT_scr = dscr("krT_scr", [32, T], BF16)
    ckTs_scr = dscr("ckTs_scr", [128, 2, 128], BF16)
    krTs_scr = dscr("krTs_scr", [32, 128], BF16)
    qT_scr = dscr("qT_scr", [96, NH, NTOK], BF16)
    aT_scr = dscr("aT_scr", [64, NH, NTOK], BF16)
    oT_scr = dscr("oT_scr", [128, 4, NTOK], BF16)
    x1_scr = dscr("x1_scr", [NTOK, D], F32)

    uid = {"i": 0}

    def sb(st, name, shape, dt=F32):
        uid["i"] += 1
        return st.enter_context(nc.sbuf_tensor(f"{name}_u{uid['i']}", list(shape), dt))

    def psb(st, name):
        return st.enter_context(nc.psum_tensor(name, [128, 512], F32))

    rr = {"i": 0}

    def cast_eng():
        rr["i"] += 1
        return ("pool", "dve", "act")[rr["i"] % 3]

    def fsz(ap):
        n = 1
        for s in ap.shape[1:]:
            n *= int(s)
        return n

    def ecost(eng, ap):
        n = fsz(ap)
        if eng == "act":
            return 220.0 + n * 1.0
        if eng == "pool":
            return 200.0 + n * 2.0
        return 120.0 + n * 1.05

    def copy(eng, out, in_, r=None, w=None):
        r = [in_] if r is None else r
        w = [out] if w is None else w
        if eng == "act":
            return P.op("act", lambda e, o=out, i=in_: e.copy(out=o, in_=i), r=r, w=w, cost=ecost("act", out))
        return P.op(eng, lambda e, o=out, i=in_: e.tensor_copy(out=o, in_=i), r=r, w=w, cost=ecost(eng, out))

    def dma(eng, out, in_, sem, r=None, w=None):
        r = [in_] if r is None else r
        w = [out] if w is None else w
        nb = fsz(out) * int(out.shape[0]) * (2 if out.dtype == BF16 else 4)
        return P.op(eng, lambda e, o=out, i=in_: e.dma_start(out=o, in_=i), r=r, w=w, dma=sem, cost=2000.0 + nb / 100.0)

    def mm(out, lhsT, rhs, start=True, stop=True, r=None, w=None, tr=False):
        r = [lhsT, rhs] if r is None else r
        w = [out] if w is None else w
        c = 30.0 + fsz(rhs) / 2.0 * (4.0 if rhs.dtype == F32 else 1.0)
        if tr:
            return P.op("pe", lambda e: e.matmul(out, lhsT, rhs, start=True, stop=True, is_transpose=True), r=r, w=w, cost=c)
        return P.op("pe", lambda e: e.matmul(out, lhsT, rhs, start=start, stop=stop), r=r, w=w, cost=c)

    def act(out, in_, func, bias=None, scale=None, accum=None, r=None, w=None):
        r = [in_] + ([bias] if (bias is not None and not isinstance(bias, float)) else []) if r is None else r
        w = ([out] + ([accum] if accum is not None else [])) if w is None else w
        kw = {}
        if bias is not None:
            kw["bias"] = bias
        if scale is not None:
            kw["scale"] = scale
        if accum is not None:
            kw["accum_out"] = accum
        return P.op("act", lambda e: e.activation(out=out, in_=in_, func=func, **kw), r=r, w=w, cost=ecost("act", out))

    def tt(out, in0, in1, op, eng="dve", r=None, w=None):
        r = [in0, in1] if r is None else r
        w = [out] if w is None else w
        return P.op(eng, lambda e: e.tensor_tensor(out=out, in0=in0, in1=in1, op=op), r=r, w=w, cost=ecost(eng, out))

    def ts(out, in0, s1, op0, s2=None, op1=None, eng="dve", r=None, w=None):
        r = [in0] + [s for s in (s1, s2) if s is not None and not isinstance(s, float)] if r is None else r
        w = [out] if w is None else w
        if op1 is None:
            return P.op(eng, lambda e: e.tensor_scalar(out=out, in0=in0, scalar1=s1, scalar2=None, op0=op0), r=r, w=w, cost=ecost(eng, out))
        return P.op(eng, lambda e: e.tensor_scalar(out=out, in0=in0, scalar1=s1, scalar2=s2, op0=op0, op1=op1), r=r, w=w, cost=ecost(eng, out))

    def stt(out, in0, scalar, in1, op0, op1, eng="dve", r=None, w=None):
        r = [in0, in1] + ([scalar] if not isinstance(scalar, float) else []) if r is None else r
        w = [out] if w is None else w
        return P.op(eng, lambda e: e.scalar_tensor_tensor(out=out, in0=in0, scalar=scalar, in1=in1, op0=op0, op1=op1), r=r, w=w, cost=ecost(eng, out))

    def memset(eng, ap, val, w=None):
        return P.op(eng, lambda e: e.memset(ap, val), r=[], w=[ap] if w is None else w, cost=ecost(eng, ap) * 0.5)

    def recip(out, in_):
        return P.op("dve", lambda e: e.reciprocal(out=out, in_=in_), r=[in_], w=[out], cost=ecost("dve", out))

    def rstd_act(dst, tmp, ssq, n):
        act(tmp, ssq, AF.Ln, bias=EPS, scale=1.0 / n)
        act(dst, tmp, AF.Exp, scale=-0.5)

    def bc(ap, shape):
        return ap.to_broadcast(list(shape))

    cb = sb(stack, "cb", [128, 2, 1284], BF16)
    stg = [sb(stack, f"stg{i}", [128, 1024]) for i in range(2)]
    ps = [psb(stack, f"ps{i}") for i in range(8)]

    def psbf(i):
        return ps[i][:].bitcast(BF16)

    C_CG, C_TREV, C_TRI, C_TREVT, C_ID, C_COL = 0, 260, 388, 516, 644, 772

    def load_w(dst, src, ncols, kcn, c0=0, kc0=0, extra=()):
        step = 1024
        bufs = list(stg) + list(extra)
        for kc in range(kcn):
            for cc in range(0, ncols, step):
                n = min(step, ncols - cc)
                bi = rr["i"] % len(bufs)
                s = bufs[bi]
                rr["i"] += 1
                dma("sp" if bi < 2 else "pool", s[:, 0:n], src[(kc0 + kc) * 128:(kc0 + kc + 1) * 128, c0 + cc:c0 + cc + n], "stg" + s.name)
                copy(("dve", "act")[rr["i"] % 2], dst[:, kc, cc:cc + n], s[:, 0:n])

    def ident_b(set_=0):
        return cb[:, set_, C_ID:C_ID + 128]

    with ExitStack() as st:
        sc5 = sb(st, "sc5", [128, KC, 5])
        mod5 = sb(st, "mod5", [5, 6 * D])
        bad = sb(st, "bad", [5, 6 * D])
        ohs = sb(st, "ohs", [5, 256])
        nrb = sb(st, "nrb", [128, 2, D])
        wst = [sb(st, f"wst{i}", [128, KC, 512]) for i in range(4)]
        mt = [sb(st, f"mt{i}", [128, D]) for i in range(2)]
        cf0 = sb(st, "cf0", [128, 2, 1284])
        dma("sp", cf0[:], cst, "cf0")
        copy("dve", cb[:], cf0[:])
        dma("sp", sc5[:], c5T, "sc5")
        act(sc5[:], sc5[:], AF.Silu)
        dma("sp", bad[:], b_ada.partition_broadcast(5), "bad")
        dma("sp", ohs[:], oh5, "ohs")
        dma("sp", nrb[:, 0, :], nrm[0:1, :].partition_broadcast(128), "nrb0")
        dma("sp", nrb[:, 1, :], nrm[1:2, :].partition_broadcast(128), "nrb1")
        for j in range(12):
            wt = wst[j % 4]
            dma("sp" if j % 2 == 0 else "pool", wt[:], w_ada[:, j * 512:(j + 1) * 512].rearrange("(kc p) n -> p kc n", p=128), "wst" + wt.name)
            for kc in range(KC):
                mm(ps[j % 2][0:5, :], sc5[:, kc, :], wt[:, kc, :], start=(kc == 0), stop=(kc == KC - 1))
            tt(mod5[:, j * 512:(j + 1) * 512], ps[j % 2][0:5, :], bad[:, j * 512:(j + 1) * 512], ALU.add)
        k = 0
        for grp in range(2):
            oh = ohs[:, grp * 128:(grp + 1) * 128]
            for blk in range(2):
                for part, kind in ((1, "W"), (0, "B"), (2, "G")):
                    col = (blk * 3 + part) * D
                    m = mt[k % 2]
                    for hf in range(2):
                        mm(ps[2 + hf][:], oh, mod5[:, col + hf * 512:col + (hf + 1) * 512])
                    for hf in range(2):
                        dst = m[:, hf * 512:(hf + 1) * 512]
                        if kind == "W":
                            stt(dst, ps[2 + hf][:], 1.0, nrb[:, blk, hf * 512:(hf + 1) * 512], ALU.add, ALU.mult)
                        else:
                            copy("act", dst, ps[2 + hf][:])
                    idx = grp * 6 + blk * 3 + {"W": 0, "B": 1, "G": 2}[kind]
                    dma("sp", modt[idx], m[:], "mt" + m.name)
                    k += 1
        P.barrier()
        P.emit()

    with ExitStack() as st:
        cf = sb(st, "cf", [128, 2, 772])
        dma("sp", cf[:], cst[:, :, 0:772], "cf")
        wA = sb(st, "wA", [128, KC, NA], BF16)
        wq = sb(st, "wq", [128, 3, 768], BF16)
        wqr = sb(st, "wqr", [128, 3, NH, 96], BF16)
        W1 = [sb(st, "W1_0", [128, D])]
        B1 = [sb(st, "B1_0", [128, D])]
        qnb = sb(st, "qnb", [128, QL])
        kvnb = sb(st, "kvnb", [128, KVL])
        hgnb = sb(st, "hgnb", [128, 128])
        lbt = sb(st, "lbt", [128, HW])
        omlt = sb(st, "omlt", [128, HW])
        lbtmp = sb(st, "lbtmp", [128, HW])
        selt = sb(st, "selt", [128, 4])
        S = sb(st, "S", [128, HW])
        Sown = sb(st, "Sown", [128, HW])
        S0s = sb(st, "S0s", [128, 4, HW])
        Snew = sb(st, "Snew", [128, 4, HW])
        xt = [sb(st, f"xt{i}", [128, D]) for i in range(2)]
        rp = [sb(st, f"rp{i}", [128, 64]) for i in range(4)]
        junk = sb(st, "junk", [128, D], BF16)
        junk2 = sb(st, "junk2", [128, QL], BF16)
        tmpf = sb(st, "tmpf", [128, D])
        hb = sb(st, "hb", [128, D], BF16)
        hbo = sb(st, "hbo", [128, D], BF16)
        xto = sb(st, "xto", [128, D])
        rpo = sb(st, "rpo", [128, 64])
        rpTo = sb(st, "rpTo", [96, 2, 128])
        hT = sb(st, "hT", [128, KC, 128], BF16)
        st1 = sb(st, "st1", [128, 8])
        sth = sb(st, "sth", [128, 8])
        zkvo = sb(st, "zkvo", [128, 288])
        stA = sb(st, "stA", [128, 4])
        zkvs2 = [sb(st, f"zkvs{i}", [128, 288]) for i in range(2)]
        ckv = sb(st, "ckv", [128, KVL])
        ckvb = sb(st, "ckvb", [128, KVL], BF16)
        ckT = sb(st, "ckT", [128, 2, 128], BF16)
        kr = sb(st, "kr", [128, RD])
        krt = sb(st, "krt", [128, RD])
        krb = sb(st, "krb", [128, RD], BF16)
        krT = sb(st, "krT", [32, 128], BF16)
        sig2 = [sb(st, f"sig{i}", [128, HW]) for i in range(2)]
        tsg2 = [sb(st, f"tsg{i}", [128, HW]) for i in range(2)]
        gg2 = [sb(st, f"gg{i}", [128, HW]) for i in range(2)]
        kk2 = [sb(st, f"kk{i}", [128, HW]) for i in range(2)]
        sig, tsg, gg, kk = sig2[0], tsg2[0], gg2[0], kk2[0]
        qq = sb(st, "qq", [128, HW])
        sgt = sb(st, "sgt", [128, HW])
        Vb2 = [sb(st, f"Vb{i}", [128, HW], BF16) for i in range(2)]
        Vb = Vb2[0]
        erev = sb(st, "erev", [128, HW])
        Ke = sb(st, "Ke", [128, HW], BF16)
        Kem = sb(st, "Kem", [128, 4, HW], BF16)
        at4 = sb(st, "at4", [128, 8])
        cq = sb(st, "cq", [128, QL], BF16)
        cqT = sb(st, "cqT", [128, 3, 128], BF16)
        qTs = sb(st, "qTs", [96, NH, 128], BF16)
        qtmp = sb(st, "qtmp", [96, 4, 128])
        E12 = sb(st, "E12", [128, 4, 2, 128])
        E3 = sb(st, "E3", [128, 4, 128])
        aexp = sb(st, "aexp", [128, 16])
        Qo4 = sb(st, "Qo4", [128, 4, 128], BF16)
        Qom4 = sb(st, "Qom4", [128, 4, 4, 128], BF16)
        Qd4 = sb(st, "Qd4", [128, 4, 128], BF16)
        Kd4 = sb(st, "Kd4", [128, 4, 128], BF16)
        Scf4 = sb(st, "Scf4", [128, 4, 128])
        Scb4 = sb(st, "Scb4", [128, 4, 4, 128], BF16)
        attb4 = sb(st, "attb4", [128, 4, 128], BF16)
        osq = sb(st, "osq", [128, HW])
        og = sb(st, "og", [128, HW])
        ogb = sb(st, "ogb", [128, HW], BF16)
        oTs = sb(st, "oTs", [128, 4, 128], BF16)

        load_w(wA, w_in, NA, KC, extra=[tmpf])
        load_w(wq, w_uq, 768, 3, extra=[tmpf])
        memset("pool", wqr[:], 0.0)
        for kc in range(3):
            s = stg[rr["i"] % 2]
            rr["i"] += 1
            dma("sp", s[:, 0:256], w_uqr[kc * 128:(kc + 1) * 128, :], "stg" + s.name)
            copy("dve", wqr[:, kc, :, 64:96], s[:, 0:256].rearrange("p (h d) -> p h d", h=NH))
        dma("sp", W1[0][:], modt[0], "W10")
        dma("sp", B1[0][:], modt[1], "B10")
        dma("sp", qnb[:], q_norm.partition_broadcast(128), "qnb")
        dma("sp", kvnb[:], kv_norm.partition_broadcast(128), "kvnb")
        dma("sp", hgnb[:], hg_norm.partition_broadcast(128), "hgnb")
        dma("sp", lbt[:], lb_param[0:1, :].partition_broadcast(128), "lbt")
        dma("sp", lbtmp[:], lb_param[1:2, :].partition_broadcast(128), "lbtmp")
        dma("sp", selt[:], sel, "selt")
        dma("sp", S0s[:].rearrange("p b (h v) -> p b h v", h=4), state_s, "S0s")
        tt(lbt[:], lbt[:], lbtmp[:], ALU.subtract)
        act(lbt[:], lbt[:], AF.Sigmoid)
        ts(omlt[:], lbt[:], -1.0, ALU.mult, 1.0, ALU.add)
        memset("dve", S[:], 0.0)
        memset("dve", Sown[:], 0.0)

        def norm_only(xtile, g, hb_):
            act(junk[:], xtile[:], AF.Square, accum=stA[:, 0:1])
            rstd_act(stA[:, 2:3], stA[:, 1:2], stA[:, 0:1], D)
            stt(tmpf[:], xtile[:], stA[:, 2:3], W1[g][:], ALU.mult, ALU.mult)
            tt(hb_[:], tmpf[:], B1[g][:], ALU.add)

        def trans_only(hb_):
            pv = psbf(7)
            for kc in range(KC):
                mm(pv[:, kc * 128:(kc + 1) * 128], hb_[:, kc * 128:(kc + 1) * 128], ident_b(), tr=True)
            copy("act", hT[:].rearrange("p a b -> p (a b)"), pv[:, 0:1024])

        def zproj(groups):
            for kc in range(KC):
                for (pa, c0, n) in groups:
                    mm(pa, hT[:, kc, :], wA[:, kc, c0:c0 + n], start=(kc == 0), stop=(kc == KC - 1))

        def latents(zkv, rpt, out_row=None, scan_cols=None, smp=False):
            act(junk2[:, 0:KVL], zkv[:, 0:KVL], AF.Square, accum=st1[:, 3:4])
            rstd_act(st1[:, 5:6], st1[:, 4:5], st1[:, 3:4], KVL)
            stt(ckv[:], zkv[:, 0:KVL], st1[:, 5:6], kvnb[:], ALU.mult, ALU.mult)
            tt(krt[:, 0:16], zkv[:, KVL + 16:KVL + 32], rpt[:, 32:48], ALU.mult)
            tt(krt[:, 16:32], zkv[:, KVL:KVL + 16], rpt[:, 48:64], ALU.mult)
            tt(kr[:], zkv[:, KVL:KVL + 32], rpt[:, 0:32], ALU.mult)
            tt(kr[:], kr[:], krt[:], ALU.add)
            if out_row is not None:
                dma("pool", ckv_o[out_row:out_row + 128, :], ckv[:], "ckvo")
                dma("pool", kr_o[out_row:out_row + 128, :], kr[:], "kro")
            if scan_cols is not None or smp:
                copy("act", ckvb[:], ckv[:])
                copy("act", krb[:], kr[:])
                pv = psbf(6)
                for c in range(2):
                    mm(pv[:, c * 128:(c + 1) * 128], ckvb[:, c * 128:(c + 1) * 128], ident_b(), tr=True)
                mm(pv[0:32, 256:384], krb[:], ident_b(), tr=True)
                copy("dve", ckT[:].rearrange("p a b -> p (a b)"), pv[:, 0:256])
                copy("dve", krT[:], pv[0:32, 256:384])
                if smp:
                    dma("pool", ckTs_scr, ckT[:], "ckTo")
                    dma("pool", krTs_scr, krT[:], "krTo")
                else:
                    dma("pool", ckT_scr[:, :, scan_cols:scan_cols + 128], ckT[:], "ckTo")
                    dma("pool", krT_scr[:, scan_cols:scan_cols + 128], krT[:], "krTo")

        def gates_fk(zhf, zhi, ez_done=None, p=0):
            sig, tsg, gg, kk, Vb = sig2[p], tsg2[p], gg2[p], kk2[p], Vb2[p]
            if ez_done is None:
                act(sig[:], zhf, AF.Exp, scale=-1.0)
            act(sig[:], sig[:], AF.Ln, bias=1.0)
            act(sig[:], sig[:], AF.Exp, scale=-1.0)
            tt(tsg[:], sig[:], omlt[:], ALU.mult)
            tt(gg[:], tsg[:], lbt[:], ALU.add)
            act(gg[:], gg[:], AF.Ln)
            tt(kk[:], omlt[:], tsg[:], ALU.subtract)
            if ez_done is None:
                copy("act", Vb[:], zhi)

        def scan_A1(j):
            xs = xt[j % 2]
            rpt = rp[j % 4]
            dma("sp", xs[:], x_seq[j * 128:(j + 1) * 128, :], "xt" + xs.name)
            dma("sp", rpt[:], rope_seq[j * 128:(j + 1) * 128, :], "rp" + rpt.name)
            norm_only(xs, 0, hb)

        def scan_A2(j):
            trans_only(hb)
            zproj([(ps[0][:, 0:288], O_KV, 288), (ps[1][:], O_HF, 512), (ps[2][:], O_HI, 512)])

        def scan_E(j):
            act(sig2[j % 2][:], ps[1][:], AF.Exp, scale=-1.0)
            copy("act", Vb2[j % 2][:], ps[2][:])
            copy("act", zkvs2[j % 2][:], ps[0][:, 0:288])

        def scan_L(j):
            latents(zkvs2[j % 2], rp[j % 4], scan_cols=j * 128)

        def scan_G1(j):
            gates_fk(None, None, ez_done=True, p=j % 2)

        def scan_G2(j):
            gg, kk, Vb = gg2[j % 2], kk2[j % 2], Vb2[j % 2]
            mm(ps[3][:], cf[:, 0, C_TREVT:C_TREVT + 128], gg[:])
            act(erev[:], ps[3][:], AF.Exp)
            tt(Ke[:], kk[:], erev[:], ALU.mult)
            for h in range(4):
                mm(ps[5][:, h * 128:(h + 1) * 128], Ke[:, h * 128:(h + 1) * 128], Vb[:, h * 128:(h + 1) * 128])
            stt(Sown[:], S[:], selt[:, j % 4:j % 4 + 1], Sown[:], ALU.mult, ALU.add)
            scan_tile_state(j)

        def scan_tile_state(j):
            gg = gg2[j % 2]
            for h in range(4):
                mm(ps[4][:, 4 + h:5 + h], gg[:, h * 128:(h + 1) * 128], onesc)
            act(at4[:, 0:4], ps[4][:, 4:8], AF.Exp)
            for h in range(4):
                stt(S[:, h * 128:(h + 1) * 128], S[:, h * 128:(h + 1) * 128], at4[:, h:h + 1],
                    ps[5][:, h * 128:(h + 1) * 128], ALU.mult, ALU.add)

        ones_t = sb(st, "ones_t", [128, 2])
        memset("dve", ones_t[:], 1.0)
        onesc = ones_t[:, 0:1]

        def own_pre(m, smp):
            g = 1 if smp else 0
            row = m * 128
            dma("sp", xto[:], x_own[row:row + 128, :], "xto")
            dma("sp", rpo[:], rope_own[row:row + 128, :], "rpo")
            dma("sp", rpTo[:], ropeT_own[:, :, row:row + 128].rearrange("a p t -> p a t"), "rpTo")
            norm_only(xto, g, hbo)

        def own_tile(m, smp):
            g = 1 if smp else 0
            cs = 1 if smp else 0
            row = m * 128
            rpt = rpo
            rT = rpTo
            trans_only(hbo)
            zproj([(ps[6][:, 0:QL], O_Q, QL), (ps[7][:, 0:288], O_KV, 288),
                   (ps[0][:], O_HQ, 512), (ps[1][:], O_HF, 512), (ps[2][:], O_HI, 512), (ps[3][:], O_HG, 512)])

            def chain_q():
                act(junk2[:, 0:QL], ps[6][:, 0:QL], AF.Square, accum=st1[:, 6:7])
                copy("act", zkvo[:], ps[7][:, 0:288])
                rstd_act(st1[:, 2:3], st1[:, 7:8], st1[:, 6:7], QL)
                stt(cq[:], ps[6][:, 0:QL], st1[:, 2:3], qnb[:], ALU.mult, ALU.mult)
                pv = psbf(6)
                for c in range(3):
                    mm(pv[:, c * 128:(c + 1) * 128], cq[:, c * 128:(c + 1) * 128], ident_b(), tr=True)
                copy("act", cqT[:].rearrange("p a b -> p (a b)"), pv[:, 0:384])
                for hh in range(2):
                    for h in range(hh * 4, hh * 4 + 4):
                        pa = ps[6][0:96, (h % 4) * 128:(h % 4 + 1) * 128]
                        for c in range(3):
                            mm(pa, wq[:, c, h * 96:(h + 1) * 96], cqT[:, c, :], start=(c == 0), stop=(c == 2))
                    for h in range(hh * 4, hh * 4 + 4):
                        pb = ps[7][0:96, (h % 4) * 128:(h % 4 + 1) * 128]
                        for c in range(3):
                            mm(pb, wqr[:, c, h, :], cqT[:, c, :], start=(c == 0), stop=(c == 2))
                    pa3 = ps[6][0:96, :].rearrange("p (h t) -> p h t", h=4)
                    pb3 = ps[7][0:96, :].rearrange("p (h t) -> p h t", h=4)
                    qs = qTs[:, hh * 4:(hh + 1) * 4, :]
                    qt_ = qtmp[:, 0:4, :]
                    copy("act", qs[0:64], pa3[0:64])
                    tt(qt_[64:96], pa3[64:96], bc(rT[64:96, 0:1, :], [32, 4, 128]), ALU.mult)
                    tt(qs[64:96], pb3[64:96], bc(rT[64:96, 1:2, :], [32, 4, 128]), ALU.mult)
                    tt(qs[64:96], qs[64:96], qt_[64:96], ALU.add)
                dma("pool", qT_scr[:, :, row:row + 128], qTs[:], "qTo")
                latents(zkvo, rpt, out_row=row, smp=smp)

            def chain_h():
                hgrn_own(cs, smp, row)

            P.interleave(chain_h, chain_q)

        def hgrn_own(cs, smp, row):
            st1 = sth
            gates_fk(ps[1][:], ps[2][:])
            act(qq[:], ps[0][:], AF.Silu)
            act(sgt[:], ps[3][:], AF.Silu)
            tt(sgt[:].rearrange("p (h v) -> p h v", h=4), sgt[:].rearrange("p (h v) -> p h v", h=4),
               bc(hgnb[:].unsqueeze(1), [128, 4, 128]), ALU.mult)
            mm(ps[0][:], cf[:, cs, C_TREV:C_TREV + 128], gg[:])
            act(erev[:], ps[0][:], AF.Exp)
            tt(Ke[:], kk[:], erev[:], ALU.mult)
            tt(Kem[:], bc(Ke[:].unsqueeze(1), [128, 4, HW]),
               bc(cb[:, cs, C_CG + 256:C_CG + 260].unsqueeze(2), [128, 4, HW]), ALU.mult, eng="pool")
            CGb = [ps[1], ps[2]]
            QKb = [ps[3], ps[4]]
            idf = cf[:, 0, C_ID:C_ID + 128]
            for h in range(4):
                hs = slice(h * 128, (h + 1) * 128)
                o2 = (h % 2) * 256
                mm(CGb[h // 2][:, o2:o2 + 256], gg[:, hs], cf[:, cs, C_CG:C_CG + 256])
            for h in range(4):
                hs = slice(h * 128, (h + 1) * 128)
                mm(ps[5][:, h * 4:(h + 1) * 4], gg[:, hs], cf[:, cs, C_CG + 256:C_CG + 260])
            for h in range(4):
                hs = slice(h * 128, (h + 1) * 128)
                o2 = (h % 2) * 256
                mm(QKb[h // 2][:, o2:o2 + 128], qq[:, hs], idf)
                mm(QKb[h // 2][:, o2 + 128:o2 + 256], kk[:, hs], idf)
            for b2 in range(2):
                act(E12[:, 2 * b2:2 * b2 + 2, :, :].rearrange("p h x t -> p (h x t)"), CGb[b2][:], AF.Exp)
                act(E3[:, 2 * b2:2 * b2 + 2, :], CGb[b2][:].rearrange("p (h x t) -> p h x t", h=2, x=2)[:, :, 1, :], AF.Exp, scale=-1.0)
            act(aexp[:], ps[5][:, 0:16], AF.Exp)
            for b2 in range(2):
                qk4 = QKb[b2][:].rearrange("p (h x t) -> p h x t", h=2, x=2)
                h2 = slice(2 * b2, 2 * b2 + 2)
                stt(Qo4[:, h2, :], qk4[:, :, 0, :], HG_SCALE, E12[:, h2, 0, :], ALU.mult, ALU.mult)
                stt(Qd4[:, h2, :], qk4[:, :, 0, :], HG_SCALE, E12[:, h2, 1, :], ALU.mult, ALU.mult)
                tt(Kd4[:, h2, :], qk4[:, :, 1, :], E3[:, h2, :], ALU.mult)
            cm4 = cb[:, cs, C_COL:C_COL + 512].rearrange("p (c t) -> p c t", c=4)
            for h in range(4):
                tt(Qom4[:, h, :, :], bc(Qo4[:, h:h + 1, :], [128, 4, 128]), cm4, ALU.mult, eng="pool")
            DSb = [ps[1], ps[2], ps[3], ps[4]]
            for c in range(4):
                for h in range(4):
                    hs = slice(h * 128, (h + 1) * 128)
                    mm(DSb[c][:, hs], Kem[:, c, hs], Vb[:, hs])
            a4 = aexp[:].rearrange("p (h c) -> p h c", h=4)
            if smp:
                copy("act", Scb4[:].rearrange("p h c v -> p c h v"), S0s[:].rearrange("p c (h v) -> p c h v", h=4))
                for c in range(4):
                    tt(Scf4[:], S0s[:, c, :].rearrange("p (h v) -> p h v", h=4), bc(a4[:, :, c:c + 1], [128, 4, 128]), ALU.mult)
                    tt(Snew[:, c, :], Scf4[:].rearrange("p h v -> p (h v)"), DSb[c][:], ALU.add)
            else:
                copy("act", Scb4[:, :, 0, :], Sown[:].rearrange("p (h v) -> p h v", h=4))
                for c in range(3):
                    src_ = Sown[:].rearrange("p (h v) -> p h v", h=4) if c == 0 else Scf4[:]
                    tt(Scf4[:], src_, bc(a4[:, :, c:c + 1], [128, 4, 128]), ALU.mult)
                    tt(Scf4[:], Scf4[:], DSb[c][:].rearrange("p (h v) -> p h v", h=4), ALU.add)
                    copy("act", Scb4[:, :, c + 1, :], Scf4[:])
            for h in range(4):
                mm(ps[0][:, h * 128:(h + 1) * 128], Kd4[:, h, :], Qd4[:, h, :])
            tt(attb4[:], ps[0][:].rearrange("p (h t) -> p h t", h=4), bc(cf[:, cs, C_TRI:C_TRI + 128].unsqueeze(1), [128, 4, 128]), ALU.mult)
            for h in range(4):
                hs = slice(h * 128, (h + 1) * 128)
                po = ps[5][:, hs]
                for c in range(4):
                    mm(po, Qom4[:, h, c, :], Scb4[:, h, c, :], start=(c == 0), stop=False)
                mm(po, attb4[:, h, :], Vb[:, hs], start=False, stop=True)
            act(osq[:], ps[5][:], AF.Square)
            P.op("dve", lambda e: e.tensor_reduce(out=st1[:, 0:4], in_=osq[:].rearrange("p (h v) -> p h v", h=4), axis=AX.X, op=ALU.add),
                 r=[osq], w=[st1])
            rstd_act(st1[:, 0:4], st1[:, 4:8], st1[:, 0:4], 128)
            tt(og[:].rearrange("p (h v) -> p h v", h=4), ps[5][:].rearrange("p (h v) -> p h v", h=4),
               bc(st1[:, 0:4].unsqueeze(2), [128, 4, 128]), ALU.mult)
            tt(ogb[:], og[:], sgt[:], ALU.mult)
            pv = psbf(0)
            for h in range(4):
                mm(pv[:, h * 128:(h + 1) * 128], ogb[:, h * 128:(h + 1) * 128], ident_b(), tr=True)
            copy("act", oTs[:].rearrange("p a b -> p (a b)"), pv[:, 0:512])
            dma("pool", oT_scr[:, :, row:row + 128], oTs[:], "oTo")
            if not smp:
                memset("dve", Sown[:], 0.0)

        for m in range(NO):
            scan_A1(4 * m)
            scan_A2(4 * m)
            scan_A1(4 * m + 1)
            for rr_ in range(4):
                j = 4 * m + rr_
                scan_E(j)
                if rr_ < 3:
                    scan_A2(j + 1)
                if rr_ < 2:
                    scan_A1(j + 2)
                if rr_ == 2:
                    own_pre(m, False)
                if rr_ > 0:
                    P.interleave(lambda j=j: scan_G1(j), lambda j=j: scan_G2(j - 1), lambda j=j: scan_L(j - 1))
                else:
                    scan_G1(j)
            P.interleave(lambda: scan_G2(4 * m + 3), lambda: scan_L(4 * m + 3))
            own_tile(m, False)
        dma("sp", hst_p.rearrange("p h v -> p (h v)"), S[:], "hstp")
        W1.append(xt[0])
        B1.append(xt[1])
        dma("sp", xt[0][:], modt[6], "xt" + xt[0].name)
        dma("sp", xt[1][:], modt[7], "xt" + xt[1].name)
        own_pre(NO, True)
        own_tile(NO, True)
        dma("sp", hst_s.rearrange("p b h v -> p b (h v)"), Snew[:], "hsts")
        P.barrier()
        P.emit()

    with ExitStack() as st:
        TK = max(T, PAST + 128)
        CK = sb(st, "CK", [128, 2, TK], BF16)
        KT = sb(st, "KT", [96, TK], BF16)
        Va = sb(st, "Va", [128, TK // 128 + 1, 128], BF16)
        wkv = sb(st, "wkv", [128, 2, 1024], BF16)
        QTh = [sb(st, f"QTh{i}", [96, NTOK], BF16) for i in range(2)]
        PT = [sb(st, f"PT{i}", [128, 512], BF16) for i in range(3)]
        mdg = sb(st, "mdg", [128, 4, 128])
        mdgb = sb(st, "mdgb", [128, 4, 128], BF16)
        rsum = sb(st, "rsum", [128, 512])
        aTn = [sb(st, f"aTn{i}", [64, 512], BF16) for i in range(2)]
        QTs = sb(st, "QTs", [96, NH, 128], BF16)
        load_w(wkv, w_ukv, 1024, 2)
        dma("sp", mdg[:], mdiag, "mdg")
        copy("dve", mdgb[:], mdg[:])
        memset("pool", Va[:, :, 64:128], 1.0)
        pti = {"i": 0}
        evi = {"i": 0}

        def expand_head(h, ntile, nk):
            for c0 in range(0, nk, 512):
                n = min(512, nk - c0)
                pk = ps[evi["i"] % 2]
                evi["i"] += 1
                for c in range(2):
                    mm(pk[0:64, 0:n], wkv[:, c, h * 128:h * 128 + 64], CK[:, c, c0:c0 + n], start=(c == 0), stop=(c == 1))
                copy("dve" if (c0 // 512) % 2 == 0 else "pool" if False else "dve", KT[0:64, c0:c0 + n], pk[0:64, 0:n],
                     w=[f"KT{c0 // 512}"])
            for t0 in range(0, ntile, 8):
                nt_ = min(8, ntile - t0)
                pvv = ps[2]
                for t in range(nt_):
                    kt = t0 + t
                    rows = min(128, nk - kt * 128)
                    for c in range(2):
                        mm(pvv[0:rows, t * 64:(t + 1) * 64], CK[:, c, kt * 128:kt * 128 + rows], wkv[:, c, h * 128 + 64:h * 128 + 128],
                           start=(c == 0), stop=(c == 1))
                rows_all = 128 if (t0 + nt_) * 128 <= nk else None
                if rows_all is not None:
                    copy("dve", Va[:, t0:t0 + nt_, 0:64], pvv[:, 0:nt_ * 64].rearrange("p (t v) -> p t v", v=64), w=[f"Va{t0 // 8}"])
                else:
                    for t in range(nt_):
                        kt = t0 + t
                        rows = min(128, nk - kt * 128)
                        copy("dve", Va[0:rows, kt, 0:64], pvv[0:rows, t * 64:(t + 1) * 64], w=[f"Va{t0 // 8}"])

        def attend(h, qt, qcols, ktiles, out_cols, nq, diag_base=None):
            LOOK = 2
            po = ps[3 + (pti["i"] % 2)]
            pti["i"] += 1
            n_k = len(ktiles)
            for it in range(n_k + LOOK):
                if it < n_k:
                    kt, rows, qoff = ktiles[it]
                    pS = ps[5 + it % 3]
                    mm(pS[0:rows, qoff:nq], KT[:, kt * 128:kt * 128 + rows], qt[:, qcols + qoff:qcols + nq],
                       r=[f"KT{kt // 4}", qt])
                idx = it - LOOK
                if idx < 0:
                    continue
                kt, rows, qoff = ktiles[idx]
                pS = ps[5 + idx % 3]
                pt = PT[idx % 3]
                act(pt[0:rows, qoff:nq], pS[0:rows, qoff:nq], AF.Exp, scale=MLA_SCALE)
                if diag_base is not None and kt >= diag_base:
                    loc = kt - diag_base
                    r_ = loc // 4
                    tt(pt[:, r_ * 128:(r_ + 1) * 128], pt[:, r_ * 128:(r_ + 1) * 128], mdgb[:, loc % 4, :], ALU.mult, eng="pool")
                mm(po[:, qoff:nq], Va[0:rows, kt, :], pt[0:rows, qoff:nq], start=(idx == 0), stop=(idx == n_k - 1),
                   r=[f"Va{kt // 8}", pt])
            return po

        def finish(po, h, out_cols, nq):
            an = aTn[h % 2]
            copy("dve", rsum[64:128, 0:nq], po[64:128, 0:nq])
            recip(rsum[64:128, 0:nq], rsum[64:128, 0:nq])
            tt(an[:, 0:nq], po[0:64, 0:nq], rsum[64:128, 0:nq], ALU.mult)
            dma("pool", aT_scr[:, h, out_cols:out_cols + nq], an[:, 0:nq], "aTo" + an.name)

        CH = min(T, 4096)
        for c in range(2):
            for t0 in range(0, T, CH):
                dma("sp" if c == 0 else "pool", CK[:, c, t0:t0 + CH], ckT_scr[:, c, t0:t0 + CH], f"CK{c}")
        for t0 in range(0, T, CH):
            dma("sp", KT[64:96, t0:t0 + CH], krT_scr[:, t0:t0 + CH], "KTr", w=[f"KT{g}" for g in range(TK // 512 + 2)])
        for h in range(NH):
            qt = QTh[h % 2]
            dma("sp", qt[:], qT_scr[:, h, :], "QTh" + qt.name)
            expand_head(h, NT, T)
            for g in range(NG):
                kts = [(kt, 128, 0) for kt in range(16 * g)]
                kts += [(16 * g + loc, 128, (loc // 4) * 128) for loc in range(16)]
                po = attend(h, qt, g * 512, kts, g * 512, 512, diag_base=16 * g)
                finish(po, h, g * 512, 512)
        nk_s = PAST + 16
        nts = NKS + 1
        allkt = [f"KT{g}" for g in range(TK // 512 + 2)]
        dma("sp", QTs[:], qT_scr[:, :, NO * 128:NO * 128 + 128], "QTs")
        zpad = sb(st, "zpad", [64, NH, 64], BF16)
        memset("pool", zpad[:], 0.0)
        dma("pool", aT_scr[:, :, NO * 128 + 64:NO * 128 + 128], zpad[:], "zpad")
        for b in range(4):
            si = 0
            for c in range(2):
                for p0 in range(0, PAST, 1024):
                    pn = min(1024, PAST - p0)
                    s_ = stg[si % 2]
                    si += 1
                    dma("sp", s_[:, 0:pn], ckvT_c[b, :, c, p0:p0 + pn], "stg" + s_.name)
                    copy("dve" if si % 2 == 0 else "pool", CK[:, c, p0:p0 + pn], s_[:, 0:pn])
            for p0 in range(0, PAST, 1024):
                pn = min(1024, PAST - p0)
                s_ = stg[si % 2]
                si += 1
                dma("sp", s_[0:32, 0:pn], krT_c[b, :, p0:p0 + pn], "stg" + s_.name)
                copy("act", KT[64:96, p0:p0 + pn], s_[0:32, 0:pn], w=allkt)
            dma("sp", CK[:, :, PAST:PAST + 16], ckTs_scr[:, :, b * 16:(b + 1) * 16], "CKn")
            dma("sp", KT[64:96, PAST:PAST + 16], krTs_scr[:, b * 16:(b + 1) * 16], "KTn", w=allkt)
            for h in range(NH):
                expand_head(h, nts, nk_s)
                kts = [(kt, min(128, nk_s - kt * 128), 0) for kt in range(nts)]
                po = attend(h, QTs[:, h, :], b * 16, kts, NO * 128 + b * 16, 16)
                finish(po, h, NO * 128 + b * 16, 16)
        P.barrier()
        P.emit()

    def make_norm(Wt, Bt, junk, tmpf, hb, st1, hT_dst_fn):
        def norm_T(xtile, g, k=0):
            act(junk[:], xtile[:], AF.Square, accum=st1[:, 0:1])
            act(st1[:, 1:2], st1[:, 0:1], AF.Sqrt, bias=EPS, scale=1.0 / D)
            recip(st1[:, 2:3], st1[:, 1:2])
            stt(tmpf[:], xtile[:], st1[:, 2:3], Wt[g][:], ALU.mult, ALU.mult)
            tt(hb[:], tmpf[:], Bt[g][:], ALU.add)
            pv = psbf(7)
            for kc in range(KC):
                mm(pv[:, kc * 128:(kc + 1) * 128], hb[:, kc * 128:(kc + 1) * 128], ident_b(), tr=True)
            copy("act", hT_dst_fn(k), pv[:, 0:1024].rearrange("p (a b) -> p a b", a=KC))
        return norm_T

    with ExitStack() as st:
        wG = sb(st, "wG", [128, KC, 2048], BF16)
        wpa = sb(st, "wpa", [128, 4, D], BF16)
        wpb = sb(st, "wpb", [128, 4, D], BF16)
        wo = sb(st, "wo", [128, KC, D], BF16)
        W1 = [sb(st, f"W1_{g}", [128, D]) for g in range(2)]
        B1 = [sb(st, f"B1_{g}", [128, D]) for g in range(2)]
        G1 = [sb(st, f"G1_{g}", [128, D]) for g in range(2)]
        xt = [sb(st, f"xt{i}", [128, D]) for i in range(3)]
        aTt = [sb(st, f"aTt{i}", [128, 4, 128], BF16) for i in range(3)]
        oTt = [sb(st, f"oTt{i}", [128, 4, 128], BF16) for i in range(3)]
        junk = sb(st, "junk", [128, D], BF16)
        tmpf = sb(st, "tmpf", [128, D])
        hbs = [sb(st, f"hb{i}", [128, D], BF16) for i in range(2)]
        hTs = [sb(st, f"hT{i}", [128, KC, 128], BF16) for i in range(2)]
        st3 = [sb(st, f"st3_{i}", [128, 4]) for i in range(2)]
        sgq = [sb(st, f"sgq{i}", [128, 512]) for i in range(2)]
        tq = [sb(st, f"tq{i}", [128, 512]) for i in range(2)]
        mixb = sb(st, "mixb", [128, D], BF16)
        mT = sb(st, "mT", [128, KC, 128], BF16)
        x1 = [sb(st, f"x1_{i}", [128, D]) for i in range(2)]
        load_w(wG, w_in, 2048, KC, c0=O_GA, extra=[tmpf])
        load_w(wpa, w_pa, D, 4, extra=[tmpf])
        load_w(wpb, w_pb, D, 4, extra=[tmpf])
        load_w(wo, w_out, D, KC, extra=[tmpf])
        for g in range(2):
            dma("sp", W1[g][:], modt[g * 6 + 0], f"W1{g}")
            dma("sp", B1[g][:], modt[g * 6 + 1], f"B1{g}")
            dma("sp", G1[g][:], modt[g * 6 + 2], f"G1{g}")
        aT_v = aT_scr.rearrange("d (c hh) t -> d hh c t", hh=2)

        def p3_load(m):
            row = m * 128
            p = m % 3
            xs, at_, ot_ = xt[p], aTt[p], oTt[p]
            dma("sp", xs[:], x_own[row:row + 128, :], "xt" + xs.name)
            for hh in range(2):
                dma("sp", at_[hh * 64:(hh + 1) * 64, :, :], aT_v[:, hh, :, row:row + 128], f"aTt{hh}" + at_.name)
            dma("sp", ot_[:], oT_scr[:, :, row:row + 128], "oTt" + ot_.name)

        def p3_load_norm(m):
            g = 1 if m == NO else 0
            p = m % 2
            xs, s3, hb_ = xt[m % 3], st3[p], hbs[p]
            act(junk[:], xs[:], AF.Square, accum=s3[:, 0:1])
            rstd_act(s3[:, 2:3], s3[:, 1:2], s3[:, 0:1], D)
            stt(tmpf[:], xs[:], s3[:, 2:3], W1[g][:], ALU.mult, ALU.mult)
            tt(hb_[:], tmpf[:], B1[g][:], ALU.add)

        def p3_trans(m):
            pv = psbf(4)
            hb_ = hbs[m % 2]
            for kc in range(KC):
                mm(pv[:, kc * 128:(kc + 1) * 128], hb_[:, kc * 128:(kc + 1) * 128], ident_b(), tr=True)

        def p3_tcopy(m):
            copy("act", hTs[m % 2][:], psbf(4)[:, 0:1024].rearrange("p (a b) -> p a b", a=KC))

        def p3_quarter_mm(m, q):
            p = m % 2
            hT_, at_, ot_ = hTs[p], aTt[m % 3], oTt[m % 3]
            A, B = ps[(q % 2) * 2], ps[(q % 2) * 2 + 1]
            c0 = q * 256
            for kc in range(KC):
                mm(A[:, 0:256], hT_[:, kc, :], wG[:, kc, c0:c0 + 256], start=(kc == 0), stop=(kc == KC - 1))
            for kc in range(KC):
                mm(A[:, 256:512], hT_[:, kc, :], wG[:, kc, D + c0:D + c0 + 256], start=(kc == 0), stop=(kc == KC - 1))
            for c in range(4):
                mm(B[:, 0:256], at_[:, c, :], wpa[:, c, c0:c0 + 256], start=(c == 0), stop=(c == 3))
            for c in range(4):
                mm(B[:, 256:512], ot_[:, c, :], wpb[:, c, c0:c0 + 256], start=(c == 0), stop=(c == 3))

        def p3_quarter_ew(m, q):
            A, B = ps[(q % 2) * 2], ps[(q % 2) * 2 + 1]
            sg_, t_ = sgq[q % 2], tq[q % 2]
            act(sg_[:], A[:], AF.Exp, scale=-1.0)
            act(sg_[:], sg_[:], AF.Ln, bias=1.0)
            act(sg_[:], sg_[:], AF.Exp, scale=-1.0)
            tt(t_[:], sg_[:], B[:], ALU.mult)
            tt(mixb[:, q * 256:(q + 1) * 256], t_[:, 0:256], t_[:, 256:512], ALU.add)

        def p3_out(m):
            g = 1 if m == NO else 0
            row = m * 128
            xs, xo = xt[m % 3], x1[m % 2]
            pv = psbf(5)
            for kc in range(KC):
                mm(pv[:, kc * 128:(kc + 1) * 128], mixb[:, kc * 128:(kc + 1) * 128], ident_b(), tr=True)
            copy("act", mT[:], pv[:, 0:1024].rearrange("p (a b) -> p a b", a=KC))
            for kc in range(KC):
                for hf in range(2):
                    mm(ps[6 + hf][:], mT[:, kc, :], wo[:, kc, hf * 512:(hf + 1) * 512], start=(kc == 0), stop=(kc == KC - 1))
            for hf in range(2):
                cs_ = slice(hf * 512, (hf + 1) * 512)
                tt(xo[:, cs_], ps[6 + hf][:], G1[g][:, cs_], ALU.mult)
                tt(xo[:, cs_], xo[:, cs_], xs[:, cs_], ALU.add)
            dma("pool", x1_scr[row:row + 128, :], xo[:], "x1o" + xo.name)

        p3_load(0)
        p3_load_norm(0)
        p3_trans(0)
        p3_tcopy(0)
        if NO1 > 1:
            p3_load(1)
        if NO1 > 2:
            p3_load(2)
        p3_quarter_mm(0, 0)
        p3_quarter_mm(0, 1)
        for m in range(NO1):
            if m + 1 < NO1:
                p3_load_norm(m + 1)
            p3_quarter_ew(m, 0)
            p3_quarter_mm(m, 2)
            p3_quarter_ew(m, 1)
            p3_quarter_mm(m, 3)
            if m + 1 < NO1:
                p3_trans(m + 1)
            p3_quarter_ew(m, 2)
            p3_quarter_ew(m, 3)
            if m + 1 < NO1:
                p3_tcopy(m + 1)
                p3_quarter_mm(m + 1, 0)
                p3_quarter_mm(m + 1, 1)
            p3_out(m)
            if m + 3 < NO1:
                p3_load(m + 3)
        P.barrier()
        P.emit()

    with ExitStack() as st:
        GT = 2
        wgu = sb(st, "wgu", [128, KC, 2 * DFF], BF16)
        wd = sb(st, "wd", [128, NFC, D], BF16)
        W2 = sb(st, "W2", [128, D])
        B2 = sb(st, "B2", [128, D])
        G2 = sb(st, "G2", [128, D])
        fnb = sb(st, "fnb", [128, D])
        xt = [[sb(st, f"xt{s}_{i}", [128, D]) for i in range(GT)] for s in range(2)]
        junk = sb(st, "junk", [128, D], BF16)
        tmpf = sb(st, "tmpf", [128, D])
        hb = sb(st, "hb", [128, D], BF16)
        hT4 = [sb(st, f"hT4_{s}", [128, KC, GT * 128], BF16) for s in range(2)]
        st4 = sb(st, "st4", [128, 4])
        st5 = sb(st, "st5", [128, 4])
        sgl = [sb(st, f"sgl{i}", [128, GT * 128]) for i in range(2)]
        actT = sb(st, "actT", [128, NFC, GT * 128], BF16)
        load_w(wgu, w_gu, 2 * DFF, KC, extra=[tmpf])
        load_w(wd, w_down, D, NFC, extra=[tmpf])
        dma("sp", fnb[:], nrm[2:3, :].partition_broadcast(128), "fnb")
        groups = [list(range(a, min(a + GT, NO))) for a in range(0, NO, GT)] + [[NO]]
        cur = {"g": -1}

        def p4_A(gi):
            grp = groups[gi]
            s = gi % 2
            g = 1 if grp[0] == NO else 0
            if g != cur["g"]:
                dma("sp", W2[:], modt[g * 6 + 3], "W2")
                dma("sp", B2[:], modt[g * 6 + 4], "B2")
                cur["g"] = g
            for k, m in enumerate(grp):
                xk = xt[s][k]
                dma("sp", xk[:], x1_scr[m * 128:(m + 1) * 128, :], "xt" + xk.name)
                act(junk[:], xk[:], AF.Square, accum=st4[:, 0:1])
                rstd_act(st4[:, 2:3], st4[:, 1:2], st4[:, 0:1], D)
                stt(tmpf[:], xk[:], st4[:, 2:3], W2[:], ALU.mult, ALU.mult)
                tt(hb[:], tmpf[:], B2[:], ALU.add)
                pv = psbf(7)
                for kc in range(KC):
                    mm(pv[:, kc * 128:(kc + 1) * 128], hb[:, kc * 128:(kc + 1) * 128], ident_b(), tr=True)
                copy("act", hT4[s][:, :, k * 128:(k + 1) * 128], pv[:, 0:1024].rearrange("p (a b) -> p a b", a=KC))

        def p4_B(gi):
            grp = groups[gi]
            s = gi % 2
            n = len(grp) * 128
            for c in range(NFC):
                pg, pu = ps[(c % 2) * 2], ps[(c % 2) * 2 + 1]
                for kc in range(KC):
                    mm(pg[:, 0:n], wgu[:, kc, c * 128:(c + 1) * 128], hT4[s][:, kc, 0:n], start=(kc == 0), stop=(kc == KC - 1))
                for kc in range(KC):
                    mm(pu[:, 0:n], wgu[:, kc, DFF + c * 128:DFF + (c + 1) * 128], hT4[s][:, kc, 0:n], start=(kc == 0), stop=(kc == KC - 1))
                sg_ = sgl[c % 2]
                act(sg_[:, 0:n], pg[:, 0:n], AF.Silu)
                tt(actT[:, c, 0:n], sg_[:, 0:n], pu[:, 0:n], ALU.mult)

        def p4_C(gi):
            grp = groups[gi]
            s = gi % 2
            g = 1 if grp[0] == NO else 0
            if g == 1 or gi == 0:
                dma("sp", G2[:], modt[g * 6 + 5], "G2")
            for k, m in enumerate(grp):
                for c in range(NFC):
                    for hf in range(2):
                        mm(ps[4 + hf][:], actT[:, c, k * 128:(k + 1) * 128], wd[:, c, hf * 512:(hf + 1) * 512], start=(c == 0), stop=(c == NFC - 1))
                xk = xt[s][k]
                for hf in range(2):
                    cs_ = slice(hf * 512, (hf + 1) * 512)
                    tt(tmpf[:, cs_], ps[4 + hf][:], G2[:, cs_], ALU.mult)
                    tt(xk[:, cs_], xk[:, cs_], tmpf[:, cs_], ALU.add)
                act(junk[:], xk[:], AF.Square, accum=st5[:, 0:1])
                rstd_act(st5[:, 2:3], st5[:, 1:2], st5[:, 0:1], D)
                stt(xk[:], xk[:], st5[:, 2:3], fnb[:], ALU.mult, ALU.mult)
                dma("pool", y_o[m * 128:(m + 1) * 128, :], xk[:], "yo" + xk.name)

        p4_A(0)
        for gi in range(len(groups)):
            p4_B(gi)
            if gi + 1 < len(groups):
                p4_A(gi + 1)
            p4_C(gi)
        P.barrier()
        P.emit()
    return nc, stack, P


def _rope_tabs(pos):
    half = 16
    inv = (np.float32(10000.0) ** (-np.arange(half, dtype=np.float32) / np.float32(half))).astype(np.float32)
    ang = pos.astype(np.float32)[:, None] * inv[None, :]
    return np.cos(ang).astype(np.float32), np.sin(ang).astype(np.float32)


def make_inputs(inp, T, PAST, n_cores=8):
    NT = T // 128
    NO = NT // 4
    NTOK = (NO + 1) * 128
    f = np.float32
    g = lambda k: np.asarray(inp[k], dtype=f)
    xp, xs_ = g("x_prompt"), g("x_sample")
    cst = np.zeros((128, 2, 1284), f)
    for si, (sbk, rows) in enumerate(((32, 128), (16, 64))):
        cg, trev, tri, colmask, ind = _consts(sbk, rows)
        cst[:, si, 0:260] = cg
        cst[:, si, 260:388] = trev
        cst[:, si, 388:516] = tri
        s = np.arange(128)
        cst[:, si, 516:644] = (s[:, None] > s[None, :]).astype(f)
        cst[:, si, 644:772] = np.eye(128, dtype=f)
        cst[:, si, 772:1284] = colmask
    w_uq = g("w_uq")[0]
    perm = []
    for h in range(NH):
        base = h * 96 + 64
        perm += [base + 16 + d for d in range(16)] + [base + d for d in range(16)]
    w_uqr = np.ascontiguousarray(w_uq[:, perm])
    oh5 = np.zeros((5, 256), f)
    oh5[0, 0:128] = 1.0
    for b in range(4):
        oh5[1 + b, 128 + 16 * b:128 + 16 * (b + 1)] = 1.0
    shared = {
        "w_ada": g("w_ada")[0], "b_ada": g("b_ada"), "w_in": g("w_in")[0], "w_uq": w_uq, "w_uqr": w_uqr,
        "w_ukv": g("w_ukv")[0], "w_pa": g("w_pa")[0], "w_pb": g("w_pb")[0], "w_out": g("w_out")[0],
        "w_gu": g("w_gu")[0], "w_down": g("w_down")[0],
        "nrm": np.stack([g("norm1")[0], g("norm2")[0], g("final_norm")]),
        "q_norm": g("q_norm"), "kv_norm": g("kv_norm"), "hg_norm": g("hg_norm"), "lb_param": g("lb_param"),
        "cst": cst, "oh5": oh5,
    }
    pos_seq = np.arange(T)
    cs_seq, sn_seq = _rope_tabs(pos_seq)
    rope_seq = np.concatenate([cs_seq, cs_seq, -sn_seq, sn_seq], axis=1)
    pos_s = np.zeros(128, np.int64)
    for b in range(4):
        pos_s[16 * b:16 * (b + 1)] = PAST + np.arange(16)
    maps = []
    for c in range(n_cores):
        b, i = c // 4, c % 4
        own = np.concatenate([np.arange((4 * m + i) * 128, (4 * m + i + 1) * 128) for m in range(NO)])
        xsm = np.zeros((128, D), f)
        xsm[0:64] = xs_[4 * c:4 * c + 4].reshape(64, D)
        pos_own = np.concatenate([own, pos_s])
        co, so = _rope_tabs(pos_own)
        rope_own = np.concatenate([co, co, -so, so], axis=1)
        ropeT = np.zeros((2, 96, NTOK), f)
        ropeT[0, 64:96] = np.concatenate([co, co], axis=1).T
        ropeT[1, 64:96] = np.concatenate([-so, so], axis=1).T
        c5 = np.concatenate([g("c_prompt")[b:b + 1], g("c_sample")[4 * c:4 * c + 4]], axis=0)
        c5T = np.ascontiguousarray(c5.T.reshape(KC, 128, 5).transpose(1, 0, 2))
        sel = np.zeros((128, 4), f)
        sel[:, i] = 1.0
        md = np.zeros((128, 4, 128), f)
        for r in range(4):
            if r < i:
                md[:, r, :] = 1.0
            elif r == i:
                kk_ = np.arange(128)[:, None] // 64
                qq_ = np.arange(128)[None, :] // 64
                md[:, r, :] = (kk_ <= qq_).astype(f)
        ck = g("cache_ckv")[0, 4 * c:4 * c + 4]
        ckvT = np.ascontiguousarray(ck.reshape(4, PAST, 2, 128).transpose(0, 3, 2, 1))
        krT = np.ascontiguousarray(g("cache_krope")[0, 4 * c:4 * c + 4].transpose(0, 2, 1))
        sts = np.ascontiguousarray(g("state_hgrn")[0, 4 * c:4 * c + 4].transpose(2, 0, 1, 3))
        m = dict(shared)
        m.update({
            "x_seq": np.ascontiguousarray(xp[b]), "x_own": np.concatenate([xp[b][own], xsm], axis=0),
            "rope_seq": rope_seq.astype(f), "rope_own": rope_own.astype(f), "ropeT_own": ropeT,
            "c5T": c5T, "sel": sel, "mdiag": md, "ckvT_c": ckvT, "krT_c": krT, "state_s": sts,
        })
        maps.append(m)
    return maps


def assemble(results, T, PAST, n_cores=8):
    NT = T // 128
    NO = NT // 4
    f = np.float32
    nb = n_cores // 4
    y_p = np.zeros((nb, T, D), f)
    ckv_p = np.zeros((1, nb, T, KVL), f)
    kr_p = np.zeros((1, nb, T, RD), f)
    hs_p = np.zeros((1, nb, 4, 128, 128), f)
    y_s = np.zeros((4 * n_cores, 16, D), f)
    ckv_s = np.zeros((1, 4 * n_cores, 16, KVL), f)
    kr_s = np.zeros((1, 4 * n_cores, 16, RD), f)
    hs_s = np.zeros((1, 4 * n_cores, 4, 128, 128), f)
    for c in range(n_cores):
        r = results[c]
        b, i = c // 4, c % 4
        own = np.concatenate([np.arange((4 * m + i) * 128, (4 * m + i + 1) * 128) for m in range(NO)])
        y_p[b, own] = r["y_o"][0:NO * 128]
        ckv_p[0, b, own] = r["ckv_o"][0:NO * 128]
        kr_p[0, b, own] = r["kr_o"][0:NO * 128]
        if i == 0:
            hs_p[0, b] = r["hst_p"].transpose(1, 0, 2)
        y_s[4 * c:4 * c + 4] = r["y_o"][NO * 128:NO * 128 + 64].reshape(4, 16, D)
        ckv_s[0, 4 * c:4 * c + 4] = r["ckv_o"][NO * 128:NO * 128 + 64].reshape(4, 16, KVL)
        kr_s[0, 4 * c:4 * c + 4] = r["kr_o"][NO * 128:NO * 128 + 64].reshape(4, 16, RD)
        hs_s[0, 4 * c:4 * c + 4] = r["hst_s"].transpose(1, 2, 0, 3)
    return (y_p, y_s, ckv_p, kr_p, hs_p, ckv_s, kr_s, hs_s)


def kernel(**inputs):
    T, PAST = 16384, 2048
    nc, stack, _ = build(T, PAST)
    maps = make_inputs(inputs, T, PAST)
    res = run_bass_kernel_spmd(nc, maps, core_ids=list(range(8)))
    stack.close()
    return assemble(res.results, T, PAST)
```

```python
import math
import os
from contextlib import ExitStack
import numpy as np
import concourse.bass as bass
import concourse.mybir as mybir
from concourse.bass_utils import run_bass_kernel_spmd

F32 = mybir.dt.float32
BF16 = mybir.dt.bfloat16
AF = mybir.ActivationFunctionType
ALU = mybir.AluOpType
AX = mybir.AxisListType

D = 1024
KC = 8
QL, KVL, RD = 384, 256, 32
NH, QKN, VH = 8, 64, 64
HGH, HGK = 4, 128
HW = 512
DFF = 2816
NFC = DFF // 128
EPS = 1e-6
MLA_SCALE = 1.0 / math.sqrt(96.0)
HG_SCALE = 1.0 / math.sqrt(128.0)
O_Q, O_KV, O_HQ, O_HF, O_HI, O_HG, O_GA = 0, 384, 672, 1184, 1696, 2208, 2720
NA = 2720


class _Op:
    __slots__ = ("eng", "fn", "dma", "deps", "odeps", "signal", "tok", "cost", "idx", "pos", "fin")

    def __init__(self, eng, fn, dma, cost):
        self.eng, self.fn, self.dma, self.cost = eng, fn, dma, cost
        self.deps, self.odeps, self.signal, self.tok = [], [], False, None
        self.idx = self.pos = 0
        self.fin = 0.0


SYNC_NS = 300.0


class Prog:
    def __init__(self, nc, stack):
        self.nc, self.stack = nc, stack
        self.engs = {"pe": nc.tensor, "act": nc.scalar, "dve": nc.vector, "pool": nc.gpsimd, "sp": nc.sync}
        self.sem, self.cnt = {}, {}
        self.seen = {e: {} for e in self.engs}
        self.last, self.readers = {}, {}
        self.ops = []
        self.nops = 0
        self.reorder = False
        self.cap = None
        self.reorder_engs = set(os.environ.get('REORDER_ENGS', 'pe,act,dve,pool,sp').split(','))

    def _sem(self, name):
        if name not in self.sem:
            self.sem[name] = self.stack.enter_context(self.nc.semaphore("s_" + name))
            self.cnt[name] = 0
        return self.sem[name]

    @staticmethod
    def _key(k):
        return k if isinstance(k, str) else k.name

    def op(self, eng, fn, r=(), w=(), dma=None, cost=300.0):
        if self.cap is not None:
            self.cap.append((eng, fn, list(r), list(w), dma, cost))
            return None
        o = _Op(eng, fn, dma, cost)
        deps = {}
        rk = [self._key(k) for k in r]
        wk = [self._key(k) for k in w]
        for k in rk:
            lw = self.last.get(k)
            if lw is not None:
                deps[id(lw)] = lw
        for k in wk:
            lw = self.last.get(k)
            if lw is not None:
                deps[id(lw)] = lw
            for rd in self.readers.get(k, ()):
                deps[id(rd)] = rd
        for p in deps.values():
            if p is o:
                continue
            o.odeps.append(p)
            if p.eng == eng and p.dma is None and dma is None and eng == "pe":
                continue
            o.deps.append(p)
        for k in rk:
            self.readers.setdefault(k, []).append(o)
        for k in wk:
            self.last[k] = o
            self.readers[k] = []
        o.idx = len(self.ops)
        self.ops.append(o)
        return o

    def interleave(self, *thunks):
        chains = []
        for t in thunks:
            self.cap = []
            t()
            chains.append(self.cap)
        self.cap = None
        idx = [0] * len(chains)
        while True:
            best, bk = None, None
            for k, ch in enumerate(chains):
                if idx[k] < len(ch):
                    fr = (idx[k] + 0.5) / len(ch)
                    if best is None or fr < best:
                        best, bk = fr, k
            if bk is None:
                break
            self.op(*chains[bk][idx[bk]])
            idx[bk] += 1

    def barrier(self):
        self._barrier = True
        self.last, self.readers = {}, {}

    def _schedule(self, ops):
        import heapq
        n = len(ops)
        if not self.reorder:
            for i, o in enumerate(ops):
                o.pos = i
            return {e: [o for o in ops if o.eng == e] for e in self.engs}
        ndep = [len(o.odeps) for o in ops]
        users = [[] for _ in range(n)]
        for o in ops:
            for p in o.odeps:
                users[p.idx].append(o)
        ready_t = [0.0] * n
        heaps = {e: [] for e in self.engs}
        for o in ops:
            if ndep[o.idx] == 0:
                heapq.heappush(heaps[o.eng], (0.0, o.idx))
        tE = {e: 0.0 for e in self.engs}
        order = {e: [] for e in self.engs}
        done = 0
        WIN = 600
        nxt = 0
        sched = [False] * n
        byeng = {e: [o for o in ops if o.eng == e] for e in self.engs}
        pq = {e: 0 for e in self.engs}
        while done < n:
            best = None
            for e in self.engs:
                h = heaps[e]
                if not h:
                    continue
                cand = None
                tmp = []
                if e not in self.reorder_engs:
                    while pq[e] < len(byeng[e]) and sched[byeng[e][pq[e]].idx]:
                        pq[e] += 1
                    if pq[e] < len(byeng[e]):
                        want = byeng[e][pq[e]].idx
                        found = None
                        for it in h:
                            if it[1] == want:
                                found = it
                                break
                        if found is not None:
                            h.remove(found)
                            heapq.heapify(h)
                            cand = found
                    h_iter = []
                else:
                    h_iter = h
                while h_iter:
                    rt, i = heapq.heappop(h)
                    if i <= nxt + WIN:
                        cand = (rt, i)
                        break
                    tmp.append((rt, i))
                for it in tmp:
                    heapq.heappush(h, it)
                if cand is None:
                    continue
                st = max(tE[e], cand[0])
                if best is None or (st, cand[1]) < (best[0], best[2]):
                    if best is not None:
                        heapq.heappush(heaps[best[1]], (best[3], best[2]))
                    best = (st, e, cand[1], cand[0])
                else:
                    heapq.heappush(h, cand)
            if best is None:
                WIN *= 2
                continue
            st, e, i, rt = best
            o = ops[i]
            issue = 60.0 if o.dma is not None else o.cost
            o.fin = st + (o.cost if o.dma is None else o.cost)
            tE[e] = st + issue
            o.pos = len(order[e])
            order[e].append(o)
            sched[i] = True
            done += 1
            while nxt < n and sched[nxt]:
                nxt += 1
            for u in users[i]:
                ndep[u.idx] -= 1
                lat = 0.0 if (u.eng == e and o.dma is None) else SYNC_NS
                ready_t[u.idx] = max(ready_t[u.idx], o.fin + lat)
                if ndep[u.idx] == 0:
                    heapq.heappush(heaps[u.eng], (ready_t[u.idx], u.idx))
        return order

    def emit(self):
        ops = self.ops
        self.nops += len(ops)
        order = self._schedule(ops)
        for o in ops:
            best = {}
            keep = []
            for p in o.deps:
                if p.dma is not None:
                    keep.append(p)
                else:
                    b = best.get(p.eng)
                    if b is None or p.pos > b.pos:
                        best[p.eng] = p
            o.deps = keep + list(best.values())
            for p in o.deps:
                p.signal = True
        lasts = []
        if getattr(self, "_barrier", False):
            for e in self.engs:
                if order[e]:
                    cl = [o for o in order[e] if o.dma is None]
                    if cl:
                        cl[-1].signal = True
                        lasts.append(cl[-1])
            dl = {}
            for o in ops:
                if o.dma is not None:
                    dl[o.dma] = o if (o.dma not in dl or True) else dl[o.dma]
            lasts_dma = set(o.dma for o in ops if o.dma is not None)
        for e in self.engs:
            for o in order[e]:
                if o.dma is not None:
                    self._sem(o.dma)
                    self.cnt[o.dma] += 16
                    o.tok = (o.dma, self.cnt[o.dma])
                elif o.signal:
                    self._sem(o.eng)
                    self.cnt[o.eng] += 1
                    o.tok = (o.eng, self.cnt[o.eng])
        bar = []
        if getattr(self, "_barrier", False):
            bar = [p.tok for p in lasts] + [(s, self.cnt[s]) for s in lasts_dma]
            self._barrier = False

        def run(ename):
            def f(e):
                seen = self.seen[ename]
                for o in order[ename]:
                    for s, v in sorted((p.tok for p in o.deps), key=lambda t: -t[1]):
                        if seen.get(s, 0) < v:
                            e.wait_ge(self.sem[s], v)
                            seen[s] = v
                    ins = o.fn(e)
                    if o.tok is not None:
                        ins.then_inc(self.sem[o.tok[0]], 16 if o.dma is not None else 1)
                for s, v in bar:
                    if seen.get(s, 0) < v:
                        e.wait_ge(self.sem[s], v)
                        seen[s] = v
            return f

        with self.nc.Block() as block:
            block.tensor(run("pe"))
            block.scalar(run("act"))
            block.vector(run("dve"))
            block.gpsimd(run("pool"))
            block.sync(run("sp"))
        self.ops = []


def _consts(sb, rows):
    s = np.arange(128)
    sub = np.where(s < rows, s // sb, -1)
    same = (sub[:, None] == sub[None, :]) & (sub[:, None] >= 0)
    tri = (same & (s[:, None] <= s[None, :])).astype(np.float32)
    mid = np.where(sub >= 0, sub * sb + sb // 2 - 1, 0)
    tdm = tri - tri[:, mid] * (sub[None, :] >= 0)
    ind = np.zeros((128, 4), np.float32)
    for c in range(4):
        ind[sub == c, c] = 1.0
    trev = (same & (s[:, None] > s[None, :])).astype(np.float32)
    colmask = np.zeros((128, 4, 128), np.float32)
    for c in range(4):
        colmask[:, c, sub == c] = 1.0
    cg = np.concatenate([tri, tdm, ind], axis=1)
    return cg.astype(np.float32), trev, tri, colmask.reshape(128, 512), ind


def build(T, PAST):
    NT = T // 128
    NO = NT // 4
    NG = NO // 4
    NO1 = NO + 1
    NTOK = NO1 * 128
    NKS = PAST // 128
    nc = bass.Bass("TRN2", target_bir_lowering=False)
    stack = ExitStack()
    P = Prog(nc, stack)

    def din(name, shape, dt=F32):
        return nc.dram_tensor(name, list(shape), dt, kind="ExternalInput").ap()

    def dout(name, shape):
        return nc.dram_tensor(name, list(shape), F32, kind="ExternalOutput").ap()

    def dscr(name, shape, dt):
        return nc.dram_tensor(name, list(shape), dt, kind="Internal").ap()

    x_seq = din("x_seq", [T, D])
    x_own = din("x_own", [NTOK, D])
    rope_seq = din("rope_seq", [T, 64])
    rope_own = din("rope_own", [NTOK, 64])
    ropeT_own = din("ropeT_own", [2, 96, NTOK])
    c5T = din("c5T", [128, KC, 5])
    oh5 = din("oh5", [5, 256])
    w_ada = din("w_ada", [D, 6 * D])
    b_ada = din("b_ada", [1, 6 * D])
    w_in = din("w_in", [D, 4768])
    w_uq = din("w_uq", [QL, 768])
    w_uqr = din("w_uqr", [QL, 256])
    w_ukv = din("w_ukv", [KVL, 1024])
    w_pa = din("w_pa", [512, D])
    w_pb = din("w_pb", [512, D])
    w_out = din("w_out", [D, D])
    w_gu = din("w_gu", [D, 2 * DFF])
    w_down = din("w_down", [DFF, D])
    nrm = din("nrm", [3, D])
    q_norm = din("q_norm", [1, QL])
    kv_norm = din("kv_norm", [1, KVL])
    hg_norm = din("hg_norm", [1, 128])
    lb_param = din("lb_param", [2, HW])
    cst = din("cst", [128, 2, 1284])
    sel = din("sel", [128, 4])
    mdiag = din("mdiag", [128, 4, 128])
    ckvT_c = din("ckvT_c", [4, 128, 2, PAST])
    krT_c = din("krT_c", [4, 32, PAST])
    state_s = din("state_s", [128, 4, 4, 128])

    y_o = dout("y_o", [NTOK, D])
    ckv_o = dout("ckv_o", [NTOK, KVL])
    kr_o = dout("kr_o", [NTOK, RD])
    hst_p = dout("hst_p", [128, 4, 128])
    hst_s = dout("hst_s", [128, 4, 4, 128])

    modt = dscr("modt", [12, 128, D], F32)
    ckT_scr = dscr("ckT_scr", [128, 2, T], BF16)
    krT_scr = dscr("krT_scr", [32, T], BF16)
    ckTs_scr = dscr("ckTs_scr", [128, 2, 128], BF16)
    krTs_scr = dscr("krTs_scr", [32, 128], BF16)
    qT_scr = dscr("qT_scr", [96, NH, NTOK], BF16)
    aT_scr = dscr("aT_scr", [64, NH, NTOK], BF16)
    oT_scr = dscr("oT_scr", [128, 4, NTOK], BF16)
    x1_scr = dscr("x1_scr", [NTOK, D], F32)

    uid = {"i": 0}

    def sb(st, name, shape, dt=F32):
        uid["i"] += 1
        return st.enter_context(nc.sbuf_tensor(f"{name}_u{uid['i']}", list(shape), dt))

    def psb(st, name):
        return st.enter_context(nc.psum_tensor(name, [128, 512], F32))

    rr = {"i": 0}

    def cast_eng():
        rr["i"] += 1
        return ("pool", "dve", "act")[rr["i"] % 3]

    def fsz(ap):
        n = 1
        for s in ap.shape[1:]:
            n *= int(s)
        return n

    def ecost(eng, ap):
        n = fsz(ap)
        if eng == "act":
            return 220.0 + n * 1.0
        if eng == "pool":
            return 200.0 + n * 2.0
        return 120.0 + n * 1.05

    def copy(eng, out, in_, r=None, w=None):
        r = [in_] if r is None else r
        w = [out] if w is None else w
        if eng == "act":
            return P.op("act", lambda e, o=out, i=in_: e.copy(out=o, in_=i), r=r, w=w, cost=ecost("act", out))
        return P.op(eng, lambda e, o=out, i=in_: e.tensor_copy(out=o, in_=i), r=r, w=w, cost=ecost(eng, out))

    def dma(eng, out, in_, sem, r=None, w=None):
        r = [in_] if r is None else r
        w = [out] if w is None else w
        nb = fsz(out) * int(out.shape[0]) * (2 if out.dtype == BF16 else 4)
        return P.op(eng, lambda e, o=out, i=in_: e.dma_start(out=o, in_=i), r=r, w=w, dma=sem, cost=2000.0 + nb / 100.0)

    def mm(out, lhsT, rhs, start=True, stop=True, r=None, w=None, tr=False):
        r = [lhsT, rhs] if r is None else r
        w = [out] if w is None else w
        c = 30.0 + fsz(rhs) / 2.0 * (4.0 if rhs.dtype == F32 else 1.0)
        if tr:
            return P.op("pe", lambda e: e.matmul(out, lhsT, rhs, start=True, stop=True, is_transpose=True), r=r, w=w, cost=c)
        return P.op("pe", lambda e: e.matmul(out, lhsT, rhs, start=start, stop=stop), r=r, w=w, cost=c)

    def act(out, in_, func, bias=None, scale=None, accum=None, r=None, w=None):
        r = [in_] + ([bias] if (bias is not None and not isinstance(bias, float)) else []) if r is None else r
        w = ([out] + ([accum] if accum is not None else [])) if w is None else w
        kw = {}
        if bias is not None:
            kw["bias"] = bias
        if scale is not None:
            kw["scale"] = scale
        if accum is not None:
            kw["accum_out"] = accum
        return P.op("act", lambda e: e.activation(out=out, in_=in_, func=func, **kw), r=r, w=w, cost=ecost("act", out))

    def tt(out, in0, in1, op, eng="dve", r=None, w=None):
        r = [in0, in1] if r is None else r
        w = [out] if w is None else w
        return P.op(eng, lambda e: e.tensor_tensor(out=out, in0=in0, in1=in1, op=op), r=r, w=w, cost=ecost(eng, out))

    def ts(out, in0, s1, op0, s2=None, op1=None, eng="dve", r=None, w=None):
        r = [in0] + [s for s in (s1, s2) if s is not None and not isinstance(s, float)] if r is None else r
        w = [out] if w is None else w
        if op1 is None:
            return P.op(eng, lambda e: e.tensor_scalar(out=out, in0=in0, scalar1=s1, scalar2=None, op0=op0), r=r, w=w, cost=ecost(eng, out))
        return P.op(eng, lambda e: e.tensor_scalar(out=out, in0=in0, scalar1=s1, scalar2=s2, op0=op0, op1=op1), r=r, w=w, cost=ecost(eng, out))

    def stt(out, in0, scalar, in1, op0, op1, eng="dve", r=None, w=None):
        r = [in0, in1] + ([scalar] if not isinstance(scalar, float) else []) if r is None else r
        w = [out] if w is None else w
        return P.op(eng, lambda e: e.scalar_tensor_tensor(out=out, in0=in0, scalar=scalar, in1=in1, op0=op0, op1=op1), r=r, w=w, cost=ecost(eng, out))

    def memset(eng, ap, val, w=None):
        return P.op(eng, lambda e: e.memset(ap, val), r=[], w=[ap] if w is None else w, cost=ecost(eng, ap) * 0.5)

    def recip(out, in_):
        return P.op("dve", lambda e: e.reciprocal(out=out, in_=in_), r=[in_], w=[out], cost=ecost("dve", out))

    def rstd_act(dst, tmp, ssq, n):
        act(tmp, ssq, AF.Ln, bias=EPS, scale=1.0 / n)
        act(dst, tmp, AF.Exp, scale=-0.5)

    def bc(ap, shape):
        return ap.to_broadcast(list(shape))

    cb = sb(stack, "cb", [128, 2, 1284], BF16)
    stg = [sb(stack, f"stg{i}", [128, 1024]) for i in range(2)]
    ps = [psb(stack, f"ps{i}") for i in range(8)]

    def psbf(i):
        return ps[i][:].bitcast(BF16)

    C_CG, C_TREV, C_TRI, C_TREVT, C_ID, C_COL = 0, 260, 388, 516, 644, 772

    def load_w(dst, src, ncols, kcn, c0=0, kc0=0, extra=()):
        step = 1024
        bufs = list(stg) + list(extra)
        for kc in range(kcn):
            for cc in range(0, ncols, step):
                n = min(step, ncols - cc)
                bi = rr["i"] % len(bufs)
                s = bufs[bi]
                rr["i"] += 1
                dma("sp" if bi < 2 else "pool", s[:, 0:n], src[(kc0 + kc) * 128:(kc0 + kc + 1) * 128, c0 + cc:c0 + cc + n], "stg" + s.name)
                copy(("dve", "act")[rr["i"] % 2], dst[:, kc, cc:cc + n], s[:, 0:n])

    def ident_b(set_=0):
        return cb[:, set_, C_ID:C_ID + 128]

    with ExitStack() as st:
        sc5 = sb(st, "sc5", [128, KC, 5])
        mod5 = sb(st, "mod5", [5, 6 * D])
        bad = sb(st, "bad", [5, 6 * D])
        ohs = sb(st, "ohs", [5, 256])
        nrb = sb(st, "nrb", [128, 2, D])
        wst = [sb(st, f"wst{i}", [128, KC, 512]) for i in range(4)]
        mt = [sb(st, f"mt{i}", [128, D]) for i in range(2)]
        cf0 = sb(st, "cf0", [128, 2, 1284])
        dma("sp", cf0[:], cst, "cf0")
        copy("dve", cb[:], cf0[:])
        dma("sp", sc5[:], c5T, "sc5")
        act(sc5[:], sc5[:], AF.Silu)
        dma("sp", bad[:], b_ada.partition_broadcast(5), "bad")
        dma("sp", ohs[:], oh5, "ohs")
        dma("sp", nrb[:, 0, :], nrm[0:1, :].partition_broadcast(128), "nrb0")
        dma("sp", nrb[:, 1, :], nrm[1:2, :].partition_broadcast(128), "nrb1")
        for j in range(12):
            wt = wst[j % 4]
            dma("sp" if j % 2 == 0 else "pool", wt[:], w_ada[:, j * 512:(j + 1) * 512].rearrange("(kc p) n -> p kc n", p=128), "wst" + wt.name)
            for kc in range(KC):
                mm(ps[j % 2][0:5, :], sc5[:, kc, :], wt[:, kc, :], start=(kc == 0), stop=(kc == KC - 1))
            tt(mod5[:, j * 512:(j + 1) * 512], ps[j % 2][0:5, :], bad[:, j * 512:(j + 1) * 512], ALU.add)
        k = 0
        for grp in range(2):
            oh = ohs[:, grp * 128:(grp + 1) * 128]
            for blk in range(2):
                for part, kind in ((1, "W"), (0, "B"), (2, "G")):
                    col = (blk * 3 + part) * D
                    m = mt[k % 2]
                    for hf in range(2):
                        mm(ps[2 + hf][:], oh, mod5[:, col + hf * 512:col + (hf + 1) * 512])
                    for hf in range(2):
                        dst = m[:, hf * 512:(hf + 1) * 512]
                        if kind == "W":
                            stt(dst, ps[2 + hf][:], 1.0, nrb[:, blk, hf * 512:(hf + 1) * 512], ALU.add, ALU.mult)
                        else:
                            copy("act", dst, ps[2 + hf][:])
                    idx = grp * 6 + blk * 3 + {"W": 0, "B": 1, "G": 2}[kind]
                    dma("sp", modt[idx], m[:], "mt" + m.name)
                    k += 1
        P.barrier()
        P.emit()

    with ExitStack() as st:
        cf = sb(st, "cf", [128, 2, 772])
        dma("sp", cf[:], cst[:, :, 0:772], "cf")
        wA = sb(st, "wA", [128, KC, NA], BF16)
        wq = sb(st, "wq", [128, 3, 768], BF16)
        wqr = sb(st, "wqr", [128, 3, NH, 96], BF16)
        W1 = [sb(st, "W1_0", [128, D])]
        B1 = [sb(st, "B1_0", [128, D])]
        qnb = sb(st, "qnb", [128, QL])
        kvnb = sb(st, "kvnb", [128, KVL])
        hgnb = sb(st, "hgnb", [128, 128])
        lbt = sb(st, "lbt", [128, HW])
        omlt = sb(st, "omlt", [128, HW])
        lbtmp = sb(st, "lbtmp", [128, HW])
        selt = sb(st, "selt", [128, 4])
        S = sb(st, "S", [128, HW])
        Sown = sb(st, "Sown", [128, HW])
        S0s = sb(st, "S0s", [128, 4, HW])
        Snew = sb(st, "Snew", [128, 4, HW])
        xt = [sb(st, f"xt{i}", [128, D]) for i in range(2)]
        rp = [sb(st, f"rp{i}", [128, 64]) for i in range(4)]
        junk = sb(st, "junk", [128, D], BF16)
        junk2 = sb(st, "junk2", [128, QL], BF16)
        tmpf = sb(st, "tmpf", [128, D])
        hb = sb(st, "hb", [128, D], BF16)
        hbo = sb(st, "hbo", [128, D], BF16)
        xto = sb(st, "xto", [128, D])
        rpo = sb(st, "rpo", [128, 64])
        rpTo = sb(st, "rpTo", [96, 2, 128])
        hT = sb(st, "hT", [128, KC, 128], BF16)
        st1 = sb(st, "st1", [128, 8])
        sth = sb(st, "sth", [128, 8])
        zkvo = sb(st, "zkvo", [128, 288])
        stA = sb(st, "stA", [128, 4])
        zkvs2 = [sb(st, f"zkvs{i}", [128, 288]) for i in range(2)]
        ckv = sb(st, "ckv", [128, KVL])
        ckvb = sb(st, "ckvb", [128, KVL], BF16)
        ckT = sb(st, "ckT", [128, 2, 128], BF16)
        kr = sb(st, "kr", [128, RD])
        krt = sb(st, "krt", [128, RD])
        krb = sb(st, "krb", [128, RD], BF16)
        krT = sb(st, "krT", [32, 128], BF16)
        sig2 = [sb(st, f"sig{i}", [128, HW]) for i in range(2)]
        tsg2 = [sb(st, f"tsg{i}", [128, HW]) for i in range(2)]
        gg2 = [sb(st, f"gg{i}", [128, HW]) for i in range(2)]
        kk2 = [sb(st, f"kk{i}", [128, HW]) for i in range(2)]
        sig, tsg, gg, kk = sig2[0], tsg2[0], gg2[0], kk2[0]
        qq = sb(st, "qq", [128, HW])
        sgt = sb(st, "sgt", [128, HW])
        Vb2 = [sb(st, f"Vb{i}", [128, HW], BF16) for i in range(2)]
        Vb = Vb2[0]
        erev = sb(st, "erev", [128, HW])
        Ke = sb(st, "Ke", [128, HW], BF16)
        Kem = sb(st, "Kem", [128, 4, HW], BF16)
        at4 = sb(st, "at4", [128, 8])
        cq = sb(st, "cq", [128, QL], BF16)
        cqT = sb(st, "cqT", [128, 3, 128], BF16)
        qTs = sb(st, "qTs", [96, NH, 128], BF16)
        qtmp = sb(st, "qtmp", [96, 4, 128])
        E12 = sb(st, "E12", [128, 4, 2, 128])
        E3 = sb(st, "E3", [128, 4, 128])
        aexp = sb(st, "aexp", [128, 16])
        Qo4 = sb(st, "Qo4", [128, 4, 128], BF16)
        Qom4 = sb(st, "Qom4", [128, 4, 4, 128], BF16)
        Qd4 = sb(st, "Qd4", [128, 4, 128], BF16)
        Kd4 = sb(st, "Kd4", [128, 4, 128], BF16)
        Scf4 = sb(st, "Scf4", [128, 4, 128])
        Scb4 = sb(st, "Scb4", [128, 4, 4, 128], BF16)
        attb4 = sb(st, "attb4", [128, 4, 128], BF16)
        osq = sb(st, "osq", [128, HW])
        og = sb(st, "og", [128, HW])
        ogb = sb(st, "ogb", [128, HW], BF16)
        oTs = sb(st, "oTs", [128, 4, 128], BF16)

        load_w(wA, w_in, NA, KC, extra=[tmpf])
        load_w(wq, w_uq, 768, 3, extra=[tmpf])
        memset("pool", wqr[:], 0.0)
        for kc in range(3):
            s = stg[rr["i"] % 2]
            rr["i"] += 1
            dma("sp", s[:, 0:256], w_uqr[kc * 128:(kc + 1) * 128, :], "stg" + s.name)
            copy("dve", wqr[:, kc, :, 64:96], s[:, 0:256].rearrange("p (h d) -> p h d", h=NH))
        dma("sp", W1[0][:], modt[0], "W10")
        dma("sp", B1[0][:], modt[1], "B10")
        dma("sp", qnb[:], q_norm.partition_broadcast(128), "qnb")
        dma("sp", kvnb[:], kv_norm.partition_broadcast(128), "kvnb")
        dma("sp", hgnb[:], hg_norm.partition_broadcast(128), "hgnb")
        dma("sp", lbt[:], lb_param[0:1, :].partition_broadcast(128), "lbt")
        dma("sp", lbtmp[:], lb_param[1:2, :].partition_broadcast(128), "lbtmp")
        dma("sp", selt[:], sel, "selt")
        dma("sp", S0s[:].rearrange("p b (h v) -> p b h v", h=4), state_s, "S0s")
        tt(lbt[:], lbt[:], lbtmp[:], ALU.subtract)
        act(lbt[:], lbt[:], AF.Sigmoid)
        ts(omlt[:], lbt[:], -1.0, ALU.mult, 1.0, ALU.add)
        memset("dve", S[:], 0.0)
        memset("dve", Sown[:], 0.0)

        def norm_only(xtile, g, hb_):
            act(junk[:], xtile[:], AF.Square, accum=stA[:, 0:1])
            rstd_act(stA[:, 2:3], stA[:, 1:2], stA[:, 0:1], D)
            stt(tmpf[:], xtile[:], stA[:, 2:3], W1[g][:], ALU.mult, ALU.mult)
            tt(hb_[:], tmpf[:], B1[g][:], ALU.add)

        def trans_only(hb_):
            pv = psbf(7)
            for kc in range(KC):
                mm(pv[:, kc * 128:(kc + 1) * 128], hb_[:, kc * 128:(kc + 1) * 128], ident_b(), tr=True)
            copy("act", hT[:].rearrange("p a b -> p (a b)"), pv[:, 0:1024])

        def zproj(groups):
            for kc in range(KC):
                for (pa, c0, n) in groups:
                    mm(pa, hT[:, kc, :], wA[:, kc, c0:c0 + n], start=(kc == 0), stop=(kc == KC - 1))

        def latents(zkv, rpt, out_row=None, scan_cols=None, smp=False):
            act(junk2[:, 0:KVL], zkv[:, 0:KVL], AF.Square, accum=st1[:, 3:4])
            rstd_act(st1[:, 5:6], st1[:, 4:5], st1[:, 3:4], KVL)
            stt(ckv[:], zkv[:, 0:KVL], st1[:, 5:6], kvnb[:], ALU.mult, ALU.mult)
            tt(krt[:, 0:16], zkv[:, KVL + 16:KVL + 32], rpt[:, 32:48], ALU.mult)
            tt(krt[:, 16:32], zkv[:, KVL:KVL + 16], rpt[:, 48:64], ALU.mult)
            tt(kr[:], zkv[:, KVL:KVL + 32], rpt[:, 0:32], ALU.mult)
            tt(kr[:], kr[:], krt[:], ALU.add)
            if out_row is not None:
                dma("pool", ckv_o[out_row:out_row + 128, :], ckv[:], "ckvo")
                dma("pool", kr_o[out_row:out_row + 128, :], kr[:], "kro")
            if scan_cols is not None or smp:
                copy("act", ckvb[:], ckv[:])
                copy("act", krb[:], kr[:])
                pv = psbf(6)
                for c in range(2):
                    mm(pv[:, c * 128:(c + 1) * 128], ckvb[:, c * 128:(c + 1) * 128], ident_b(), tr=True)
                mm(pv[0:32, 256:384], krb[:], ident_b(), tr=True)
                copy("dve", ckT[:].rearrange("p a b -> p (a b)"), pv[:, 0:256])
                copy("dve", krT[:], pv[0:32, 256:384])
                if smp:
                    dma("pool", ckTs_scr, ckT[:], "ckTo")
                    dma("pool", krTs_scr, krT[:], "krTo")
                else:
                    dma("pool", ckT_scr[:, :, scan_cols:scan_cols + 128], ckT[:], "ckTo")
                    dma("pool", krT_scr[:, scan_cols:scan_cols + 128], krT[:], "krTo")

        def gates_fk(zhf, zhi, ez_done=None, p=0):
            sig, tsg, gg, kk, Vb = sig2[p], tsg2[p], gg2[p], kk2[p], Vb2[p]
            if ez_done is None:
                act(sig[:], zhf, AF.Exp, scale=-1.0)
            act(sig[:], sig[:], AF.Ln, bias=1.0)
            act(sig[:], sig[:], AF.Exp, scale=-1.0)
            tt(tsg[:], sig[:], omlt[:], ALU.mult)
            tt(gg[:], tsg[:], lbt[:], ALU.add)
            act(gg[:], gg[:], AF.Ln)
            tt(kk[:], omlt[:], tsg[:], ALU.subtract)
            if ez_done is None:
                copy("act", Vb[:], zhi)

        def scan_A1(j):
            xs = xt[j % 2]
            rpt = rp[j % 4]
            dma("sp", xs[:], x_seq[j * 128:(j + 1) * 128, :], "xt" + xs.name)
            dma("sp", rpt[:], rope_seq[j * 128:(j + 1) * 128, :], "rp" + rpt.name)
            norm_only(xs, 0, hb)

        def scan_A2(j):
            trans_only(hb)
            zproj([(ps[0][:, 0:288], O_KV, 288), (ps[1][:], O_HF, 512), (ps[2][:], O_HI, 512)])

        def scan_E(j):
            act(sig2[j % 2][:], ps[1][:], AF.Exp, scale=-1.0)
            copy("act", Vb2[j % 2][:], ps[2][:])
            copy("act", zkvs2[j % 2][:], ps[0][:, 0:288])

        def scan_L(j):
            latents(zkvs2[j % 2], rp[j % 4], scan_cols=j * 128)

        def scan_G1(j):
            gates_fk(None, None, ez_done=True, p=j % 2)

        def scan_G2(j):
            gg, kk, Vb = gg2[j % 2], kk2[j % 2], Vb2[j % 2]
            mm(ps[3][:], cf[:, 0, C_TREVT:C_TREVT + 128], gg[:])
            act(erev[:], ps[3][:], AF.Exp)
            tt(Ke[:], kk[:], erev[:], ALU.mult)
            for h in range(4):
                mm(ps[5][:, h * 128:(h + 1) * 128], Ke[:, h * 128:(h + 1) * 128], Vb[:, h * 128:(h + 1) * 128])
            stt(Sown[:], S[:], selt[:, j % 4:j % 4 + 1], Sown[:], ALU.mult, ALU.add)
            scan_tile_state(j)

        def scan_tile_state(j):
            gg = gg2[j % 2]
            for h in range(4):
                mm(ps[4][:, 4 + h:5 + h], gg[:, h * 128:(h + 1) * 128], onesc)
            act(at4[:, 0:4], ps[4][:, 4:8], AF.Exp)
            for h in range(4):
                stt(S[:, h * 128:(h + 1) * 128], S[:, h * 128:(h + 1) * 128], at4[:, h:h + 1],
                    ps[5][:, h * 128:(h + 1) * 128], ALU.mult, ALU.add)

        ones_t = sb(st, "ones_t", [128, 2])
        memset("dve", ones_t[:], 1.0)
        onesc = ones_t[:, 0:1]

        def own_pre(m, smp):
            g = 1 if smp else 0
            row = m * 128
            dma("sp", xto[:], x_own[row:row + 128, :], "xto")
            dma("sp", rpo[:], rope_own[row:row + 128, :], "rpo")
            dma("sp", rpTo[:], ropeT_own[:, :, row:row + 128].rearrange("a p t -> p a t"), "rpTo")
            norm_only(xto, g, hbo)

        def own_tile(m, smp):
            g = 1 if smp else 0
            cs = 1 if smp else 0
            row = m * 128
            rpt = rpo
            rT = rpTo
            trans_only(hbo)
            zproj([(ps[6][:, 0:QL], O_Q, QL), (ps[7][:, 0:288], O_KV, 288),
                   (ps[0][:], O_HQ, 512), (ps[1][:], O_HF, 512), (ps[2][:], O_HI, 512), (ps[3][:], O_HG, 512)])

            def chain_q():
                act(junk2[:, 0:QL], ps[6][:, 0:QL], AF.Square, accum=st1[:, 6:7])
                copy("act", zkvo[:], ps[7][:, 0:288])
                rstd_act(st1[:, 2:3], st1[:, 7:8], st1[:, 6:7], QL)
                stt(cq[:], ps[6][:, 0:QL], st1[:, 2:3], qnb[:], ALU.mult, ALU.mult)
                pv = psbf(6)
                for c in range(3):
                    mm(pv[:, c * 128:(c + 1) * 128], cq[:, c * 128:(c + 1) * 128], ident_b(), tr=True)
                copy("act", cqT[:].rearrange("p a b -> p (a b)"), pv[:, 0:384])
                for hh in range(2):
                    for h in range(hh * 4, hh * 4 + 4):
                        pa = ps[6][0:96, (h % 4) * 128:(h % 4 + 1) * 128]
                        for c in range(3):
                            mm(pa, wq[:, c, h * 96:(h + 1) * 96], cqT[:, c, :], start=(c == 0), stop=(c == 2))
                    for h in range(hh * 4, hh * 4 + 4):
                        pb = ps[7][0:96, (h % 4) * 128:(h % 4 + 1) * 128]
                        for c in range(3):
                            mm(pb, wqr[:, c, h, :], cqT[:, c, :], start=(c == 0), stop=(c == 2))
                    pa3 = ps[6][0:96, :].rearrange("p (h t) -> p h t", h=4)
                    pb3 = ps[7][0:96, :].rearrange("p (h t) -> p h t", h=4)
                    qs = qTs[:, hh * 4:(hh + 1) * 4, :]
                    qt_ = qtmp[:, 0:4, :]
                    copy("act", qs[0:64], pa3[0:64])
                    tt(qt_[64:96], pa3[64:96], bc(rT[64:96, 0:1, :], [32, 4, 128]), ALU.mult)
                    tt(qs[64:96], pb3[64:96], bc(rT[64:96, 1:2, :], [32, 4, 128]), ALU.mult)
                    tt(qs[64:96], qs[64:96], qt_[64:96], ALU.add)
                dma("pool", qT_scr[:, :, row:row + 128], qTs[:], "qTo")
                latents(zkvo, rpt, out_row=row, smp=smp)

            def chain_h():
                hgrn_own(cs, smp, row)

            P.interleave(chain_h, chain_q)

        def hgrn_own(cs, smp, row):
            st1 = sth
            gates_fk(ps[1][:], ps[2][:])
            act(qq[:], ps[0][:], AF.Silu)
            act(sgt[:], ps[3][:], AF.Silu)
            tt(sgt[:].rearrange("p (h v) -> p h v", h=4), sgt[:].rearrange("p (h v) -> p h v", h=4),
               bc(hgnb[:].unsqueeze(1), [128, 4, 128]), ALU.mult)
            mm(ps[0][:], cf[:, cs, C_TREV:C_TREV + 128], gg[:])
            act(erev[:], ps[0][:], AF.Exp)
            tt(Ke[:], kk[:], erev[:], ALU.mult)
            tt(Kem[:], bc(Ke[:].unsqueeze(1), [128, 4, HW]),
               bc(cb[:, cs, C_CG + 256:C_CG + 260].unsqueeze(2), [128, 4, HW]), ALU.mult, eng="pool")
            CGb = [ps[1], ps[2]]
            QKb = [ps[3], ps[4]]
            idf = cf[:, 0, C_ID:C_ID + 128]
            for h in range(4):
                hs = slice(h * 128, (h + 1) * 128)
                o2 = (h % 2) * 256
                mm(CGb[h // 2][:, o2:o2 + 256], gg[:, hs], cf[:, cs, C_CG:C_CG + 256])
            for h in range(4):
                hs = slice(h * 128, (h + 1) * 128)
                mm(ps[5][:, h * 4:(h + 1) * 4], gg[:, hs], cf[:, cs, C_CG + 256:C_CG + 260])
            for h in range(4):
                hs = slice(h * 128, (h + 1) * 128)
                o2 = (h % 2) * 256
                mm(QKb[h // 2][:, o2:o2 + 128], qq[:, hs], idf)
                mm(QKb[h // 2][:, o2 + 128:o2 + 256], kk[:, hs], idf)
            for b2 in range(2):
                act(E12[:, 2 * b2:2 * b2 + 2, :, :].rearrange("p h x t -> p (h x t)"), CGb[b2][:], AF.Exp)
                act(E3[:, 2 * b2:2 * b2 + 2, :], CGb[b2][:].rearrange("p (h x t) -> p h x t", h=2, x=2)[:, :, 1, :], AF.Exp, scale=-1.0)
            act(aexp[:], ps[5][:, 0:16], AF.Exp)
            for b2 in range(2):
                qk4 = QKb[b2][:].rearrange("p (h x t) -> p h x t", h=2, x=2)
                h2 = slice(2 * b2, 2 * b2 + 2)
                stt(Qo4[:, h2, :], qk4[:, :, 0, :], HG_SCALE, E12[:, h2, 0, :], ALU.mult, ALU.mult)
                stt(Qd4[:, h2, :], qk4[:, :, 0, :], HG_SCALE, E12[:, h2, 1, :], ALU.mult, ALU.mult)
                tt(Kd4[:, h2, :], qk4[:, :, 1, :], E3[:, h2, :], ALU.mult)
            cm4 = cb[:, cs, C_COL:C_COL + 512].rearrange("p (c t) -> p c t", c=4)
            for h in range(4):
                tt(Qom4[:, h, :, :], bc(Qo4[:, h:h + 1, :], [128, 4, 128]), cm4, ALU.mult, eng="pool")
            DSb = [ps[1], ps[2], ps[3], ps[4]]
            for c in range(4):
                for h in range(4):
                    hs = slice(h * 128, (h + 1) * 128)
                    mm(DSb[c][:, hs], Kem[:, c, hs], Vb[:, hs])
            a4 = aexp[:].rearrange("p (h c) -> p h c", h=4)
            if smp:
                copy("act", Scb4[:].rearrange("p h c v -> p c h v"), S0s[:].rearrange("p c (h v) -> p c h v", h=4))
                for c in range(4):
                    tt(Scf4[:], S0s[:, c, :].rearrange("p (h v) -> p h v", h=4), bc(a4[:, :, c:c + 1], [128, 4, 128]), ALU.mult)
                    tt(Snew[:, c, :], Scf4[:].rearrange("p h v -> p (h v)"), DSb[c][:], ALU.add)
            else:
                copy("act", Scb4[:, :, 0, :], Sown[:].rearrange("p (h v) -> p h v", h=4))
                for c in range(3):
                    src_ = Sown[:].rearrange("p (h v) -> p h v", h=4) if c == 0 else Scf4[:]
                    tt(Scf4[:], src_, bc(a4[:, :, c:c + 1], [128, 4, 128]), ALU.mult)
                    tt(Scf4[:], Scf4[:], DSb[c][:].rearrange("p (h v) -> p h v", h=4), ALU.add)
                    copy("act", Scb4[:, :, c + 1, :], Scf4[:])
            for h in range(4):
                mm(ps[0][:, h * 128:(h + 1) * 128], Kd4[:, h, :], Qd4[:, h, :])
            tt(attb4[:], ps[0][:].rearrange("p (h t) -> p h t", h=4), bc(cf[:, cs, C_TRI:C_TRI + 128].unsqueeze(1), [128, 4, 128]), ALU.mult)
            for h in range(4):
                hs = slice(h * 128, (h + 1) * 128)
                po = ps[5][:, hs]
                for c in range(4):
                    mm(po, Qom4[:, h, c, :], Scb4[:, h, c, :], start=(c == 0), stop=False)
                mm(po, attb4[:, h, :], Vb[:, hs], start=False, stop=True)
            act(osq[:], ps[5][:], AF.Square)
            P.op("dve", lambda e: e.tensor_reduce(out=st1[:, 0:4], in_=osq[:].rearrange("p (h v) -> p h v", h=4), axis=AX.X, op=ALU.add),
                 r=[osq], w=[st1])
            rstd_act(st1[:, 0:4], st1[:, 4:8], st1[:, 0:4], 128)
            tt(og[:].rearrange("p (h v) -> p h v", h=4), ps[5][:].rearrange("p (h v) -> p h v", h=4),
               bc(st1[:, 0:4].unsqueeze(2), [128, 4, 128]), ALU.mult)
            tt(ogb[:], og[:], sgt[:], ALU.mult)
            pv = psbf(0)
            for h in range(4):
                mm(pv[:, h * 128:(h + 1) * 128], ogb[:, h * 128:(h + 1) * 128], ident_b(), tr=True)
            copy("act", oTs[:].rearrange("p a b -> p (a b)"), pv[:, 0:512])
            dma("pool", oT_scr[:, :, row:row + 128], oTs[:], "oTo")
            if not smp:
                memset("dve", Sown[:], 0.0)

        for m in range(NO):
            scan_A1(4 * m)
            scan_A2(4 * m)
            scan_A1(4 * m + 1)
            for rr_ in range(4):
                j = 4 * m + rr_
                scan_E(j)
                if rr_ < 3:
                    scan_A2(j + 1)
                if rr_ < 2:
                    scan_A1(j + 2)
                if rr_ == 2:
                    own_pre(m, False)
                if rr_ > 0:
                    P.interleave(lambda j=j: scan_G1(j), lambda j=j: scan_G2(j - 1), lambda j=j: scan_L(j - 1))
                else:
                    scan_G1(j)
            P.interleave(lambda: scan_G2(4 * m + 3), lambda: scan_L(4 * m + 3))
            own_tile(m, False)
        dma("sp", hst_p.rearrange("p h v -> p (h v)"), S[:], "hstp")
        W1.append(xt[0])
        B1.append(xt[1])
        dma("sp", xt[0][:], modt[6], "xt" + xt[0].name)
        dma("sp", xt[1][:], modt[7], "xt" + xt[1].name)
        own_pre(NO, True)
        own_tile(NO, True)
        dma("sp", hst_s.rearrange("p b h v -> p b (h v)"), Snew[:], "hsts")
        P.barrier()
        P.emit()

    with ExitStack() as st:
        TK = max(T, PAST + 128)
        CK = sb(st, "CK", [128, 2, TK], BF16)
        KT = sb(st, "KT", [96, TK], BF16)
        Vas = [sb(st, f"Va{i}", [128, TK // 128 + 1, 128], BF16) for i in range(2)]
        wkv = sb(st, "wkv", [128, 2, 1024], BF16)
        QTh = [sb(st, f"QTh{i}", [96, NTOK], BF16) for i in range(2)]
        PT = [sb(st, f"PT{i}", [128, 512], BF16) for i in range(3)]
        mdg = sb(st, "mdg", [128, 4, 128])
        mdgb = sb(st, "mdgb", [128, 4, 128], BF16)
        rsum = sb(st, "rsum", [128, 512])
        aTn = [sb(st, f"aTn{i}", [64, 512], BF16) for i in range(2)]
        QTs = sb(st, "QTs", [96, NH, 128], BF16)
        load_w(wkv, w_ukv, 1024, 2)
        dma("sp", mdg[:], mdiag, "mdg")
        copy("dve", mdgb[:], mdg[:])
        for Va_ in Vas:
            memset("pool", Va_[:, :, 64:128], 1.0, w=[f"Va{Va_.name}_{g}" for g in range(TK // 1024 + 2)])
        pti = {"i": 0}
        evi = {"i": 0}

        def expand_head(h, ntile, nk, vi=0):
            expand_K(h, nk)
            expand_V(h, ntile, nk, vi)

        def expand_K(h, nk):
            for c0 in range(0, nk, 512):
                n = min(512, nk - c0)
                pk = ps[evi["i"] % 2]
                evi["i"] += 1
                for c in range(2):
                    mm(pk[0:64, 0:n], wkv[:, c, h * 128:h * 128 + 64], CK[:, c, c0:c0 + n], start=(c == 0), stop=(c == 1))
                copy("dve" if (c0 // 512) % 2 == 0 else "pool" if False else "dve", KT[0:64, c0:c0 + n], pk[0:64, 0:n],
                     w=[f"KT{c0 // 512}"])

        def expand_V(h, ntile, nk, vi):
            Va = Vas[vi]
            vk = "Va" + Va.name
            for t0 in range(0, ntile, 8):
                nt_ = min(8, ntile - t0)
                pvv = ps[2]
                for t in range(nt_):
                    kt = t0 + t
                    rows = min(128, nk - kt * 128)
                    for c in range(2):
                        mm(pvv[0:rows, t * 64:(t + 1) * 64], CK[:, c, kt * 128:kt * 128 + rows], wkv[:, c, h * 128 + 64:h * 128 + 128],
                           start=(c == 0), stop=(c == 1))
                rows_all = 128 if (t0 + nt_) * 128 <= nk else None
                if rows_all is not None:
                    copy("dve", Va[:, t0:t0 + nt_, 0:64], pvv[:, 0:nt_ * 64].rearrange("p (t v) -> p t v", v=64), w=[f"{vk}_{t0 // 8}"])
                else:
                    for t in range(nt_):
                        kt = t0 + t
                        rows = min(128, nk - kt * 128)
                        copy("dve", Va[0:rows, kt, 0:64], pvv[0:rows, t * 64:(t + 1) * 64], w=[f"{vk}_{t0 // 8}"])

        def attend(h, qt, qcols, ktiles, out_cols, nq, diag_base=None, vi=0):
            LOOK = 2
            Va = Vas[vi]
            vk = "Va" + Va.name
            po = ps[3 + (pti["i"] % 2)]
            pti["i"] += 1
            n_k = len(ktiles)
            for it in range(n_k + LOOK):
                if it < n_k:
                    kt, rows, qoff = ktiles[it]
                    pS = ps[5 + it % 3]
                    mm(pS[0:rows, qoff:nq], KT[:, kt * 128:kt * 128 + rows], qt[:, qcols + qoff:qcols + nq],
                       r=[f"KT{kt // 4}", qt])
                idx = it - LOOK
                if idx < 0:
                    continue
                kt, rows, qoff = ktiles[idx]
                pS = ps[5 + idx % 3]
                pt = PT[idx % 3]
                act(pt[0:rows, qoff:nq], pS[0:rows, qoff:nq], AF.Exp, scale=MLA_SCALE)
                if diag_base is not None and kt >= diag_base:
                    loc = kt - diag_base
                    r_ = loc // 4
                    tt(pt[:, r_ * 128:(r_ + 1) * 128], pt[:, r_ * 128:(r_ + 1) * 128], mdgb[:, loc % 4, :], ALU.mult, eng="pool")
                mm(po[:, qoff:nq], Va[0:rows, kt, :], pt[0:rows, qoff:nq], start=(idx == 0), stop=(idx == n_k - 1),
                   r=[f"{vk}_{kt // 8}", pt])
            return po

        def finish(po, h, out_cols, nq):
            an = aTn[h % 2]
            copy("dve", rsum[64:128, 0:nq], po[64:128, 0:nq])
            recip(rsum[64:128, 0:nq], rsum[64:128, 0:nq])
            tt(an[:, 0:nq], po[0:64, 0:nq], rsum[64:128, 0:nq], ALU.mult)
            dma("pool", aT_scr[:, h, out_cols:out_cols + nq], an[:, 0:nq], "aTo" + an.name)

        CH = min(T, 4096)
        for c in range(2):
            for t0 in range(0, T, CH):
                dma("sp" if c == 0 else "pool", CK[:, c, t0:t0 + CH], ckT_scr[:, c, t0:t0 + CH], f"CK{c}")
        for t0 in range(0, T, CH):
            dma("sp", KT[64:96, t0:t0 + CH], krT_scr[:, t0:t0 + CH], "KTr", w=[f"KT{g}" for g in range(TK // 512 + 2)])
        for h in range(NH):
            qt = QTh[h % 2]
            dma("sp", qt[:], qT_scr[:, h, :], "QTh" + qt.name)
            expand_K(h, T)
            if h == 0:
                expand_V(0, NT, T, 0)

            def att_all(h=h, qt=qt):
                for g in range(NG):
                    kts = [(kt, 128, 0) for kt in range(16 * g)]
                    kts += [(16 * g + loc, 128, (loc // 4) * 128) for loc in range(16)]
                    po = attend(h, qt, g * 512, kts, g * 512, 512, diag_base=16 * g, vi=h % 2)
                    finish(po, h, g * 512, 512)

            if h + 1 < NH:
                P.interleave(att_all, lambda h=h: expand_V(h + 1, NT, T, (h + 1) % 2))
            else:
                att_all()
        nk_s = PAST + 16
        nts = NKS + 1
        allkt = [f"KT{g}" for g in range(TK // 512 + 2)]
        dma("sp", QTs[:], qT_scr[:, :, NO * 128:NO * 128 + 128], "QTs")
        zpad = sb(st, "zpad", [64, NH, 64], BF16)
        memset("pool", zpad[:], 0.0)
        dma("pool", aT_scr[:, :, NO * 128 + 64:NO * 128 + 128], zpad[:], "zpad")
        for b in range(4):
            si = 0
            for c in range(2):
                for p0 in range(0, PAST, 1024):
                    pn = min(1024, PAST - p0)
                    s_ = stg[si % 2]
                    si += 1
                    dma("sp", s_[:, 0:pn], ckvT_c[b, :, c, p0:p0 + pn], "stg" + s_.name)
                    copy("dve" if si % 2 == 0 else "pool", CK[:, c, p0:p0 + pn], s_[:, 0:pn])
            for p0 in range(0, PAST, 1024):
                pn = min(1024, PAST - p0)
                s_ = stg[si % 2]
                si += 1
                dma("sp", s_[0:32, 0:pn], krT_c[b, :, p0:p0 + pn], "stg" + s_.name)
                copy("act", KT[64:96, p0:p0 + pn], s_[0:32, 0:pn], w=allkt)
            dma("sp", CK[:, :, PAST:PAST + 16], ckTs_scr[:, :, b * 16:(b + 1) * 16], "CKn")
            dma("sp", KT[64:96, PAST:PAST + 16], krTs_scr[:, b * 16:(b + 1) * 16], "KTn", w=allkt)
            for h in range(NH):
                expand_head(h, nts, nk_s)
                kts = [(kt, min(128, nk_s - kt * 128), 0) for kt in range(nts)]
                po = attend(h, QTs[:, h, :], b * 16, kts, NO * 128 + b * 16, 16)
                finish(po, h, NO * 128 + b * 16, 16)
        P.barrier()
        P.emit()

    def make_norm(Wt, Bt, junk, tmpf, hb, st1, hT_dst_fn):
        def norm_T(xtile, g, k=0):
            act(junk[:], xtile[:], AF.Square, accum=st1[:, 0:1])
            act(st1[:, 1:2], st1[:, 0:1], AF.Sqrt, bias=EPS, scale=1.0 / D)
            recip(st1[:, 2:3], st1[:, 1:2])
            stt(tmpf[:], xtile[:], st1[:, 2:3], Wt[g][:], ALU.mult, ALU.mult)
            tt(hb[:], tmpf[:], Bt[g][:], ALU.add)
            pv = psbf(7)
            for kc in range(KC):
                mm(pv[:, kc * 128:(kc + 1) * 128], hb[:, kc * 128:(kc + 1) * 128], ident_b(), tr=True)
            copy("act", hT_dst_fn(k), pv[:, 0:1024].rearrange("p (a b) -> p a b", a=KC))
        return norm_T

    with ExitStack() as st:
        wG = sb(st, "wG", [128, KC, 2048], BF16)
        wpa = sb(st, "wpa", [128, 4, D], BF16)
        wpb = sb(st, "wpb", [128, 4, D], BF16)
        wo = sb(st, "wo", [128, KC, D], BF16)
        W1 = [sb(st, f"W1_{g}", [128, D]) for g in range(2)]
        B1 = [sb(st, f"B1_{g}", [128, D]) for g in range(2)]
        G1 = [sb(st, f"G1_{g}", [128, D]) for g in range(2)]
        xt = [sb(st, f"xt{i}", [128, D]) for i in range(3)]
        aTt = [sb(st, f"aTt{i}", [128, 4, 128], BF16) for i in range(3)]
        oTt = [sb(st, f"oTt{i}", [128, 4, 128], BF16) for i in range(3)]
        junk = sb(st, "junk", [128, D], BF16)
        tmpf = sb(st, "tmpf", [128, D])
        hbs = [sb(st, f"hb{i}", [128, D], BF16) for i in range(2)]
        hTs = [sb(st, f"hT{i}", [128, KC, 128], BF16) for i in range(2)]
        st3 = [sb(st, f"st3_{i}", [128, 4]) for i in range(2)]
        sgq = [sb(st, f"sgq{i}", [128, 512]) for i in range(2)]
        tq = [sb(st, f"tq{i}", [128, 512]) for i in range(2)]
        mixb = sb(st, "mixb", [128, D], BF16)
        mT = sb(st, "mT", [128, KC, 128], BF16)
        x1 = [sb(st, f"x1_{i}", [128, D]) for i in range(2)]
        load_w(wG, w_in, 2048, KC, c0=O_GA, extra=[tmpf])
        load_w(wpa, w_pa, D, 4, extra=[tmpf])
        load_w(wpb, w_pb, D, 4, extra=[tmpf])
        load_w(wo, w_out, D, KC, extra=[tmpf])
        for g in range(2):
            dma("sp", W1[g][:], modt[g * 6 + 0], f"W1{g}")
            dma("sp", B1[g][:], modt[g * 6 + 1], f"B1{g}")
            dma("sp", G1[g][:], modt[g * 6 + 2], f"G1{g}")
        aT_v = aT_scr.rearrange("d (c hh) t -> d hh c t", hh=2)

        def p3_load(m):
            row = m * 128
            p = m % 3
            xs, at_, ot_ = xt[p], aTt[p], oTt[p]
            dma("sp", xs[:], x_own[row:row + 128, :], "xt" + xs.name)
            for hh in range(2):
                dma("sp", at_[hh * 64:(hh + 1) * 64, :, :], aT_v[:, hh, :, row:row + 128], f"aTt{hh}" + at_.name)
            dma("sp", ot_[:], oT_scr[:, :, row:row + 128], "oTt" + ot_.name)

        def p3_load_norm(m):
            g = 1 if m == NO else 0
            p = m % 2
            xs, s3, hb_ = xt[m % 3], st3[p], hbs[p]
            act(junk[:], xs[:], AF.Square, accum=s3[:, 0:1])
            rstd_act(s3[:, 2:3], s3[:, 1:2], s3[:, 0:1], D)
            stt(tmpf[:], xs[:], s3[:, 2:3], W1[g][:], ALU.mult, ALU.mult)
            tt(hb_[:], tmpf[:], B1[g][:], ALU.add)

        def p3_trans(m):
            pv = psbf(4)
            hb_ = hbs[m % 2]
            for kc in range(KC):
                mm(pv[:, kc * 128:(kc + 1) * 128], hb_[:, kc * 128:(kc + 1) * 128], ident_b(), tr=True)

        def p3_tcopy(m):
            copy("act", hTs[m % 2][:], psbf(4)[:, 0:1024].rearrange("p (a b) -> p a b", a=KC))

        def p3_quarter_mm(m, q):
            p = m % 2
            hT_, at_, ot_ = hTs[p], aTt[m % 3], oTt[m % 3]
            A, B = ps[(q % 2) * 2], ps[(q % 2) * 2 + 1]
            c0 = q * 256
            for kc in range(KC):
                mm(A[:, 0:256], hT_[:, kc, :], wG[:, kc, c0:c0 + 256], start=(kc == 0), stop=(kc == KC - 1))
            for kc in range(KC):
                mm(A[:, 256:512], hT_[:, kc, :], wG[:, kc, D + c0:D + c0 + 256], start=(kc == 0), stop=(kc == KC - 1))
            for c in range(4):
                mm(B[:, 0:256], at_[:, c, :], wpa[:, c, c0:c0 + 256], start=(c == 0), stop=(c == 3))
            for c in range(4):
                mm(B[:, 256:512], ot_[:, c, :], wpb[:, c, c0:c0 + 256], start=(c == 0), stop=(c == 3))

        def p3_quarter_ew(m, q):
            A, B = ps[(q % 2) * 2], ps[(q % 2) * 2 + 1]
            sg_, t_ = sgq[q % 2], tq[q % 2]
            act(sg_[:], A[:], AF.Exp, scale=-1.0)
            act(sg_[:], sg_[:], AF.Ln, bias=1.0)
            act(sg_[:], sg_[:], AF.Exp, scale=-1.0)
            tt(t_[:], sg_[:], B[:], ALU.mult)
            tt(mixb[:, q * 256:(q + 1) * 256], t_[:, 0:256], t_[:, 256:512], ALU.add)

        def p3_out(m):
            g = 1 if m == NO else 0
            row = m * 128
            xs, xo = xt[m % 3], x1[m % 2]
            pv = psbf(5)
            for kc in range(KC):
                mm(pv[:, kc * 128:(kc + 1) * 128], mixb[:, kc * 128:(kc + 1) * 128], ident_b(), tr=True)
            copy("act", mT[:], pv[:, 0:1024].rearrange("p (a b) -> p a b", a=KC))
            for kc in range(KC):
                for hf in range(2):
                    mm(ps[6 + hf][:], mT[:, kc, :], wo[:, kc, hf * 512:(hf + 1) * 512], start=(kc == 0), stop=(kc == KC - 1))
            for hf in range(2):
                cs_ = slice(hf * 512, (hf + 1) * 512)
                tt(xo[:, cs_], ps[6 + hf][:], G1[g][:, cs_], ALU.mult)
                tt(xo[:, cs_], xo[:, cs_], xs[:, cs_], ALU.add)
            dma("pool", x1_scr[row:row + 128, :], xo[:], "x1o" + xo.name)

        p3_load(0)
        p3_load_norm(0)
        p3_trans(0)
        p3_tcopy(0)
        if NO1 > 1:
            p3_load(1)
        if NO1 > 2:
            p3_load(2)
        p3_quarter_mm(0, 0)
        p3_quarter_mm(0, 1)
        for m in range(NO1):
            if m + 1 < NO1:
                p3_load_norm(m + 1)
            p3_quarter_ew(m, 0)
            p3_quarter_mm(m, 2)
            p3_quarter_ew(m, 1)
            p3_quarter_mm(m, 3)
            if m + 1 < NO1:
                p3_trans(m + 1)
            p3_quarter_ew(m, 2)
            p3_quarter_ew(m, 3)
            if m + 1 < NO1:
                p3_tcopy(m + 1)
                p3_quarter_mm(m + 1, 0)
                p3_quarter_mm(m + 1, 1)
            p3_out(m)
            if m + 3 < NO1:
                p3_load(m + 3)
        P.barrier()
        P.emit()

    with ExitStack() as st:
        GT = 2
        wgu = sb(st, "wgu", [128, KC, 2 * DFF], BF16)
        wd = sb(st, "wd", [128, NFC, D], BF16)
        W2 = sb(st, "W2", [128, D])
        B2 = sb(st, "B2", [128, D])
        G2 = sb(st, "G2", [128, D])
        fnb = sb(st, "fnb", [128, D])
        xt = [[sb(st, f"xt{s}_{i}", [128, D]) for i in range(GT)] for s in range(2)]
        junk = sb(st, "junk", [128, D], BF16)
        tmpf = sb(st, "tmpf", [128, D])
        hb = sb(st, "hb", [128, D], BF16)
        hT4 = [sb(st, f"hT4_{s}", [128, KC, GT * 128], BF16) for s in range(2)]
        st4 = sb(st, "st4", [128, 4])
        st5 = sb(st, "st5", [128, 4])
        sgl = [sb(st, f"sgl{i}", [128, GT * 128]) for i in range(2)]
        actT = sb(st, "actT", [128, NFC, GT * 128], BF16)
        load_w(wgu, w_gu, 2 * DFF, KC, extra=[tmpf])
        load_w(wd, w_down, D, NFC, extra=[tmpf])
        dma("sp", fnb[:], nrm[2:3, :].partition_broadcast(128), "fnb")
        groups = [list(range(a, min(a + GT, NO))) for a in range(0, NO, GT)] + [[NO]]
        cur = {"g": -1}

        def p4_A(gi):
            grp = groups[gi]
            s = gi % 2
            g = 1 if grp[0] == NO else 0
            if g != cur["g"]:
                dma("sp", W2[:], modt[g * 6 + 3], "W2")
                dma("sp", B2[:], modt[g * 6 + 4], "B2")
                cur["g"] = g
            for k, m in enumerate(grp):
                xk = xt[s][k]
                dma("sp", xk[:], x1_scr[m * 128:(m + 1) * 128, :], "xt" + xk.name)
                act(junk[:], xk[:], AF.Square, accum=st4[:, 0:1])
                rstd_act(st4[:, 2:3], st4[:, 1:2], st4[:, 0:1], D)
                stt(tmpf[:], xk[:], st4[:, 2:3], W2[:], ALU.mult, ALU.mult)
                tt(hb[:], tmpf[:], B2[:], ALU.add)
                pv = psbf(7)
                for kc in range(KC):
                    mm(pv[:, kc * 128:(kc + 1) * 128], hb[:, kc * 128:(kc + 1) * 128], ident_b(), tr=True)
                copy("act", hT4[s][:, :, k * 128:(k + 1) * 128], pv[:, 0:1024].rearrange("p (a b) -> p a b", a=KC))

        def p4_B(gi):
            grp = groups[gi]
            s = gi % 2
            n = len(grp) * 128
            for c in range(NFC):
                pg, pu = ps[(c % 2) * 2], ps[(c % 2) * 2 + 1]
                for kc in range(KC):
                    mm(pg[:, 0:n], wgu[:, kc, c * 128:(c + 1) * 128], hT4[s][:, kc, 0:n], start=(kc == 0), stop=(kc == KC - 1))
                for kc in range(KC):
                    mm(pu[:, 0:n], wgu[:, kc, DFF + c * 128:DFF + (c + 1) * 128], hT4[s][:, kc, 0:n], start=(kc == 0), stop=(kc == KC - 1))
                sg_ = sgl[c % 2]
                act(sg_[:, 0:n], pg[:, 0:n], AF.Silu)
                tt(actT[:, c, 0:n], sg_[:, 0:n], pu[:, 0:n], ALU.mult)

        def p4_C(gi):
            grp = groups[gi]
            s = gi % 2
            g = 1 if grp[0] == NO else 0
            if g == 1 or gi == 0:
                dma("sp", G2[:], modt[g * 6 + 5], "G2")
            for k, m in enumerate(grp):
                for c in range(NFC):
                    for hf in range(2):
                        mm(ps[4 + hf][:], actT[:, c, k * 128:(k + 1) * 128], wd[:, c, hf * 512:(hf + 1) * 512], start=(c == 0), stop=(c == NFC - 1))
                xk = xt[s][k]
                for hf in range(2):
                    cs_ = slice(hf * 512, (hf + 1) * 512)
                    tt(tmpf[:, cs_], ps[4 + hf][:], G2[:, cs_], ALU.mult)
                    tt(xk[:, cs_], xk[:, cs_], tmpf[:, cs_], ALU.add)
                act(junk[:], xk[:], AF.Square, accum=st5[:, 0:1])
                rstd_act(st5[:, 2:3], st5[:, 1:2], st5[:, 0:1], D)
                stt(xk[:], xk[:], st5[:, 2:3], fnb[:], ALU.mult, ALU.mult)
                dma("pool", y_o[m * 128:(m + 1) * 128, :], xk[:], "yo" + xk.name)

        p4_A(0)
        for gi in range(len(groups)):
            p4_B(gi)
            if gi + 1 < len(groups):
                p4_A(gi + 1)
            p4_C(gi)
        P.barrier()
        P.emit()
    return nc, stack, P


def _rope_tabs(pos):
    half = 16
    inv = (np.float32(10000.0) ** (-np.arange(half, dtype=np.float32) / np.float32(half))).astype(np.float32)
    ang = pos.astype(np.float32)[:, None] * inv[None, :]
    return np.cos(ang).astype(np.float32), np.sin(ang).astype(np.float32)


def make_inputs(inp, T, PAST, n_cores=8):
    NT = T // 128
    NO = NT // 4
    NTOK = (NO + 1) * 128
    f = np.float32
    g = lambda k: np.asarray(inp[k], dtype=f)
    xp, xs_ = g("x_prompt"), g("x_sample")
    cst = np.zeros((128, 2, 1284), f)
    for si, (sbk, rows) in enumerate(((32, 128), (16, 64))):
        cg, trev, tri, colmask, ind = _consts(sbk, rows)
        cst[:, si, 0:260] = cg
        cst[:, si, 260:388] = trev
        cst[:, si, 388:516] = tri
        s = np.arange(128)
        cst[:, si, 516:644] = (s[:, None] > s[None, :]).astype(f)
        cst[:, si, 644:772] = np.eye(128, dtype=f)
        cst[:, si, 772:1284] = colmask
    w_uq = g("w_uq")[0]
    perm = []
    for h in range(NH):
        base = h * 96 + 64
        perm += [base + 16 + d for d in range(16)] + [base + d for d in range(16)]
    w_uqr = np.ascontiguousarray(w_uq[:, perm])
    oh5 = np.zeros((5, 256), f)
    oh5[0, 0:128] = 1.0
    for b in range(4):
        oh5[1 + b, 128 + 16 * b:128 + 16 * (b + 1)] = 1.0
    shared = {
        "w_ada": g("w_ada")[0], "b_ada": g("b_ada"), "w_in": g("w_in")[0], "w_uq": w_uq, "w_uqr": w_uqr,
        "w_ukv": g("w_ukv")[0], "w_pa": g("w_pa")[0], "w_pb": g("w_pb")[0], "w_out": g("w_out")[0],
        "w_gu": g("w_gu")[0], "w_down": g("w_down")[0],
        "nrm": np.stack([g("norm1")[0], g("norm2")[0], g("final_norm")]),
        "q_norm": g("q_norm"), "kv_norm": g("kv_norm"), "hg_norm": g("hg_norm"), "lb_param": g("lb_param"),
        "cst": cst, "oh5": oh5,
    }
    pos_seq = np.arange(T)
    cs_seq, sn_seq = _rope_tabs(pos_seq)
    rope_seq = np.concatenate([cs_seq, cs_seq, -sn_seq, sn_seq], axis=1)
    pos_s = np.zeros(128, np.int64)
    for b in range(4):
        pos_s[16 * b:16 * (b + 1)] = PAST + np.arange(16)
    maps = []
    for c in range(n_cores):
        b, i = c // 4, c % 4
        own = np.concatenate([np.arange((4 * m + i) * 128, (4 * m + i + 1) * 128) for m in range(NO)])
        xsm = np.zeros((128, D), f)
        xsm[0:64] = xs_[4 * c:4 * c + 4].reshape(64, D)
        pos_own = np.concatenate([own, pos_s])
        co, so = _rope_tabs(pos_own)
        rope_own = np.concatenate([co, co, -so, so], axis=1)
        ropeT = np.zeros((2, 96, NTOK), f)
        ropeT[0, 64:96] = np.concatenate([co, co], axis=1).T
        ropeT[1, 64:96] = np.concatenate([-so, so], axis=1).T
        c5 = np.concatenate([g("c_prompt")[b:b + 1], g("c_sample")[4 * c:4 * c + 4]], axis=0)
        c5T = np.ascontiguousarray(c5.T.reshape(KC, 128, 5).transpose(1, 0, 2))
        sel = np.zeros((128, 4), f)
        sel[:, i] = 1.0
        md = np.zeros((128, 4, 128), f)
        for r in range(4):
            if r < i:
                md[:, r, :] = 1.0
            elif r == i:
                kk_ = np.arange(128)[:, None] // 64
                qq_ = np.arange(128)[None, :] // 64
                md[:, r, :] = (kk_ <= qq_).astype(f)
        ck = g("cache_ckv")[0, 4 * c:4 * c + 4]
        ckvT = np.ascontiguousarray(ck.reshape(4, PAST, 2, 128).transpose(0, 3, 2, 1))
        krT = np.ascontiguousarray(g("cache_krope")[0, 4 * c:4 * c + 4].transpose(0, 2, 1))
        sts = np.ascontiguousarray(g("state_hgrn")[0, 4 * c:4 * c + 4].transpose(2, 0, 1, 3))
        m = dict(shared)
        m.update({
            "x_seq": np.ascontiguousarray(xp[b]), "x_own": np.concatenate([xp[b][own], xsm], axis=0),
            "rope_seq": rope_seq.astype(f), "rope_own": rope_own.astype(f), "ropeT_own": ropeT,
            "c5T": c5T, "sel": sel, "mdiag": md, "ckvT_c": ckvT, "krT_c": krT, "state_s": sts,
        })
        maps.append(m)
    return maps


def assemble(results, T, PAST, n_cores=8):
    NT = T // 128
    NO = NT // 4
    f = np.float32
    nb = n_cores // 4
    y_p = np.zeros((nb, T, D), f)
    ckv_p = np.zeros((1, nb, T, KVL), f)
    kr_p = np.zeros((1, nb, T, RD), f)
    hs_p = np.zeros((1, nb, 4, 128, 128), f)
    y_s = np.zeros((4 * n_cores, 16, D), f)
    ckv_s = np.zeros((1, 4 * n_cores, 16, KVL), f)
    kr_s = np.zeros((1, 4 * n_cores, 16, RD), f)
    hs_s = np.zeros((1, 4 * n_cores, 4, 128, 128), f)
    for c in range(n_cores):
        r = results[c]
        b, i = c // 4, c % 4
        own = np.concatenate([np.arange((4 * m + i) * 128, (4 * m + i + 1) * 128) for m in range(NO)])
        y_p[b, own] = r["y_o"][0:NO * 128]
        ckv_p[0, b, own] = r["ckv_o"][0:NO * 128]
        kr_p[0, b, own] = r["kr_o"][0:NO * 128]
        if i == 0:
            hs_p[0, b] = r["hst_p"].transpose(1, 0, 2)
        y_s[4 * c:4 * c + 4] = r["y_o"][NO * 128:NO * 128 + 64].reshape(4, 16, D)
        ckv_s[0, 4 * c:4 * c + 4] = r["ckv_o"][NO * 128:NO * 128 + 64].reshape(4, 16, KVL)
        kr_s[0, 4 * c:4 * c + 4] = r["kr_o"][NO * 128:NO * 128 + 64].reshape(4, 16, RD)
        hs_s[0, 4 * c:4 * c + 4] = r["hst_s"].transpose(1, 2, 0, 3)
    return (y_p, y_s, ckv_p, kr_p, hs_p, ckv_s, kr_s, hs_s)


def kernel(**inputs):
    T, PAST = 16384, 2048
    nc, stack, _ = build(T, PAST)
    maps = make_inputs(inputs, T, PAST)
    res = run_bass_kernel_spmd(nc, maps, core_ids=list(range(8)))
    stack.close()
    return assemble(res.results, T, PAST)
```
